# Optimizing a Trainium2 kernel written in Bass

```python
import jax, jax.numpy as jnp
from jax import lax
import numpy as np

D_MODEL = 1024
BATCH = 1
SEQ = 16384
DEPTH = 1

HEAD_DIM = 64
ATTN_GROUPS = ((128, 1), (512, 4), (2048, 16))
HEADS_PER_GROUP = 4
N_ATTN_HEADS = HEADS_PER_GROUP * len(ATTN_GROUPS)
ATTN_WIDTH = N_ATTN_HEADS * HEAD_DIM
ATTN_OUT_WIDTH = HEADS_PER_GROUP * HEAD_DIM
POOL_WINDOWS = (2, 4, 8, 16)
POOL_GROUP_WIDTH = D_MODEL // 16
POOL_WIDTH = POOL_GROUP_WIDTH * len(POOL_WINDOWS)
N_BRANCHES = 2
IN_WIDTH = 3 * ATTN_WIDTH + POOL_WIDTH + N_BRANCHES * D_MODEL
FFN_HIDDEN = -(-8 * D_MODEL // (3 * 256)) * 256
ROPE_THETA = 10000.0
EPS = 1e-6
NEG_INF = -1e30

kernel_name = "hybrid_dilated_attn_pool_gated_swiglu"


def rms_norm(x, g):
    xf = x.astype(jnp.float32)
    y = xf * lax.rsqrt(jnp.mean(xf * xf, axis=-1, keepdims=True) + EPS) * g.astype(jnp.float32)
    return y.astype(x.dtype)


def apply_rope(t):
    S, Dh = t.shape[1], t.shape[3]
    half = Dh // 2
    inv_freq = ROPE_THETA ** (-(jnp.arange(half, dtype=jnp.float32) * 2.0 / Dh))
    ang = jnp.arange(S, dtype=jnp.float32)[:, None] * inv_freq[None, :]
    cos = jnp.cos(ang)[None, :, None, :]
    sin = jnp.sin(ang)[None, :, None, :]
    tf = t.astype(jnp.float32)
    t1, t2 = tf[..., :half], tf[..., half:]
    out = jnp.concatenate([t1 * cos - t2 * sin, t2 * cos + t1 * sin], axis=-1)
    return out.astype(t.dtype)


def dilated_window_attention(q, k, v, window, dilation):
    B, S, H, Dh = q.shape
    steps = window // dilation
    span = steps * dilation
    S_pad = -(-S // span) * span
    pad = ((0, 0), (0, S_pad - S), (0, 0), (0, 0))
    q, k, v = jnp.pad(q, pad), jnp.pad(k, pad), jnp.pad(v, pad)
    nb = S_pad // span
    qs = q.reshape(B, nb, steps, dilation, H, Dh)
    ks = k.reshape(B, nb, steps, dilation, H, Dh)
    vs = v.reshape(B, nb, steps, dilation, H, Dh)

    def with_prev(t):
        prev = jnp.concatenate([jnp.zeros_like(t[:, :1]), t[:, :-1]], axis=1)
        return jnp.concatenate([prev, t], axis=2)

    kk, vv = with_prev(ks), with_prev(vs)
    scale = HEAD_DIM ** -0.5
    s = jnp.einsum('bnqrhd,bnkrhd->bnrhqk', qs, kk, preferred_element_type=jnp.float32) * scale
    qi = jnp.arange(steps)[:, None]
    kj = jnp.arange(2 * steps)[None, :]
    band = (kj >= qi) & (kj <= qi + steps)
    has_prev = (jnp.arange(nb) > 0)[:, None, None]
    valid = band[None] & (has_prev | (kj >= steps)[None])
    s = jnp.where(valid[None, :, None, None], s, NEG_INF)
    m = jnp.max(s, axis=-1, keepdims=True)
    p = jnp.exp(s - m)
    den = jnp.sum(p, axis=-1)
    lse = m[..., 0] + jnp.log(den)
    out = jnp.einsum('bnrhqk,bnkrhd->bnqrhd', p.astype(v.dtype), vv, preferred_element_type=jnp.float32)
    out = out / jnp.transpose(den, (0, 1, 4, 2, 3))[..., None]
    out = out.reshape(B, S_pad, H, Dh)[:, :S]
    lse = jnp.transpose(lse, (0, 1, 4, 2, 3)).reshape(B, S_pad, H)[:, :S]
    return out, lse


def multiscale_pool(u, w_mix, scale):
    B, S, _ = u.shape
    G, Cg = len(POOL_WINDOWS), POOL_GROUP_WIDTH
    ug = u.astype(jnp.float32).reshape(B, S, G, Cg)
    c0 = jnp.concatenate([jnp.zeros((B, 1, G, Cg), jnp.float32), jnp.cumsum(ug, axis=1)], axis=1)
    outs = []
    for g, w in enumerate(POOL_WINDOWS):
        c = c0[:, :, g]
        lag = jnp.concatenate([jnp.zeros((B, w - 1, Cg), jnp.float32), c[:, :S - w + 1]], axis=1)
        cnt = jnp.minimum(jnp.arange(S) + 1, w).astype(jnp.float32)[None, :, None]
        z = (c[:, 1:] - lag) / cnt - ug[:, :, g]
        outs.append(jnp.einsum('bsc,cd->bsd', z, w_mix[g].astype(jnp.float32)))
    y = jnp.concatenate(outs, axis=-1) * scale.astype(jnp.float32)
    return y.astype(u.dtype)


def setup_inputs(seed: int = 0) -> dict:
    key = jax.random.key(seed)
    ks = jax.random.split(key, 13)
    f32 = jnp.float32

    def w(k, shape, fan_in):
        return jax.random.normal(k, shape, f32) * (fan_in ** -0.5)

    def gain(k, n):
        return 1.0 + 0.1 * jax.random.normal(k, (DEPTH, n), f32)

    return {
        "x": jax.random.normal(ks[0], (BATCH, SEQ, D_MODEL), f32),
        "g_pre_mix": gain(ks[1], D_MODEL),
        "w_in": w(ks[2], (DEPTH, D_MODEL, IN_WIDTH), D_MODEL),
        "w_pool_mix": w(ks[3], (DEPTH, len(POOL_WINDOWS), POOL_GROUP_WIDTH, POOL_GROUP_WIDTH), POOL_GROUP_WIDTH),
        "pool_scale": gain(ks[4], POOL_WIDTH),
        "w_proj_attn": w(ks[5], (DEPTH, ATTN_OUT_WIDTH, D_MODEL), ATTN_OUT_WIDTH),
        "w_proj_pool": w(ks[6], (DEPTH, POOL_WIDTH, D_MODEL), POOL_WIDTH),
        "w_out": w(ks[7], (DEPTH, D_MODEL, D_MODEL), D_MODEL),
        "g_post_mix": gain(ks[8], D_MODEL),
        "g_pre_ffn": gain(ks[9], D_MODEL),
        "w_gate_up": w(ks[10], (DEPTH, D_MODEL, 2 * FFN_HIDDEN), D_MODEL),
        "w_down": w(ks[11], (DEPTH, FFN_HIDDEN, D_MODEL), FFN_HIDDEN),
        "g_post_ffn": gain(ks[12], D_MODEL),
    }


def reference(x, g_pre_mix, w_in, w_pool_mix, pool_scale, w_proj_attn, w_proj_pool, w_out,
              g_post_mix, g_pre_ffn, w_gate_up, w_down, g_post_ffn):
    B, S, _ = x.shape
    for l in range(DEPTH):
        h = rms_norm(x, g_pre_mix[l])
        proj = h @ w_in[l]
        o0 = 0
        q = proj[..., o0:o0 + ATTN_WIDTH].reshape(B, S, N_ATTN_HEADS, HEAD_DIM); o0 += ATTN_WIDTH
        k = proj[..., o0:o0 + ATTN_WIDTH].reshape(B, S, N_ATTN_HEADS, HEAD_DIM); o0 += ATTN_WIDTH
        v = proj[..., o0:o0 + ATTN_WIDTH].reshape(B, S, N_ATTN_HEADS, HEAD_DIM); o0 += ATTN_WIDTH
        u = proj[..., o0:o0 + POOL_WIDTH]; o0 += POOL_WIDTH
        gates = proj[..., o0:o0 + N_BRANCHES * D_MODEL]

        q = apply_rope(q)
        k = apply_rope(k)
        outs, lses = [], []
        for gi, (window, dilation) in enumerate(ATTN_GROUPS):
            hs = slice(gi * HEADS_PER_GROUP, (gi + 1) * HEADS_PER_GROUP)
            o_g, lse_g = dilated_window_attention(q[:, :, hs], k[:, :, hs], v[:, :, hs], window, dilation)
            outs.append(o_g)
            lses.append(lse_g)
        wts = jax.nn.softmax(jnp.stack(lses, axis=0), axis=0)
        o_attn = jnp.sum(wts[..., None] * jnp.stack(outs, axis=0), axis=0)
        o_attn = o_attn.reshape(B, S, ATTN_OUT_WIDTH).astype(x.dtype)

        y_pool = multiscale_pool(u, w_pool_mix[l], pool_scale[l])

        gate_a = jax.nn.sigmoid(gates[..., :D_MODEL])
        gate_p = jax.nn.sigmoid(gates[..., D_MODEL:])
        merged = gate_a * (o_attn @ w_proj_attn[l]) + gate_p * (y_pool @ w_proj_pool[l])
        x = x + rms_norm(merged @ w_out[l], g_post_mix[l])

        h2 = rms_norm(x, g_pre_ffn[l])
        gu = h2 @ w_gate_up[l]
        a, b = gu[..., :FFN_HIDDEN], gu[..., FFN_HIDDEN:]
        x = x + rms_norm((jax.nn.silu(a) * b) @ w_down[l], g_post_ffn[l])
    return x
```

```python
import numpy as np
import concourse.bass as bass
import concourse.mybir as mybir
from concourse.bass_utils import run_bass_kernel_spmd

F32 = mybir.dt.float32
BF16 = mybir.dt.bfloat16
AF = mybir.ActivationFunctionType
ALU = mybir.AluOpType

NCORES = 8
S = 16384
D = 1024
T = S // NCORES
NT = T // 128
HT = 2048
FF = 2816
NJ = FF // 128
EPS = 1e-6
KB = 1024

DEBUG_TAPS = False


class Ev:
    __slots__ = ("sem", "val")

    def __init__(self, sem, val):
        self.sem = sem
        self.val = val


class Eng:
    def __init__(self, name):
        self.name = name
        self.ops = []
        self.meta = []
        self.sem = None
        self.count = 0
        self.seen = {}

    def wait(self, *evs):
        for ev in evs:
            if ev is None:
                continue
            if isinstance(ev, (list, tuple)):
                self.wait(*ev)
                continue
            key = id(ev.sem)
            if self.seen.get(key, 0) >= ev.val:
                continue
            self.seen[key] = ev.val
            sem, val = ev.sem, ev.val
            self.meta.append(("wait", id(sem), val))
            self.ops.append(lambda e, sem=sem, val=val: e.wait_ge(sem, val))

    def op(self, fn, waits=(), signal=True):
        self.wait(*waits)
        if signal:
            self.count += 1
            sem, val = self.sem, self.count
            self.meta.append(("inc", id(sem), 1))
            self.ops.append(lambda e, fn=fn, sem=sem: fn(e).then_inc(sem, 1))
            return Ev(sem, val)
        self.ops.append(lambda e, fn=fn: fn(e))
        return None

    def last(self):
        return Ev(self.sem, self.count) if self.count else None

    def dma(self, out, in_, dsem, waits=(), **kw):
        self.wait(*waits)
        dsem.count += 16
        sem = dsem.sem
        self.meta.append(("inc", id(sem), 16))
        self.ops.append(lambda e, out=out, in_=in_, sem=sem, kw=kw: e.dma_start(out=out, in_=in_, **kw).then_inc(sem, 16))
        return Ev(sem, dsem.count)


class DSem:
    def __init__(self, sem):
        self.sem = sem
        self.count = 0


class Prog:
    def __init__(self, nc):
        self.nc = nc
        self.pe = Eng("tensor")
        self.act = Eng("scalar")
        self.dve = Eng("vector")
        self.pool = Eng("gpsimd")
        self.sp = Eng("sync")
        self.cengs = [self.pe, self.act, self.dve, self.pool]
        self.engs = self.cengs + [self.sp]
        self._ctx = []
        for e in self.engs:
            e.sem = self.new_sem("p_" + e.name)
        self.nsb = 0

    def new_sem(self, name):
        cm = self.nc.semaphore(name)
        h = cm.__enter__()
        self._ctx.append(cm)
        return h

    def dsem(self, name):
        return DSem(self.new_sem(name))

    def sb(self, name, shape, dtype, off):
        esz = 2 if dtype == BF16 else 4
        n = 1
        for s_ in shape[1:]:
            n *= s_
        assert off % 32 == 0, (name, off)
        assert off + n * esz <= ARENA_BYTES, (name, off, n * esz)
        self.nsb += 1
        return self.nc.alloc_sbuf_tensor_at("%s_%d" % (name, self.nsb), list(shape), dtype, offset=ARENA_BASE + off)

    def barrier(self):
        evs = [e.last() for e in self.cengs]
        for e in self.engs:
            e.wait(*evs)

    def check_deadlock(self):
        semv = {}
        pos = {e.name: 0 for e in self.engs}
        progress = True
        while progress:
            progress = False
            for e in self.engs:
                while pos[e.name] < len(e.meta):
                    kind, sid, val = e.meta[pos[e.name]]
                    if kind == "wait":
                        if semv.get(sid, 0) >= val:
                            pos[e.name] += 1
                            progress = True
                        else:
                            break
                    else:
                        semv[sid] = semv.get(sid, 0) + val
                        pos[e.name] += 1
                        progress = True
        stuck = {e.name: (pos[e.name], len(e.meta), e.meta[pos[e.name]]) for e in self.engs if pos[e.name] < len(e.meta)}
        if stuck:
            raise RuntimeError("static deadlock: %r" % (stuck,))

    def run(self):
        self.check_deadlock()
        nc = self.nc
        with nc.Block() as block:
            @block.tensor
            def _(e):
                for f in self.pe.ops:
                    f(e)

            @block.scalar
            def _(e):
                for f in self.act.ops:
                    f(e)

            @block.vector
            def _(e):
                for f in self.dve.ops:
                    f(e)

            @block.gpsimd
            def _(e):
                for f in self.pool.ops:
                    f(e)

            @block.sync
            def _(e):
                for f in self.sp.ops:
                    f(e)
        for cm in reversed(self._ctx):
            cm.__exit__(None, None, None)


ARENA_BASE = 18432
ARENA_BYTES = 229376 - ARENA_BASE


class Ring:
    def __init__(self, items):
        self.items = list(items)
        self.free = [None] * len(self.items)
        self.i = 0

    def get(self):
        k = self.i % len(self.items)
        self.i += 1
        return k, self.items[k], self.free[k]

    def release(self, k, ev):
        self.free[k] = ev


def build_program(taps=(), stop_after=None):
    nc = bass.Bass("TRN2", target_bir_lowering=False)

    def din(name, shape, dt=F32):
        return nc.dram_tensor(name, list(shape), dt, kind="ExternalInput").ap()

    xh = din("xh", [HT, D])
    xo = din("xo", [T, D])
    wq_d = din("wq", [6, 128, 8, 128])
    wk_d = din("wk", [6, 128, 8, 128])
    wv_d = din("wv", [3, 128, 8, 256])
    wu_d = din("wu", [2, 128, 8, 128])
    wg_d = din("wg", [8, 128, 2, 8, 128])
    wmix_d = din("wmix", [4, 64, 64])
    wpa_d = din("wpa", [128, 2, D])
    sel_d = din("sel", [64, 2, 128])
    wpp_d = din("wpp", [128, 2, D])
    wout_d = din("wout", [128, 8, D])
    wgu_d = din("wgu", [NJ, 128, 2, 8, 128])
    wdn_d = din("wdn", [128, NJ, D])
    gpre_d = din("gpre", [128, 8])
    gffn_d = din("gffn", [128, 8])
    gpost_d = din("gpost", [128, D])
    gfin_d = din("gfin", [128, D])
    pscale_d = din("pscale", [128, 2])
    invw_d = din("invw", [128, 2])
    invc_d = din("invc", [128, 2, 16])
    cos_d = din("cost", [128, HT + T])
    sin_d = din("sint", [128, HT + T])
    ident_d = din("ident", [128, 128])
    rotm_d = din("rotm", [128, 128])
    masks_d = din("masks", [128, 4, 512])
    out_d = nc.dram_tensor("out", [T, D], F32, kind="ExternalOutput").ap()
    x1s_d = nc.dram_tensor("x1s", [T, D], F32, kind="Internal").ap()
    tap_d = {}
    for name, shape, dt in taps:
        tap_d[name] = nc.dram_tensor("tap_" + name, list(shape), dt, kind="ExternalOutput").ap()

    P = Prog(nc)
    pe, act, dve, pool, sp = P.pe, P.act, P.dve, P.pool, P.sp
    d_tap = P.dsem("d_tap")

    def tap(name, ap):
        if name not in tap_d:
            return
        P.barrier()
        nd = len(tap_d[name].shape)
        idx = tuple(slice(None) for _ in range(nd))
        evt = sp.dma(tap_d[name][idx], ap, d_tap)
        for e_ in P.engs:
            e_.wait(evt)

    def finish():
        P.barrier()
        P.run()
        pscm.__exit__(None, None, None)
        return nc
    pscm = nc.psum_tensor("ps", [128, 8, 512], F32)
    ps = pscm.__enter__()
    bank_free = [None] * 8

    o = 0

    def take(nbytes):
        nonlocal o
        r = o
        o += (nbytes + 63) // 64 * 64
        return r

    ident = P.sb("ident", [128, 128], BF16, take(256))
    rotm = P.sb("rotm", [128, 128], BF16, take(256))
    masks = P.sb("masks", [128, 4, 512], BF16, take(4096))
    gpre = P.sb("gpre", [128, 8], F32, take(32))
    gffn = P.sb("gffn", [128, 8], F32, take(32))
    pscale = P.sb("pscale", [128, 2], F32, take(8))
    invw = P.sb("invw", [128, 2], F32, take(8))
    invc = P.sb("invc", [128, 2, 16], F32, take(128))
    uh = P.sb("uh", [128, 2, 16], F32, take(128))
    ssA = P.sb("ssA", [128, 48], F32, take(192))
    msA = P.sb("msA", [128, 48], F32, take(192))
    rsA = P.sb("rsA", [128, 48], F32, take(192))
    nhalf = P.sb("nhalf", [128, 1], F32, take(64))
    onesf = P.sb("onesf", [128, 64], F32, take(256))
    wmixb = P.sb("wmixb", [128, 2, 128], BF16, take(512))
    assert o <= 6 * KB + 512, o
    o = 7 * KB
    R_HT = take(32 * KB)
    R_K = take(35328)
    R_Q = take(24 * KB)
    R_V = take(69 * 4 * 66 * 2)
    R_W = take(24 * KB)
    R_WS = take(8 * KB)
    R_ST = take(17 * KB)
    R_ZT = take(8 * KB)
    R_YP = take(8 * KB)
    R_PT = take(4 * KB)
    assert o <= ARENA_BYTES, o

    hT = P.sb("hT", [128, 8, 2048], BF16, R_HT)
    kT = [P.sb("kT0", [128, 2, 2176], BF16, R_K),
          P.sb("kT1", [128, 2, 2560], BF16, R_K + 8704),
          P.sb("kT2", [128, 2, 4096], BF16, R_K + 8704 + 10240)]
    KOFF = [1920, 1536, 0]
    qT = P.sb("qT", [128, 6, 2048], BF16, R_Q)
    vA = P.sb("vA", [128, 69, 4, 66], BF16, R_V)

    d_c = P.dsem("d_const")
    act.dma(gpre[:, :], gpre_d[:, :], d_c)
    act.dma(gffn[:, :], gffn_d[:, :], d_c)
    act.dma(pscale[:, :], pscale_d[:, :], d_c)
    act.dma(invw[:, :], invw_d[:, :], d_c)
    ev_const = act.dma(invc[:, :, :], invc_d[:, :, :], d_c)
    ev_ms1 = dve.op(lambda e: e.memset(nhalf[:, :], -0.5))
    ev_ms2 = dve.op(lambda e: e.memset(onesf[:, :], 1.0))
    ev_ms3 = dve.op(lambda e: e.memset(vA[:, 48:69, :, 64:66], 1.0))
    ev_ms4 = dve.op(lambda e: e.memset(wmixb[:, :, :], 0.0))
    pool.wait(ev_ms1)
    dve.wait(ev_const)

    wqk = P.sb("wqk", [128, 12, 8, 128], BF16, R_W)
    wvr = P.sb("wvr", [128, 3, 8, 256], BF16, R_YP)
    wu_t = [P.sb("wu%d" % i, [128, 8, 128], BF16, R_WS + i * 2 * KB) for i in range(2)]
    pq = []
    pq_sem = {}
    pq_ev = {}
    pq_left = {}

    def pq_add(group, out, in_, waits=()):
        if group not in pq_sem:
            pq_sem[group] = P.dsem("d_pq_" + group)
            pq_left[group] = 0
        pq_left[group] += 1
        pq.append((group, out, in_, waits))

    def pq_issue(n=1):
        for _ in range(n):
            if not pq:
                return
            group, out, in_, waits = pq.pop(0)
            pq_ev[group] = pool.dma(out, in_, pq_sem[group], waits=list(waits))
            pq_left[group] -= 1

    def pq_need(group):
        while pq_left[group] > 0:
            pq_issue(1)
        return pq_ev[group]

    pq_add("ident", ident[:, :], ident_d[:, :])
    pq_add("rotm", rotm[:, :], rotm_d[:, :])
    for c in (10, 11):
        pq_add("wk2", wqk[:, c, :, :], wk_d[c - 6])
    for g in range(3):
        pq_add("wv", wvr[:, g, :, :], wv_d[g])
    for c in (8, 9, 6, 7):
        pq_add("wk%d" % ((c - 6) // 2), wqk[:, c, :, :], wk_d[c - 6])
    for c in range(2):
        pq_add("wu", wu_t[c][:, :, :], wu_d[c])
    for c in range(6):
        pq_add("wq", wqk[:, c, :, :], wq_d[c])
    pq_add("masks", masks[:, :, :], masks_d[:, :, :])
    for g in range(4):
        c, hf = g // 2, g % 2
        pq_add("wmix", wmixb[64 * hf:64 * hf + 64, c, 64 * hf:64 * hf + 64], wmix_d[g, :, :], waits=[ev_ms4])
    pe.wait(pq_need("ident"))

    ss_col = [0]

    rstd_mode = {"act": False}

    def rstd_chain(ev_ss, col):
        if rstd_mode["act"]:
            e1 = act.op(lambda e: e.activation(out=msA[:, col:col + 1], in_=ssA[:, col:col + 1], func=AF.Sqrt, scale=1.0 / D, bias=EPS), waits=[ev_ss])
            return dve.op(lambda e: e.reciprocal(out=rsA[:, col:col + 1], in_=msA[:, col:col + 1]), waits=[e1])
        e1 = pool.op(lambda e: e.tensor_scalar(out=msA[:, col:col + 1], in0=ssA[:, col:col + 1], scalar1=1.0 / D, scalar2=EPS,
                                               op0=ALU.mult, op1=ALU.add), waits=[ev_ss])
        e2 = pool.op(lambda e: e.tensor_tensor(out=rsA[:, col:col + 1], in0=msA[:, col:col + 1], in1=nhalf[:, :], op=ALU.pow), waits=[e1])
        return e2

    NXB = 4
    NXB_MAX = 10
    xst = [P.sb("xst%d" % i, [128, D], F32, (R_ST + i * 4 * KB) if i < 2 else (R_ZT + (i - 2) * 4 * KB)) for i in range(NXB)]
    xst += [P.sb("xst%d" % (4 + i), [128, D], F32, R_HT + i * 4 * KB) for i in range(NXB_MAX - NXB)]
    xsb = [P.sb("xsb%d" % i, [128, D], BF16, R_ST + 8 * KB + i * 2 * KB) for i in range(2)]
    junks = [P.sb("junk%d" % i, [128, D], BF16, R_ST + 12 * KB + i * 2 * KB) for i in range(2)]
    junk_st = {"i": 0, "ev": [None, None]}

    def sq_accum(in_ap, col, waits, jl=None, st=None):
        jl = junks if jl is None else jl
        st = junk_st if st is None else st
        k = st["i"] % 2
        st["i"] += 1
        ev = act.op(lambda e: e.activation(out=jl[k][:, :], in_=in_ap, func=AF.Square, accum_out=ssA[:, col:col + 1]),
                    waits=list(waits) + [st["ev"][k]])
        st["ev"][k] = ev
        return ev
    d_x = [P.dsem("d_x%d" % i) for i in range(NXB_MAX)]
    xst_free = [None] * NXB_MAX
    pa_nxb = [NXB]
    xsb_free = [None, None]
    tr_ring = Ring([0, 1])
    pa_state = {"q": [], "n": 0, "lag": 1}

    def pa_stage2(pd):
        (xin, xb, col, ev_rs, gvec, dst, dst_col, b2, n) = pd
        if n % 2 == 0 or pa_state.get("all_act"):
            ev_xs = act.op(lambda e: e.activation(out=xsb[b2][:, :], in_=xin[:, :], func=AF.Copy, scale=rsA[:, col:col + 1]),
                           waits=[ev_rs, xsb_free[b2]])
        else:
            ev_xs = dve.op(lambda e: e.tensor_scalar(out=xsb[b2][:, :], in0=xin[:, :], scalar1=rsA[:, col:col + 1], scalar2=None, op0=ALU.mult),
                           waits=[ev_rs, xsb_free[b2]])
        xst_free[xb] = ev_xs
        k_, bank, fr = tr_ring.get()
        psb = ps[:, bank, :].bitcast(BF16).rearrange("p (k t) -> p k t", k=8)
        pe.wait(ev_xs, fr, bank_free[bank])
        ev_tr = None
        for k in range(8):
            ev_tr = pe.op(lambda e, k=k: e.transpose(out=psb[:, k, :], in_=xsb[b2][:, k * 128:(k + 1) * 128], identity=ident[:, :]),
                          signal=(k == 7))
        xsb_free[b2] = ev_tr
        ev_ev = dve.op(lambda e: e.tensor_tensor(out=dst[:, :, dst_col:dst_col + 128], in0=psb,
                                                 in1=gvec[:, :].unsqueeze(2).to_broadcast([128, 8, 128]), op=ALU.mult), waits=[ev_tr])
        tr_ring.release(k_, ev_ev)
        bank_free[bank] = ev_ev
        return ev_ev

    def pa_issue_load(src_rows, idx):
        xb = idx % pa_nxb[0]
        return sp.dma(xst[xb][:, :], src_rows, d_x[xb], waits=[xst_free[xb]])

    def norm_transpose_tile(ev_ld, gvec, dst, dst_col, idx):
        xb = idx % pa_nxb[0]
        xin = xst[xb]
        col = ss_col[0] % 48
        ss_col[0] += 1
        ev_sq = sq_accum(xin[:, :], col, [ev_ld])
        ev_rs = rstd_chain(ev_sq, col)
        n = pa_state["n"]
        pa_state["n"] += 1
        pa_state["q"].append((xin, xb, col, ev_rs, gvec, dst, dst_col, n % 2, n))
        while len(pa_state["q"]) > pa_state["lag"]:
            pa_stage2(pa_state["q"].pop(0))

    def pa_flush():
        while pa_state["q"]:
            pa_stage2(pa_state["q"].pop(0))

    def phase_a(src, ntiles, gvec, dst, between=None, pq_from=0, nxb=NXB, lag=1):
        pa_nxb[0] = nxb
        pa_state["lag"] = lag
        NXB = nxb
        lds = {}
        for i in range(min(NXB - lag, ntiles)):
            lds[i] = pa_issue_load(src[i * 128:(i + 1) * 128, :], i)
        for i in range(ntiles):
            if i >= pq_from:
                pq_issue(2 if len(pq) > 12 else 1)
            norm_transpose_tile(lds[i], gvec, dst, i * 128, i)
            j = i + NXB - lag
            if j < ntiles:
                lds[j] = pa_issue_load(src[j * 128:(j + 1) * 128, :], j)
            if between is not None:
                between(i)
        pa_flush()

    pj_ring = Ring([2, 3, 4, 5, 6, 7])

    def proj_fm(wfn, rhsfn, n, extra_waits=()):
        k_, bank, fr = pj_ring.get()
        pe.wait(fr, bank_free[bank], *extra_waits)
        ev = None
        for k in range(8):
            ev = pe.op(lambda e, k=k: e.matmul(ps[:, bank, 0:n], lhsT=wfn(k), rhs=rhsfn(k), start=(k == 0), stop=(k == 7)),
                       signal=(k == 7))
        return k_, bank, ev

    RS = R_V + 8 * KB
    cs = [[P.sb("cos%d" % i, [128, 512], F32, RS + 0 * KB + i * 2 * KB), P.sb("sin%d" % i, [128, 512], F32, RS + 4 * KB + i * 2 * KB)]
          for i in range(2)]
    qb_t = [P.sb("qb%d" % i, [128, 512], BF16, RS + 8 * KB + i * KB) for i in range(2)]
    t1_t = [P.sb("t1_%d" % i, [128, 512], F32, RS + 10 * KB + i * 2 * KB) for i in range(2)]
    assert RS + 14 * KB <= R_V + 25344
    t2_t = [P.sb("t2b_%d" % i, [128, 512], F32, R_WS + 4 * KB + i * 2 * KB) for i in range(2)]
    d_cs = [P.dsem("d_cs%d" % i) for i in range(2)]
    cs_free = [None, None]
    cs_i = [0]

    def load_tables(col0, n=512):
        b = cs_i[0] % 2
        cs_i[0] += 1
        sp.dma(cs[b][0][:, 0:n], cos_d[:, col0:col0 + n], d_cs[b], waits=[cs_free[b]])
        ev = sp.dma(cs[b][1][:, 0:n], sin_d[:, col0:col0 + n], d_cs[b])
        return b, ev

    def q_dst(c, b):
        g = c // 2
        if g == 0:
            return qT[:, c, b * 512:(b + 1) * 512]
        if g == 1:
            return (qT[:, c, :].rearrange("p (r L) -> p r L", r=4)[:, :, b * 128:(b + 1) * 128], 4)
        return (qT[:, c, :].rearrange("p (r L) -> p r L", r=16)[:, :, b * 32:(b + 1) * 32], 16)

    def k_dst(g, pr, e0, n):
        if g == 0:
            return kT[0][:, pr, e0 - KOFF[0]:e0 - KOFF[0] + n]
        if g == 1:
            j0 = (e0 - KOFF[1]) // 4
            return (kT[1][:, pr, :].rearrange("p (r L) -> p r L", r=4)[:, :, j0:j0 + n // 4], 4)
        a0 = e0 // 16
        return (kT[2][:, pr, :].rearrange("p (r L) -> p r L", r=16)[:, :, a0:a0 + n // 16], 16)

    rope_state = {"pending": None, "u": 0, "qb_free": [None, None], "t1_free": [None, None], "t2_free": [None, None]}

    def rope_finish(pd):
        (ub, n, bankA, kA, evA, ev_qb, csb, ev_cs, dst, users) = pd
        kB, bankB, frB = pj_ring.get()
        ev_rot = pe.op(lambda e: e.matmul(ps[:, bankB, 0:n], lhsT=rotm[:, :], rhs=qb_t[ub][:, 0:n], start=True, stop=True),
                       waits=[ev_qb, frB, bank_free[bankB]])
        rope_state["qb_free"][ub] = ev_rot
        ev_t1 = dve.op(lambda e: e.tensor_tensor(out=t1_t[ub][:, 0:n], in0=ps[:, bankA, 0:n], in1=cs[csb][0][:, 0:n], op=ALU.mult),
                       waits=[evA, ev_cs, rope_state["t1_free"][ub]])
        ev_t2 = dve.op(lambda e: e.tensor_tensor(out=t2_t[ub][:, 0:n], in0=ps[:, bankB, 0:n], in1=cs[csb][1][:, 0:n], op=ALU.mult),
                       waits=[ev_rot, rope_state["t2_free"][ub]])
        pj_ring.release(kA, ev_t1)
        bank_free[bankA] = ev_t1
        pj_ring.release(kB, ev_t2)
        bank_free[bankB] = ev_t2
        if isinstance(dst, tuple):
            dst_ap, rr = dst
            i0 = t1_t[ub][:, 0:n].rearrange("p (j r) -> p r j", r=rr)
            i1 = t2_t[ub][:, 0:n].rearrange("p (j r) -> p r j", r=rr)
            ev_o = pool.op(lambda e: e.tensor_tensor(out=dst_ap, in0=i0, in1=i1, op=ALU.add), waits=[ev_t1, ev_t2])
        else:
            ev_o = pool.op(lambda e: e.tensor_tensor(out=dst, in0=t1_t[ub][:, 0:n], in1=t2_t[ub][:, 0:n], op=ALU.add), waits=[ev_t1, ev_t2])
        rope_state["t1_free"][ub] = ev_o
        rope_state["t2_free"][ub] = ev_o
        users.append(ev_t2)
        return ev_o

    def rope_unit(wfn, rhsfn, n, csb, ev_cs, dst, users, extra_waits=()):
        u = rope_state["u"]
        rope_state["u"] += 1
        ub = u % 2
        kA, bankA, evA = proj_fm(wfn, rhsfn, n, extra_waits)
        if rope_state.get("qb_dve"):
            ev_qb = dve.op(lambda e: e.tensor_copy(out=qb_t[ub][:, 0:n], in_=ps[:, bankA, 0:n]), waits=[evA, rope_state["qb_free"][ub]])
        else:
            ev_qb = act.op(lambda e: e.activation(out=qb_t[ub][:, 0:n], in_=ps[:, bankA, 0:n], func=AF.Copy),
                           waits=[evA, rope_state["qb_free"][ub]])
        prev = rope_state["pending"]
        rope_state["pending"] = (ub, n, bankA, kA, [evA, ev_qb], ev_qb, csb, ev_cs, dst, users)
        if prev is not None:
            return rope_finish(prev)
        return None

    def rope_flush():
        prev = rope_state["pending"]
        rope_state["pending"] = None
        if prev is not None:
            return rope_finish(prev)
        return None

    hTh = P.sb("hTh", [128, 8, 2048], BF16, R_Q)
    phase_a(xh, HT // 128, gpre, hTh, pq_from=9, nxb=NXB_MAX, lag=2)
    pe.wait(dve.last())

    vflip = [0]

    def v_block(wv_tile, g, tok_ap_fn, blk, extra_waits=(), evac=None):
        k_, bank, fr = pj_ring.get()
        pe.wait(fr, bank_free[bank], *extra_waits)
        ev = None
        for k in range(8):
            ev = pe.op(lambda e, k=k: e.matmul(ps[:, bank, 0:256], lhsT=tok_ap_fn(k), rhs=wv_tile[:, g, k, :], start=(k == 0), stop=(k == 7)),
                       signal=(k == 7))
        src = ps[:, bank, 0:256].rearrange("p (h d) -> p h d", h=4)
        vflip[0] += 1
        if evac == "act" or (evac is None and vflip[0] % 2):
            ev2 = act.op(lambda e: e.activation(out=vA[:, blk, :, 0:64], in_=src, func=AF.Copy), waits=[ev])
        else:
            ev2 = dve.op(lambda e: e.tensor_copy(out=vA[:, blk, :, 0:64], in_=src), waits=[ev])
        pj_ring.release(k_, ev2)
        bank_free[bank] = ev2
        return ev2

    def blk_idx(g, s_, m):
        if g == 0:
            return 48 if m == 0 else m - 1
        if g == 1:
            return 49 + s_ if m == 0 else 16 + 4 * s_ + (m - 1)
        return 53 + s_ if m == 0 else 32 + s_

    users = []
    items = []
    blk_state = {}

    def it_tables(col0, n=512):
        def f():
            blk_state["cs"] = load_tables(col0, n)
        return f

    def it_rope(wfn, rhsfn, n, dst, ew):
        def f():
            csb, ev_cs = blk_state["cs"]
            rope_unit(wfn, rhsfn, n, csb, ev_cs, dst, users, extra_waits=ew())
        return f

    def it_endblk():
        def f():
            rope_flush()
            cs_free[blk_state["cs"][0]] = users[-1]
        return f

    for b in range(4):
        items.append(it_tables(b * 512))
        for pr in range(2):
            items.append(it_rope(lambda k, pr=pr: wqk[:, 10 + pr, k, :], lambda k, b=b: hTh[:, k, b * 512:(b + 1) * 512], 512,
                                 k_dst(2, pr, b * 512, 512), lambda: [pq_need("wk2"), pq_need("rotm")]))
        if b == 3:
            for pr in range(2):
                items.append(it_rope(lambda k, pr=pr: wqk[:, 8 + pr, k, :], lambda k: hTh[:, k, 1536:2048], 512,
                                     k_dst(1, pr, 1536, 512), lambda: [pq_need("wk1")]))
        items.append(it_endblk())
    items.append(it_tables(1920, 128))
    for pr in range(2):
        items.append(it_rope(lambda k, pr=pr: wqk[:, 6 + pr, k, :], lambda k: hTh[:, k, 1920:2048], 128, kT[0][:, pr, 0:128], lambda: [pq_need("wk0")]))
    items.append(it_endblk())
    items.append(lambda: v_block(wvr, 0, lambda k: hTh[:, k, 1920:2048], blk_idx(0, 0, 0), extra_waits=[pq_need("wv")], evac="dve"))
    for rho in range(4):
        items.append(lambda rho=rho: v_block(wvr, 1, lambda k, rho=rho: hTh[:, k, 1536 + rho:2048:4], blk_idx(1, rho, 0), extra_waits=[pq_need("wv")], evac="dve"))
    for r in range(16):
        items.append(lambda r=r: v_block(wvr, 2, lambda k, r=r: hTh[:, k, r:2048:16], blk_idx(2, r, 0), extra_waits=[pq_need("wv")], evac="dve"))

    def it_u(c):
        def f():
            k_, bank, ev = proj_fm(lambda k, c=c: wu_t[c][:, k, :], lambda k: hTh[:, k, 2032:2048], 16, extra_waits=[pq_need("wu")])
            ev2 = dve.op(lambda e, c=c, bank=bank: e.tensor_copy(out=uh[:, c, :], in_=ps[:, bank, 0:16]), waits=[ev])
            pj_ring.release(k_, ev2)
            bank_free[bank] = ev2
        return f
    for c in range(2):
        items.append(it_u(c))

    def between(i):
        for _ in range(3):
            if items:
                items.pop(0)()

    pa_state["all_act"] = True
    rope_state["qb_dve"] = True
    rstd_mode["act"] = True
    phase_a(xo, NT, gpre, hT, between=between)
    while items:
        items.pop(0)()
    pa_state["all_act"] = False
    rope_state["qb_dve"] = False
    rstd_mode["act"] = False
    P.barrier()

    users = []
    for b in range(4):
        csb, ev_cs = load_tables(HT + b * 512)
        for c in range(12):
            if c < 6:
                dst = q_dst(c, b)
            else:
                dst = k_dst((c - 6) // 2, (c - 6) % 2, HT + b * 512, 512)
            rope_unit(lambda k, c=c: wqk[:, c, k, :], lambda k, b=b: hT[:, k, b * 512:(b + 1) * 512], 512, csb, ev_cs, dst, users,
                      extra_waits=[pq_need("wq")])
        rope_flush()
        cs_free[csb] = users[-1]
    P.barrier()

    ev_ones = dve.op(lambda e: e.memset(vA[:, 0:48, :, 64:66], 1.0))
    X = P.sb("X", [128, 2, 2064], F32, R_W)
    ev_uhc = []
    for c in range(2):
        ev_uhc.append(dve.op(lambda e, c=c: e.tensor_copy(out=X[:, c, 0:16], in_=uh[:, c, :])))
        for b in range(4):
            k_, bank, ev = proj_fm(lambda k, c=c: wu_t[c][:, k, :], lambda k, b=b: hT[:, k, b * 512:(b + 1) * 512], 512)
            ev2 = act.op(lambda e, c=c, b=b, bank=bank: e.activation(out=X[:, c, 16 + b * 512:16 + (b + 1) * 512], in_=ps[:, bank, 0:512], func=AF.Copy),
                         waits=[ev])
            pj_ring.release(k_, ev2)
            bank_free[bank] = ev2
    ev_X = act.last()
    Y = P.sb("Y", [128, 2064], F32, R_ST)
    Z = P.sb("Z", [128, 2064], F32, R_ST + 8256)
    zT = P.sb("zT", [128, 2, 2048], BF16, R_ZT)
    ypool = P.sb("ypool", [128, 2, 2048], BF16, R_YP)
    L = 2064
    pev = [ev_X] + ev_uhc
    engs2 = [dve, pool]
    for c in range(2):
        eng = engs2[c]
        e1 = eng.op(lambda e, c=c: e.tensor_tensor(out=Y[:, 1:L], in0=X[:, c, 1:L], in1=X[:, c, 0:L - 1], op=ALU.add), waits=pev)
        if c == 0:
            e2 = eng.op(lambda e: e.tensor_tensor(out=Z[64:128, 3:L], in0=Y[64:128, 3:L], in1=Y[64:128, 1:L - 2], op=ALU.add), waits=[e1])
            fin = [(0, 64, Y), (64, 128, Z)]
            elast = e2
        else:
            e2 = eng.op(lambda e: e.tensor_tensor(out=Z[:, 3:L], in0=Y[:, 3:L], in1=Y[:, 1:L - 2], op=ALU.add), waits=[e1])
            e3 = eng.op(lambda e: e.tensor_tensor(out=Y[:, 7:L], in0=Z[:, 7:L], in1=Z[:, 3:L - 4], op=ALU.add), waits=[e2])
            e4 = eng.op(lambda e: e.tensor_tensor(out=Z[64:128, 15:L], in0=Y[64:128, 15:L], in1=Y[64:128, 7:L - 8], op=ALU.add), waits=[e3])
            fin = [(0, 64, Y), (64, 128, Z)]
            elast = e4
        evz = []
        for (p0, p1, Sx) in fin:
            ez = dve.op(lambda e, c=c, p0=p0, p1=p1, Sx=Sx: e.scalar_tensor_tensor(out=zT[p0:p1, c, :], in0=Sx[p0:p1, 16:L], scalar=invw[p0:p1, c:c + 1],
                                                                                     in1=X[p0:p1, c, 16:L], op0=ALU.mult, op1=ALU.subtract),
                        waits=[elast] + pev)
            ez1 = dve.op(lambda e, c=c, p0=p0, p1=p1, Sx=Sx: e.tensor_tensor(out=Sx[p0:p1, 16:32], in0=Sx[p0:p1, 16:32], in1=invc[p0:p1, c, :], op=ALU.mult),
                         waits=[ez])
            ez2 = dve.op(lambda e, c=c, p0=p0, p1=p1, Sx=Sx: e.tensor_tensor(out=zT[p0:p1, c, 0:16], in0=Sx[p0:p1, 16:32], in1=X[p0:p1, c, 16:32], op=ALU.subtract),
                         waits=[ez1])
            evz.append(ez2)
        pev = evz
    for n in range(16):
        v_block(wvr, 0, lambda k, n=n: hT[:, k, n * 128:(n + 1) * 128], blk_idx(0, 0, 1 + n), evac="act")
    for rho in range(4):
        for n1 in range(4):
            v_block(wvr, 1, lambda k, rho=rho, n1=n1: hT[:, k, 512 * n1 + rho:512 * (n1 + 1):4], blk_idx(1, rho, 1 + n1), evac="act")
    for r in range(16):
        v_block(wvr, 2, lambda k, r=r: hT[:, k, r:2048:16], blk_idx(2, r, 1), evac="act")
    ev_z = pev
    for c in range(2):
        for b in range(4):
            k_, bank, fr = pj_ring.get()
            ev = pe.op(lambda e, c=c, b=b, bank=bank: e.matmul(ps[:, bank, 0:512], lhsT=wmixb[:, c, :], rhs=zT[:, c, b * 512:(b + 1) * 512], start=True, stop=True),
                       waits=[fr, bank_free[bank], pq_need("wmix")] + ev_z)
            ev2 = act.op(lambda e, c=c, b=b, bank=bank: e.activation(out=ypool[:, c, b * 512:(b + 1) * 512], in_=ps[:, bank, 0:512], func=AF.Copy,
                                                                      scale=pscale[:, c:c + 1]), waits=[ev])
            pj_ring.release(k_, ev2)
            bank_free[bank] = ev2
    P.barrier()

    acc = P.sb("acc", [128, 4, 2048], F32, R_W)
    PTc = [P.sb("PTc%d" % i, [128, 512], BF16, R_ST + i * KB) for i in range(3)]
    PTp = [P.sb("PTp%d" % i, [128, 512], BF16, R_ST + 4 * KB + i * KB) for i in range(3)]
    pt_free = [None, None, None]
    s_ring = Ring([(0, 1), (2, 3), (4, 5)])
    o_ring = Ring([6, 7])
    MC, MP, MH, MHP = 0, 1, 2, 3

    def qcols(g, s_, slot):
        if g == 0:
            n = 4 * s_ + slot
            return slice(n * 128, (n + 1) * 128)
        if g == 1:
            return slice(s_ * 512 + 128 * slot, s_ * 512 + 128 * (slot + 1))
        r = 4 * s_ + slot
        return slice(r * 128, (r + 1) * 128)

    def kcols(g, s_, slot, prev):
        if g == 0:
            n = 4 * s_ + slot
            kb = n + (0 if prev else 1)
            return slice(kb * 128, (kb + 1) * 128), blk_idx(0, 0, kb)
        if g == 1:
            m = slot + (0 if prev else 1)
            return slice(s_ * 640 + 128 * m, s_ * 640 + 128 * (m + 1)), blk_idx(1, s_, m)
        r = 4 * s_ + slot
        m = 0 if prev else 1
        return slice(r * 256 + 128 * m, r * 256 + 128 * (m + 1)), blk_idx(2, r, m)

    def acc_dst(g, s_, h):
        if g == 0:
            return acc[0:65, h, s_ * 512:(s_ + 1) * 512], None
        if g == 1:
            return acc[0:65, h, s_:2048:4], None
        a3 = acc[0:65, h, :].rearrange("p (a r) -> p r a", r=16)[:, 4 * s_:4 * s_ + 4, :]
        return a3, "p (r a) -> p r a"

    att_q = []
    acc_g0_ev = [None]
    dve.wait(pq_need("masks"))

    def att_finish(pd):
        (g, s_, h, ub, ev_mc, ev_mp) = pd
        ko, bankO, frO = o_ring.get()
        pe.wait(frO, bank_free[bankO], ev_mc, ev_mp)
        ev = None
        for slot in range(4):
            _, bp = kcols(g, s_, slot, True)
            _, bc = kcols(g, s_, slot, False)
            pe.op(lambda e, slot=slot, bp=bp: e.matmul(ps[0:65, bankO, slot * 128:(slot + 1) * 128], lhsT=vA[:, bp, h, 0:65],
                                                       rhs=PTp[ub][:, slot * 128:(slot + 1) * 128], start=True, stop=False), signal=False)
            ev = pe.op(lambda e, slot=slot, bc=bc: e.matmul(ps[0:65, bankO, slot * 128:(slot + 1) * 128], lhsT=vA[:, bc, h, 0:65],
                                                            rhs=PTc[ub][:, slot * 128:(slot + 1) * 128], start=False, stop=True), signal=(slot == 3))
        pt_free[ub] = ev
        dst, rr = acc_dst(g, s_, h)
        src = ps[0:65, bankO, :]
        if rr is not None:
            src = src.rearrange(rr, r=4)
        if g == 0:
            ev2 = act.op(lambda e: e.activation(out=dst, in_=src, func=AF.Copy), waits=[ev])
            acc_g0_ev[0] = ev2
        else:
            ev2 = dve.op(lambda e: e.tensor_tensor(out=dst, in0=src, in1=dst, op=ALU.add), waits=[ev, acc_g0_ev[0]])
        o_ring.release(ko, ev2)
        bank_free[bankO] = ev2
        return ev2

    att_u = 0
    for g in range(3):
        for s_ in range(4):
            for h in range(4):
                ub = att_u % 3
                att_u += 1
                pr, hf = h // 2, h % 2
                p0 = 64 * hf
                ks, (bC, bP), frS = s_ring.get()
                pe.wait(frS, bank_free[bC], bank_free[bP])
                evs = None
                for slot in range(4):
                    qs = qcols(g, s_, slot)
                    kc, _ = kcols(g, s_, slot, False)
                    kp, _ = kcols(g, s_, slot, True)
                    pe.op(lambda e, slot=slot, qs=qs, kc=kc, g=g, p0=p0, pr=pr, bC=bC: e.matmul(ps[:, bC, slot * 128:(slot + 1) * 128], lhsT=kT[g][p0:p0 + 64, pr, kc],
                                                                      rhs=qT[p0:p0 + 64, 2 * g + pr, qs], start=True, stop=True), signal=False)
                    evs = pe.op(lambda e, slot=slot, qs=qs, kp=kp, g=g, p0=p0, pr=pr, bP=bP: e.matmul(ps[:, bP, slot * 128:(slot + 1) * 128], lhsT=kT[g][p0:p0 + 64, pr, kp],
                                                                            rhs=qT[p0:p0 + 64, 2 * g + pr, qs], start=True, stop=True), signal=(slot == 3))
                ev_ec = act.op(lambda e, ub=ub, bC=bC: e.activation(out=PTc[ub][:, :], in_=ps[:, bC, :], func=AF.Exp, scale=0.125),
                               waits=[evs, pt_free[ub]])
                ev_ep = act.op(lambda e, ub=ub, bP=bP: e.activation(out=PTp[ub][:, :], in_=ps[:, bP, :], func=AF.Exp, scale=0.125))
                s_ring.release(ks, ev_ep)
                bank_free[bC] = ev_ep
                bank_free[bP] = ev_ep
                mp = MH if g == 2 else (MHP if (g == 1 or s_ == 0) else MP)
                ev_mc = dve.op(lambda e, ub=ub: e.tensor_tensor(out=PTc[ub][:, :], in0=PTc[ub][:, :], in1=masks[:, MC, :], op=ALU.mult), waits=[ev_ec])
                ev_mp = dve.op(lambda e, ub=ub, mp=mp: e.tensor_tensor(out=PTp[ub][:, :], in0=PTp[ub][:, :], in1=masks[:, mp, :], op=ALU.mult), waits=[ev_ep])
                att_q.append((g, s_, h, ub, ev_mc, ev_mp))
                while len(att_q) > 2:
                    att_finish(att_q.pop(0))
    while att_q:
        att_finish(att_q.pop(0))
    P.barrier()

    tap("hT", hT[:, :, :])
    tap("qT", qT[:, :, :])
    tap("kT0", kT[0][:, :, :])
    tap("kT1", kT[1][:, :, :])
    tap("kT2", kT[2][:, :, :])
    tap("vA", vA[:, :, :, :])
    tap("acc", acc[0:65, :, :])
    tap("ypool", ypool[:, :, :])
    if stop_after == "attn":
        return finish()
    wpa = P.sb("wpa", [128, 2, D], BF16, R_V)
    selb = P.sb("selb", [64, 2, 128], BF16, R_PT + 256)
    oT2 = P.sb("oT2", [128, 2, 2048], BF16, R_Q + 16 * KB)
    wpp = P.sb("wpp", [128, 2, D], BF16, R_V + 8 * KB)
    wgs = [P.sb("wgs%d" % i, [128, 2, 8, 128], BF16, R_V + 12 * KB + i * 4 * KB) for i in range(2)]
    d_wm2 = P.dsem("d_wm2")
    pool.dma(wpa[:, :, :], wpa_d[:, :, :], d_wm2)
    pool.dma(selb[:, :, :], sel_d[:, :, :], d_wm2)
    ev_wp = pool.dma(wpp[:, :, :], wpp_d[:, :, :], d_wm2)
    d_wg = [P.dsem("d_wg%d" % i) for i in range(2)]
    ev_wg_next = pool.dma(wgs[0][:, :, :, :], wg_d[0], d_wg[0])
    oT = P.sb("oT", [128, 4, 2048], BF16, R_Q)
    lnrow = [P.sb("lnrow%d" % i, [128, 2048], F32, R_ST + i * 8 * KB) for i in range(2)]
    rrow = [P.sb("rrow%d" % i, [128, 2048], F32, R_K + i * 8 * KB) for i in range(2)]
    hirow = [P.sb("hirow%d" % i, [128, 2048], BF16, R_K + 16 * KB + i * 4 * KB) for i in range(2)]
    lorow = [P.sb("lorow%d" % i, [128, 2048], BF16, R_K + 24 * KB + i * 4 * KB) for i in range(2)]
    onesb = P.sb("onesb", [128, 64], BF16, R_PT)
    ev_ob = dve.op(lambda e: e.memset(onesb[:, :], 1.0))
    row_free = [None, None]
    lo_ev = [None, None]
    oT_ev = {}
    ev_hl = []
    for h in range(4):
        hb = h % 2
        ev_ln = act.op(lambda e, h=h, hb=hb: e.activation(out=lnrow[hb][64:65, :], in_=acc[64:65, h, :], func=AF.Ln), waits=[lo_ev[hb]])
        ev_ex = act.op(lambda e, hb=hb: e.activation(out=rrow[hb][64:65, :], in_=lnrow[hb][64:65, :], func=AF.Exp, scale=-1.0), waits=[ev_ln, lo_ev[hb]])
        ev_hi = dve.op(lambda e, hb=hb: e.tensor_copy(out=hirow[hb][64:65, :], in_=rrow[hb][64:65, :]), waits=[ev_ex, row_free[hb]])
        ev_lo = pool.op(lambda e, hb=hb: e.tensor_tensor(out=lorow[hb][64:65, :], in0=rrow[hb][64:65, :], in1=hirow[hb][64:65, :], op=ALU.subtract),
                        waits=[ev_hi, row_free[hb]])
        lo_ev[hb] = ev_lo
        evu = None
        for b in range(4):
            k_, bank, fr = pj_ring.get()
            pe.op(lambda e, hb=hb, b=b, bank=bank: e.matmul(ps[0:64, bank, 0:512], lhsT=onesb[64:65, 0:64], rhs=hirow[hb][64:65, b * 512:(b + 1) * 512],
                                                            start=True, stop=False), waits=[fr, bank_free[bank], ev_hi, ev_lo, ev_ob], signal=False)
            ev = pe.op(lambda e, hb=hb, b=b, bank=bank: e.matmul(ps[0:64, bank, 0:512], lhsT=onesb[64:65, 0:64], rhs=lorow[hb][64:65, b * 512:(b + 1) * 512],
                                                                 start=False, stop=True))
            ev2 = dve.op(lambda e, h=h, b=b, bank=bank: e.tensor_tensor(out=oT[0:64, h, b * 512:(b + 1) * 512], in0=ps[0:64, bank, 0:512],
                                                                        in1=acc[0:64, h, b * 512:(b + 1) * 512], op=ALU.mult), waits=[ev])
            pj_ring.release(k_, ev2)
            bank_free[bank] = ev2
            evu = ev
            oT_ev[(h, b)] = ev2
        row_free[hb] = evu
        if h % 2 == 1:
            p_ = h // 2
            for b in range(4):
                k_, bank, fr = pj_ring.get()
                pe.op(lambda e, p_=p_, b=b, bank=bank: e.matmul(ps[:, bank, 0:512], lhsT=selb[0:64, 0, :], rhs=oT[0:64, 2 * p_, b * 512:(b + 1) * 512],
                                                                start=True, stop=False),
                      waits=[fr, bank_free[bank], ev_wp, oT_ev[(2 * p_, b)], oT_ev[(2 * p_ + 1, b)]], signal=False)
                ev = pe.op(lambda e, p_=p_, b=b, bank=bank: e.matmul(ps[:, bank, 0:512], lhsT=selb[0:64, 1, :], rhs=oT[0:64, 2 * p_ + 1, b * 512:(b + 1) * 512],
                                                                     start=False, stop=True))
                ev2 = dve.op(lambda e, p_=p_, b=b, bank=bank: e.tensor_copy(out=oT2[:, p_, b * 512:(b + 1) * 512], in_=ps[:, bank, 0:512]), waits=[ev])
                pj_ring.release(k_, ev2)
                bank_free[bank] = ev2
    P.barrier()

    tap("oT", oT[0:64, :, :])
    if stop_after == "norm":
        return finish()

    mT = P.sb("mT", [128, 8, 2048], BF16, R_K)
    sga = [P.sb("sga%d" % i, [128, 512], F32, R_V + 20 * KB + i * 2 * KB) for i in range(2)]
    sgp = [P.sb("sgp%d" % i, [128, 512], F32, R_V + 24 * KB + i * 2 * KB) for i in range(2)]
    m1 = [P.sb("m1_%d" % i, [128, 512], F32, R_ST + i * 2 * KB) for i in range(2)]
    m2 = [P.sb("m2_%d" % i, [128, 512], F32, R_ST + 4 * KB + i * 2 * KB) for i in range(2)]
    wout = P.sb("wout", [128, 8, D], BF16, R_W)
    gpost = P.sb("gpost", [128, D], F32, R_W + 16 * KB)
    wg_free = [None, None]
    d_wo = P.dsem("d_wo")
    mring = Ring([(0, 1, 2, 3), (4, 5, 6, 7)])
    sg_free = [None, None]
    m_free = [None, None]
    mu = 0
    for c in range(8):
        wb = c % 2
        ev_wg = ev_wg_next
        if c + 1 < 8:
            ev_wg_next = pool.dma(wgs[(c + 1) % 2][:, :, :, :], wg_d[c + 1], d_wg[(c + 1) % 2], waits=[wg_free[(c + 1) % 2]])
        if c == 1:
            for k in range(8):
                ev_wo = pool.dma(wout[:, k, :], wout_d[:, k, :], d_wo)
            ev_gp = sp.dma(gpost[:, :], gpost_d[:, :], P.dsem("d_gpost"))
        for b in range(4):
            ub = mu % 2
            mu += 1
            km, (b1, b2, b3, b4), frm = mring.get()
            tok = slice(b * 512, (b + 1) * 512)
            pe.wait(frm, bank_free[b1], bank_free[b2], bank_free[b3], bank_free[b4], ev_wg, ev_wp)
            for k in range(8):
                e1 = pe.op(lambda e, k=k, wb=wb, b1=b1, tok=tok: e.matmul(ps[:, b1, :], lhsT=wgs[wb][:, 0, k, :], rhs=hT[:, k, tok], start=(k == 0), stop=(k == 7)),
                           signal=(k == 7))
            for k in range(8):
                e2 = pe.op(lambda e, k=k, wb=wb, b2=b2, tok=tok: e.matmul(ps[:, b2, :], lhsT=wgs[wb][:, 1, k, :], rhs=hT[:, k, tok], start=(k == 0), stop=(k == 7)),
                           signal=(k == 7))
            for p_ in range(2):
                e3 = pe.op(lambda e, p_=p_, c=c, b3=b3, tok=tok: e.matmul(ps[:, b3, :], lhsT=wpa[:, p_, c * 128:(c + 1) * 128], rhs=oT2[:, p_, tok],
                                                                          start=(p_ == 0), stop=(p_ == 1)), signal=(p_ == 1))
            for j in range(2):
                e4 = pe.op(lambda e, j=j, c=c, b4=b4, tok=tok: e.matmul(ps[:, b4, :], lhsT=wpp[:, j, c * 128:(c + 1) * 128], rhs=ypool[:, j, tok],
                                                                        start=(j == 0), stop=(j == 1)), signal=(j == 1))
            if b == 3:
                wg_free[wb] = e2
            ea = act.op(lambda e, ub=ub, b1=b1: e.activation(out=sga[ub][:, :], in_=ps[:, b1, :], func=AF.Sigmoid), waits=[e1, sg_free[ub]])
            eb = act.op(lambda e, ub=ub, b2=b2: e.activation(out=sgp[ub][:, :], in_=ps[:, b2, :], func=AF.Sigmoid), waits=[e2])
            em1 = dve.op(lambda e, ub=ub, b3=b3: e.tensor_tensor(out=m1[ub][:, :], in0=ps[:, b3, :], in1=sga[ub][:, :], op=ALU.mult), waits=[ea, e3, m_free[ub]])
            em2 = dve.op(lambda e, ub=ub, b4=b4: e.tensor_tensor(out=m2[ub][:, :], in0=ps[:, b4, :], in1=sgp[ub][:, :], op=ALU.mult), waits=[eb, e4])
            sg_free[ub] = em2
            mring.release(km, em2)
            for bb in (b1, b2, b3, b4):
                bank_free[bb] = em2
            eo = pool.op(lambda e, ub=ub, c=c, tok=tok: e.tensor_tensor(out=mT[:, c, tok], in0=m1[ub][:, :], in1=m2[ub][:, :], op=ALU.add), waits=[em1, em2])
            m_free[ub] = eo
    P.barrier()

    tap("mT", mT[:, :, :])
    if stop_after == "merge":
        return finish()
    wdn = P.sb("wdn", [128, NJ, D], BF16, R_Q + 10 * KB)
    assert R_Q + 54 * KB <= R_W and R_K + 44 * KB <= R_Q + 10 * KB
    gfin = P.sb("gfin", [128, D], F32, R_W + 20 * KB)
    xin = [P.sb("xin%d" % i, [128, D], F32, R_ST + i * 4 * KB) for i in range(4)]
    x1b = [P.sb("x1b0", [128, D], F32, R_PT), P.sb("x1b1", [128, D], F32, R_WS), P.sb("x1b2", [128, D], F32, R_WS + 4 * KB)]
    tb = [P.sb("tb%d" % i, [128, D], F32, R_ZT + i * 4 * KB) for i in range(2)] + [P.sb("tb2", [128, D], F32, R_Q + 54 * KB)]
    assert R_Q + 58 * KB <= R_W
    xs2 = [P.sb("xs2_%d" % i, [128, D], BF16, R_YP + i * 2 * KB) for i in range(2)]
    junks2 = [P.sb("junk2_%d" % i, [128, D], BF16, R_YP + 4 * KB + i * 2 * KB) for i in range(2)]
    junk_st2 = {"i": 0, "ev": [None, None]}
    d_xi = [P.dsem("d_xi%d" % i) for i in range(4)]
    d_x1o = [P.dsem("d_x1o%d" % i) for i in range(3)]
    xin_free = [None, None, None, None]
    x1b_free = [None, None, None]
    tb_free = [None, None, None]
    xs2_free = [None, None]
    pair_ring = Ring([(2, 3), (4, 5), (6, 7)])
    tr_ring2 = Ring([0, 1])

    def pair_ap(pr_):
        return ps[:, pr_[0]:pr_[0] + 2, :].rearrange("p a b -> p (a b)")

    x1_store_ev = {}

    def wout_stage1(i):
        b = i % 4
        ev_x = sp.dma(xin[b][:, :], xo[i * 128:(i + 1) * 128, :], d_xi[b], waits=[xin_free[b]])
        kp, pr_, frp = pair_ring.get()
        pe.wait(frp, bank_free[pr_[0]], bank_free[pr_[1]], ev_wo)
        ev = None
        for hf in range(2):
            for k in range(8):
                ev = pe.op(lambda e, k=k, hf=hf, pr_=pr_: e.matmul(ps[:, pr_[0] + hf, :], lhsT=mT[:, k, i * 128:(i + 1) * 128], rhs=wout[:, k, hf * 512:(hf + 1) * 512],
                                                                  start=(k == 0), stop=(k == 7)), signal=(k == 7 and hf == 1))
        return (i, b, kp, pr_, ev, ev_x)

    def norm_part(pr_, ev_mm, col, gvec, ev_g, b):
        yap = pair_ap(pr_)
        ev_sq = sq_accum(yap, col, [ev_mm], junks2, junk_st2)
        ev_rs = rstd_chain(ev_sq, col)
        ev_t = dve.op(lambda e: e.scalar_tensor_tensor(out=tb[b][:, :], in0=yap, scalar=rsA[:, col:col + 1], in1=gvec[:, :], op0=ALU.mult, op1=ALU.mult),
                      waits=[ev_rs, tb_free[b], ev_g])
        return ev_t

    def add_part(ev_t, xres, ev_xres, outbuf, out_free, b):
        ev_o = pool.op(lambda e: e.tensor_tensor(out=outbuf[:, :], in0=tb[b][:, :], in1=xres[:, :], op=ALU.add), waits=[ev_t, ev_xres, out_free])
        tb_free[b] = ev_o
        return ev_o

    def norm_res(pr_, ev_mm, col, gvec, ev_g, xres, ev_xres, outbuf, out_free, b):
        ev_t = norm_part(pr_, ev_mm, col, gvec, ev_g, b)
        ev_o = add_part(ev_t, xres, ev_xres, outbuf, out_free, b)
        return ev_t, ev_o

    def wout_stage2a(st):
        (i, b, kp, pr_, ev_mm, ev_x) = st
        col = ss_col[0] % 48
        ss_col[0] += 1
        ev_t = norm_part(pr_, ev_mm, col, gpost, ev_gp, i % 3)
        pair_ring.release(kp, ev_t)
        bank_free[pr_[0]] = ev_t
        bank_free[pr_[1]] = ev_t
        return (i, b, ev_t, ev_x)

    def wout_stage2add(st):
        (i, b, ev_t, ev_x) = st
        xb = i % 3
        ev_o = add_part(ev_t, xin[b], ev_x, x1b[xb], x1b_free[xb], i % 3)
        xin_free[b] = ev_o
        ev_st = sp.dma(x1s_d[i * 128:(i + 1) * 128, :], x1b[xb][:, :], d_x1o[xb], waits=[ev_o])
        x1_store_ev[i] = ev_st
        return (i, xb, ev_o, ev_st)

    def wout_stage2b1(st):
        (i, b, ev_o, ev_st) = st
        col2 = ss_col[0] % 48
        ss_col[0] += 1
        ev_sq = sq_accum(x1b[b][:, :], col2, [ev_o], junks2, junk_st2)
        ev_rs = rstd_chain(ev_sq, col2)
        return (i, b, ev_st, col2, ev_rs)

    def wout_stage2b2(st):
        (i, b, ev_st, col2, ev_rs) = st
        b2 = i % 2
        ev_xs = act.op(lambda e: e.activation(out=xs2[b2][:, :], in_=x1b[b][:, :], func=AF.Copy, scale=rsA[:, col2:col2 + 1]), waits=[ev_rs, xs2_free[b2]])
        x1b_free[b] = [ev_xs, ev_st]
        k_, bank, fr = tr_ring2.get()
        psb = ps[:, bank, :].bitcast(BF16).rearrange("p (k t) -> p k t", k=8)
        pe.wait(ev_xs, fr, bank_free[bank])
        ev_tr = None
        for k in range(8):
            ev_tr = pe.op(lambda e, k=k: e.transpose(out=psb[:, k, :], in_=xs2[b2][:, k * 128:(k + 1) * 128], identity=ident[:, :]), signal=(k == 7))
        xs2_free[b2] = ev_tr
        ev_ev = dve.op(lambda e: e.tensor_tensor(out=hT[:, :, i * 128:(i + 1) * 128], in0=psb, in1=gffn[:, :].unsqueeze(2).to_broadcast([128, 8, 128]), op=ALU.mult),
                       waits=[ev_tr])
        tr_ring2.release(k_, ev_ev)
        bank_free[bank] = ev_ev

    wgu_boot = [P.sb("wgub%d" % i, [128, 2, 8, 128], BF16, R_Q + i * 4 * KB) for i in range(2)]
    d_gub = [P.dsem("d_gub%d" % i) for i in range(2)]
    boot_ev = [pool.dma(wgu_boot[i][:, :, :, :], wgu_d[i], d_gub[i]) for i in range(2)]
    d_wd = P.dsem("d_wd")
    ev_gf = sp.dma(gfin[:, :], gfin_d[:, :], P.dsem("d_gfin"))
    st1, st2, st3, st4 = {}, {}, {}, {}
    for it in range(NT + 4):
        if it < NT:
            st1[it] = wout_stage1(it)
        if 0 <= it - 3 < NT:
            st3[it - 3] = wout_stage2add(st2[it - 3])
        if 0 <= it - 4 < NT:
            st4[it - 4] = wout_stage2b1(st3[it - 4])
        if 0 <= it - 1 < NT:
            st2[it - 1] = wout_stage2a(st1[it - 1])
        if 0 <= it - 4 < NT:
            wout_stage2b2(st4[it - 4])
    P.barrier()

    tap("h2T", hT[:, :, :])
    if stop_after == "wout":
        return finish()
    ffT = P.sb("ffT", [128, NJ, 1024], BF16, R_K)
    wgu = [P.sb("wgu%d" % i, [128, 2, 8, 128], BF16, R_W + i * 4 * KB) for i in range(3)]
    sa = [P.sb("sa%d" % i, [128, 512], F32, R_W + 12 * KB + i * 2 * KB) for i in range(2)]
    d_gu = [P.dsem("d_gu%d" % i) for i in range(3)]
    gu_free = [None, None, None]
    sa_free = [None, None]
    d_x1i = [P.dsem("d_x1i%d" % i) for i in range(2)]
    d_out = [P.dsem("d_out%d" % i) for i in range(2)]
    obuf = x1b
    obuf_free = [x1b_free[0], x1b_free[1]]
    tb_free = [pool.last(), pool.last(), pool.last()]
    ab_ring = Ring([(0, 1), (2, 3)])
    pair_ring2 = Ring([(4, 5), (6, 7)])
    out_evs = []
    fu = 0
    gl = 0
    loads = [(hf, j) for hf in range(2) for j in range(NJ)]
    ld_ev = {}

    def issue_gu(idx):
        hf, j = loads[idx]
        b = idx % 3
        ld_ev[idx] = pool.dma(wgu[b][:, :, :, :], wgu_d[j], d_gu[b], waits=[gu_free[b]])

    ld_ev[0], ld_ev[1] = boot_ev

    def wgu_buf(idx):
        return wgu_boot[idx] if idx < 2 else wgu[idx % 3]
    for hf in range(2):
        for j in range(NJ):
            idx = hf * NJ + j
            if idx + 2 < len(loads):
                issue_gu(idx + 2)
            if hf == 0:
                ev_wd = pool.dma(wdn[:, j, :], wdn_d[:, j, :], d_wd)
            wb = idx % 3
            wbuf = wgu_buf(idx)
            for b in range(2):
                ub = fu % 2
                fu += 1
                tok = slice(hf * 1024 + b * 512, hf * 1024 + (b + 1) * 512)
                ka, (ba, bb), fra = ab_ring.get()
                pe.wait(fra, bank_free[ba], bank_free[bb], ld_ev[idx])
                for k in range(8):
                    e1 = pe.op(lambda e, k=k, wbuf=wbuf, ba=ba, tok=tok: e.matmul(ps[:, ba, :], lhsT=wbuf[:, 0, k, :], rhs=hT[:, k, tok], start=(k == 0), stop=(k == 7)),
                               signal=(k == 7))
                for k in range(8):
                    e2 = pe.op(lambda e, k=k, wbuf=wbuf, bb=bb, tok=tok: e.matmul(ps[:, bb, :], lhsT=wbuf[:, 1, k, :], rhs=hT[:, k, tok], start=(k == 0), stop=(k == 7)),
                               signal=(k == 7))
                if b == 1:
                    gu_free[wb] = e2
                es = act.op(lambda e, ub=ub, ba=ba: e.activation(out=sa[ub][:, :], in_=ps[:, ba, :], func=AF.Silu), waits=[e1, sa_free[ub]])
                ef = dve.op(lambda e, ub=ub, bb=bb, j=j, b=b: e.tensor_tensor(out=ffT[:, j, b * 512:(b + 1) * 512], in0=ps[:, bb, :], in1=sa[ub][:, :], op=ALU.mult),
                            waits=[es, e2])
                sa_free[ub] = ef
                ab_ring.release(ka, ef)
                bank_free[ba] = ef
                bank_free[bb] = ef
        ev_ff = dve.last()
        for il in range(8):
            i = hf * 8 + il
            b = i % 2
            ev_x1 = sp.dma(xin[b][:, :], x1s_d[i * 128:(i + 1) * 128, :], d_x1i[b], waits=[xin_free[b], x1_store_ev[i]])
            kp, pr_, frp = pair_ring2.get()
            pe.wait(frp, bank_free[pr_[0]], bank_free[pr_[1]], ev_wd, ev_ff)
            ev = None
            for h2 in range(2):
                for j in range(NJ):
                    ev = pe.op(lambda e, j=j, h2=h2, pr_=pr_, il=il: e.matmul(ps[:, pr_[0] + h2, :], lhsT=ffT[:, j, il * 128:(il + 1) * 128],
                                                                               rhs=wdn[:, j, h2 * 512:(h2 + 1) * 512], start=(j == 0), stop=(j == NJ - 1)),
                               signal=(j == NJ - 1 and h2 == 1))
            col = ss_col[0] % 48
            ss_col[0] += 1
            ev_t, ev_o = norm_res(pr_, ev, col, gfin, ev_gf, xin[b], ev_x1, obuf[b], obuf_free[b], b)
            pair_ring2.release(kp, ev_t)
            bank_free[pr_[0]] = ev_t
            bank_free[pr_[1]] = ev_t
            xin_free[b] = ev_o
            ev_out = sp.dma(out_d[i * 128:(i + 1) * 128, :], obuf[b][:, :], d_out[b], waits=[ev_o])
            obuf_free[b] = ev_out
            out_evs.append(ev_out)
        dve.wait(pe.last())
    sp.wait(out_evs[-1], out_evs[-2])
    for e_ in P.cengs:
        e_.wait(out_evs[-1], out_evs[-2])
    P.run()
    pscm.__exit__(None, None, None)
    return nc


def _rope_tables():
    half = 32
    inv_freq = 10000.0 ** (-(np.arange(half, dtype=np.float64) * 2.0 / 64.0))
    pos = np.arange(-HT, S, dtype=np.float64)
    ang = pos[:, None] * inv_freq[None, :]
    cos = np.cos(ang).astype(np.float32)
    sin = np.sin(ang).astype(np.float32)
    d = np.arange(128) % 64
    i = d % 32
    sgn = np.where(d < 32, -1.0, 1.0).astype(np.float32)
    C = np.ascontiguousarray(cos[:, i].T)
    Sn = np.ascontiguousarray((sin[:, i] * sgn[None, :]).T)
    return C, Sn


_PROG_CACHE = {}


def _host_inputs(x, g_pre_mix, w_in, w_pool_mix, pool_scale, w_proj_attn, w_proj_pool, w_out,
                 g_post_mix, g_pre_ffn, w_gate_up, w_down, g_post_ffn):
    f = np.float32
    x = np.asarray(x, f).reshape(S, D)
    w_in = np.asarray(w_in, f)[0]
    def chunks(wcols, n):
        nc_ = wcols.shape[1] // n
        return np.ascontiguousarray(wcols.reshape(8, 128, nc_, n).transpose(2, 1, 0, 3))
    wq = chunks(w_in[:, 0:768], 128)
    wk = chunks(w_in[:, 768:1536], 128)
    wv = chunks(w_in[:, 1536:2304], 256)
    wu = chunks(w_in[:, 2304:2560], 128)
    wga = chunks(w_in[:, 2560:3584], 128)
    wgp = chunks(w_in[:, 3584:4608], 128)
    wg = np.ascontiguousarray(np.stack([wga, wgp], axis=2))
    wmix = np.ascontiguousarray(np.asarray(w_pool_mix, f)[0])
    wpa = np.ascontiguousarray(np.asarray(w_proj_attn, f)[0].reshape(2, 128, D).transpose(1, 0, 2))
    sel = np.zeros((64, 2, 128), f)
    sel[np.arange(64), 0, np.arange(64)] = 1.0
    sel[np.arange(64), 1, np.arange(64) + 64] = 1.0
    wpp = np.ascontiguousarray(np.asarray(w_proj_pool, f)[0].reshape(2, 128, D).transpose(1, 0, 2))
    wout = np.ascontiguousarray(np.asarray(w_out, f)[0].reshape(8, 128, D).transpose(1, 0, 2))
    wgu_full = np.asarray(w_gate_up, f)[0]
    wa = chunks(wgu_full[:, 0:FF], 128)
    wb = chunks(wgu_full[:, FF:2 * FF], 128)
    wgu = np.ascontiguousarray(np.stack([wa, wb], axis=2))
    wdn = np.ascontiguousarray(np.asarray(w_down, f)[0].reshape(NJ, 128, D).transpose(1, 0, 2))
    def fm(g):
        return np.ascontiguousarray(np.asarray(g, f).reshape(8, 128).T)
    gpre = fm(g_pre_mix[0])
    gffn = fm(g_pre_ffn[0])
    gpost = np.ascontiguousarray(np.broadcast_to(np.asarray(g_post_mix, f)[0][None, :], (128, D)))
    gfin = np.ascontiguousarray(np.broadcast_to(np.asarray(g_post_ffn, f)[0][None, :], (128, D)))
    pscale = np.ascontiguousarray(np.asarray(pool_scale, f)[0].reshape(2, 128).T)
    wins = np.array([2, 4, 8, 16], dtype=f)
    wpart = wins[(np.arange(256) // 64)]
    invw = np.ascontiguousarray((1.0 / wpart).astype(f).reshape(2, 128).T)
    C, Sn = _rope_tables()
    ident = np.eye(128, dtype=f)
    rotm = np.zeros((128, 128), f)
    for dd in range(128):
        base = (dd // 64) * 64
        rotm[base + ((dd % 64) + 32) % 64, dd] = 1.0
    kk = np.arange(128)[:, None]
    qq = np.arange(128)[None, :]
    mC = (kk <= qq).astype(f)
    mP = (kk >= qq).astype(f)
    common = dict(wq=wq, wk=wk, wv=wv, wu=wu, wg=wg, wmix=wmix, wpa=wpa, sel=sel, wpp=wpp, wout=wout, wgu=wgu, wdn=wdn,
                  gpre=gpre, gffn=gffn, gpost=gpost, gfin=gfin, pscale=pscale, invw=invw, ident=ident, rotm=rotm)
    in_maps = []
    for c in range(NCORES):
        mH = mP if c > 0 else np.zeros_like(mP)
        masks = np.zeros((128, 4, 512), f)
        masks[:, 0] = np.tile(mC, (1, 4))
        masks[:, 1] = np.tile(mP, (1, 4))
        masks[:, 2] = np.tile(mH, (1, 4))
        masks[:, 3] = np.concatenate([mH, mP, mP, mP], axis=1)
        xo = x[c * T:(c + 1) * T]
        xh_ = x[c * T - HT:c * T] if c > 0 else np.zeros((HT, D), f)
        tpos = np.arange(16, dtype=f) + c * T
        cnt = np.minimum(tpos[None, :] + 1.0, wpart[:, None])
        invc = np.ascontiguousarray((1.0 / cnt).astype(f).reshape(2, 128, 16).transpose(1, 0, 2))
        m = dict(common)
        m.update(xh=np.ascontiguousarray(xh_), xo=np.ascontiguousarray(xo), masks=masks, invc=invc,
                 cost=np.ascontiguousarray(C[:, c * T:c * T + HT + T]), sint=np.ascontiguousarray(Sn[:, c * T:c * T + HT + T]))
        in_maps.append(m)
    return in_maps


def kernel(x, g_pre_mix, w_in, w_pool_mix, pool_scale, w_proj_attn, w_proj_pool, w_out,
           g_post_mix, g_pre_ffn, w_gate_up, w_down, g_post_ffn):
    in_maps = _host_inputs(x, g_pre_mix, w_in, w_pool_mix, pool_scale, w_proj_attn, w_proj_pool, w_out,
                           g_post_mix, g_pre_ffn, w_gate_up, w_down, g_post_ffn)
    nc = build_program()
    res = run_bass_kernel_spmd(nc, in_maps, core_ids=list(range(NCORES)))
    out = np.concatenate([np.asarray(r["out"], np.float32) for r in res.results], axis=0)
    return out.reshape(1, S, D)
```

```python
import numpy as np
import concourse.bass as bass
import concourse.mybir as mybir
from concourse.bass_utils import run_bass_kernel_spmd

F32 = mybir.dt.float32
BF16 = mybir.dt.bfloat16
AF = mybir.ActivationFunctionType
ALU = mybir.AluOpType

NCORES = 8
S = 16384
D = 1024
T = S // NCORES
NT = T // 128
HT = 2048
FF = 2816
NJ = FF // 128
EPS = 1e-6
KB = 1024

DEBUG_TAPS = False


class Ev:
    __slots__ = ("sem", "val")

    def __init__(self, sem, val):
        self.sem = sem
        self.val = val


class Eng:
    def __init__(self, name):
        self.name = name
        self.ops = []
        self.meta = []
        self.sem = None
        self.count = 0
        self.seen = {}

    def wait(self, *evs):
        for ev in evs:
            if ev is None:
                continue
            if isinstance(ev, (list, tuple)):
                self.wait(*ev)
                continue
            key = id(ev.sem)
            if self.seen.get(key, 0) >= ev.val:
                continue
            self.seen[key] = ev.val
            sem, val = ev.sem, ev.val
            self.meta.append(("wait", id(sem), val))
            self.ops.append(lambda e, sem=sem, val=val: e.wait_ge(sem, val))

    def op(self, fn, waits=(), signal=True):
        self.wait(*waits)
        if signal:
            self.count += 1
            sem, val = self.sem, self.count
            self.meta.append(("inc", id(sem), 1))
            self.ops.append(lambda e, fn=fn, sem=sem: fn(e).then_inc(sem, 1))
            return Ev(sem, val)
        self.ops.append(lambda e, fn=fn: fn(e))
        return None

    def last(self):
        return Ev(self.sem, self.count) if self.count else None

    def dma(self, out, in_, dsem, waits=(), **kw):
        self.wait(*waits)
        dsem.count += 16
        sem = dsem.sem
        self.meta.append(("inc", id(sem), 16))
        self.ops.append(lambda e, out=out, in_=in_, sem=sem, kw=kw: e.dma_start(out=out, in_=in_, **kw).then_inc(sem, 16))
        return Ev(sem, dsem.count)


class DSem:
    def __init__(self, sem):
        self.sem = sem
        self.count = 0


class Prog:
    def __init__(self, nc):
        self.nc = nc
        self.pe = Eng("tensor")
        self.act = Eng("scalar")
        self.dve = Eng("vector")
        self.pool = Eng("gpsimd")
        self.sp = Eng("sync")
        self.cengs = [self.pe, self.act, self.dve, self.pool]
        self.engs = self.cengs + [self.sp]
        self._ctx = []
        for e in self.engs:
            e.sem = self.new_sem("p_" + e.name)
        self.nsb = 0

    def new_sem(self, name):
        cm = self.nc.semaphore(name)
        h = cm.__enter__()
        self._ctx.append(cm)
        return h

    def dsem(self, name):
        return DSem(self.new_sem(name))

    def sb(self, name, shape, dtype, off):
        esz = 2 if dtype == BF16 else 4
        n = 1
        for s_ in shape[1:]:
            n *= s_
        assert off % 32 == 0, (name, off)
        assert off + n * esz <= ARENA_BYTES, (name, off, n * esz)
        self.nsb += 1
        return self.nc.alloc_sbuf_tensor_at("%s_%d" % (name, self.nsb), list(shape), dtype, offset=ARENA_BASE + off)

    def barrier(self):
        evs = [e.last() for e in self.cengs]
        for e in self.engs:
            e.wait(*evs)

    def check_deadlock(self):
        semv = {}
        pos = {e.name: 0 for e in self.engs}
        progress = True
        while progress:
            progress = False
            for e in self.engs:
                while pos[e.name] < len(e.meta):
                    kind, sid, val = e.meta[pos[e.name]]
                    if kind == "wait":
                        if semv.get(sid, 0) >= val:
                            pos[e.name] += 1
                            progress = True
                        else:
                            break
                    else:
                        semv[sid] = semv.get(sid, 0) + val
                        pos[e.name] += 1
                        progress = True
        stuck = {e.name: (pos[e.name], len(e.meta), e.meta[pos[e.name]]) for e in self.engs if pos[e.name] < len(e.meta)}
        if stuck:
            raise RuntimeError("static deadlock: %r" % (stuck,))

    def run(self):
        self.check_deadlock()
        nc = self.nc
        with nc.Block() as block:
            @block.tensor
            def _(e):
                for f in self.pe.ops:
                    f(e)

            @block.scalar
            def _(e):
                for f in self.act.ops:
                    f(e)

            @block.vector
            def _(e):
                for f in self.dve.ops:
                    f(e)

            @block.gpsimd
            def _(e):
                for f in self.pool.ops:
                    f(e)

            @block.sync
            def _(e):
                for f in self.sp.ops:
                    f(e)
        for cm in reversed(self._ctx):
            cm.__exit__(None, None, None)


ARENA_BASE = 18432
ARENA_BYTES = 229376 - ARENA_BASE


class Ring:
    def __init__(self, items):
        self.items = list(items)
        self.free = [None] * len(self.items)
        self.i = 0

    def get(self):
        k = self.i % len(self.items)
        self.i += 1
        return k, self.items[k], self.free[k]

    def release(self, k, ev):
        self.free[k] = ev


def build_program(taps=(), stop_after=None):
    nc = bass.Bass("TRN2", target_bir_lowering=False)

    def din(name, shape, dt=F32):
        return nc.dram_tensor(name, list(shape), dt, kind="ExternalInput").ap()

    xh = din("xh", [HT, D])
    xo = din("xo", [T, D])
    wq_d = din("wq", [6, 128, 8, 128])
    wk_d = din("wk", [6, 128, 8, 128])
    wv_d = din("wv", [3, 128, 8, 256])
    wu_d = din("wu", [2, 128, 8, 128])
    wg_d = din("wg", [8, 128, 2, 8, 128])
    wmix_d = din("wmix", [4, 64, 64])
    wpa_d = din("wpa", [128, 2, D])
    sel_d = din("sel", [64, 2, 128])
    wpp_d = din("wpp", [128, 2, D])
    wout_d = din("wout", [128, 8, D])
    wgu_d = din("wgu", [NJ, 128, 2, 8, 128])
    wdn_d = din("wdn", [128, NJ, D])
    gpre_d = din("gpre", [128, 8])
    gffn_d = din("gffn", [128, 8])
    gpost_d = din("gpost", [128, D])
    gfin_d = din("gfin", [128, D])
    pscale_d = din("pscale", [128, 2])
    invw_d = din("invw", [128, 2])
    invc_d = din("invc", [128, 2, 16])
    cos_d = din("cost", [128, HT + T])
    sin_d = din("sint", [128, HT + T])
    ident_d = din("ident", [128, 128])
    rotm_d = din("rotm", [128, 128])
    masks_d = din("masks", [128, 4, 512])
    out_d = nc.dram_tensor("out", [T, D], F32, kind="ExternalOutput").ap()
    x1s_d = nc.dram_tensor("x1s", [T, D], F32, kind="Internal").ap()
    tap_d = {}
    for name, shape, dt in taps:
        tap_d[name] = nc.dram_tensor("tap_" + name, list(shape), dt, kind="ExternalOutput").ap()

    P = Prog(nc)
    pe, act, dve, pool, sp = P.pe, P.act, P.dve, P.pool, P.sp
    d_tap = P.dsem("d_tap")

    def tap(name, ap):
        if name not in tap_d:
            return
        P.barrier()
        nd = len(tap_d[name].shape)
        idx = tuple(slice(None) for _ in range(nd))
        evt = sp.dma(tap_d[name][idx], ap, d_tap)
        for e_ in P.engs:
            e_.wait(evt)

    def finish():
        P.barrier()
        P.run()
        pscm.__exit__(None, None, None)
        return nc
    pscm = nc.psum_tensor("ps", [128, 8, 512], F32)
    ps = pscm.__enter__()
    bank_free = [None] * 8

    o = 0

    def take(nbytes):
        nonlocal o
        r = o
        o += (nbytes + 63) // 64 * 64
        return r

    ident = P.sb("ident", [128, 128], BF16, take(256))
    rotm = P.sb("rotm", [128, 128], BF16, take(256))
    masks = P.sb("masks", [128, 4, 512], BF16, take(4096))
    gpre = P.sb("gpre", [128, 8], F32, take(32))
    gffn = P.sb("gffn", [128, 8], F32, take(32))
    pscale = P.sb("pscale", [128, 2], F32, take(8))
    invw = P.sb("invw", [128, 2], F32, take(8))
    invc = P.sb("invc", [128, 2, 16], F32, take(128))
    uh = P.sb("uh", [128, 2, 16], F32, take(128))
    ssA = P.sb("ssA", [128, 48], F32, take(192))
    msA = P.sb("msA", [128, 48], F32, take(192))
    rsA = P.sb("rsA", [128, 48], F32, take(192))
    nhalf = P.sb("nhalf", [128, 1], F32, take(64))
    onesf = P.sb("onesf", [128, 64], F32, take(256))
    wmixb = P.sb("wmixb", [128, 2, 128], BF16, take(512))
    assert o <= 6 * KB + 512, o
    o = 7 * KB
    R_HT = take(32 * KB)
    R_K = take(35328)
    R_Q = take(24 * KB)
    R_V = take(69 * 4 * 66 * 2)
    R_W = take(24 * KB)
    R_WS = take(8 * KB)
    R_ST = take(17 * KB)
    R_ZT = take(8 * KB)
    R_YP = take(8 * KB)
    R_PT = take(4 * KB)
    assert o <= ARENA_BYTES, o

    hT = P.sb("hT", [128, 8, 2048], BF16, R_HT)
    kT = [P.sb("kT0", [128, 2, 2176], BF16, R_K),
          P.sb("kT1", [128, 2, 2560], BF16, R_K + 8704),
          P.sb("kT2", [128, 2, 4096], BF16, R_K + 8704 + 10240)]
    KOFF = [1920, 1536, 0]
    qT = P.sb("qT", [128, 6, 2048], BF16, R_Q)
    vA = P.sb("vA", [128, 69, 4, 66], BF16, R_V)

    d_c = P.dsem("d_const")
    act.dma(gpre[:, :], gpre_d[:, :], d_c)
    act.dma(gffn[:, :], gffn_d[:, :], d_c)
    act.dma(pscale[:, :], pscale_d[:, :], d_c)
    act.dma(invw[:, :], invw_d[:, :], d_c)
    ev_const = act.dma(invc[:, :, :], invc_d[:, :, :], d_c)
    ev_ms1 = dve.op(lambda e: e.memset(nhalf[:, :], -0.5))
    ev_ms2 = dve.op(lambda e: e.memset(onesf[:, :], 1.0))
    ev_ms3 = dve.op(lambda e: e.memset(vA[:, 48:69, :, 64:66], 1.0))
    ev_ms4 = dve.op(lambda e: e.memset(wmixb[:, :, :], 0.0))
    pool.wait(ev_ms1)
    dve.wait(ev_const)

    wqk = P.sb("wqk", [128, 12, 8, 128], BF16, R_W)
    wvr = P.sb("wvr", [128, 3, 8, 256], BF16, R_YP)
    wu_t = [P.sb("wu%d" % i, [128, 8, 128], BF16, R_WS + i * 2 * KB) for i in range(2)]
    pq = []
    pq_sem = {}
    pq_ev = {}
    pq_left = {}

    def pq_add(group, out, in_, waits=()):
        if group not in pq_sem:
            pq_sem[group] = P.dsem("d_pq_" + group)
            pq_left[group] = 0
        pq_left[group] += 1
        pq.append((group, out, in_, waits))

    def pq_issue(n=1):
        for _ in range(n):
            if not pq:
                return
            group, out, in_, waits = pq.pop(0)
            pq_ev[group] = pool.dma(out, in_, pq_sem[group], waits=list(waits))
            pq_left[group] -= 1

    def pq_need(group):
        while pq_left[group] > 0:
            pq_issue(1)
        return pq_ev[group]

    pq_add("ident", ident[:, :], ident_d[:, :])
    pq_add("rotm", rotm[:, :], rotm_d[:, :])
    for c in (10, 11):
        pq_add("wk2", wqk[:, c, :, :], wk_d[c - 6])
    for g in range(3):
        pq_add("wv", wvr[:, g, :, :], wv_d[g])
    for c in (8, 9, 6, 7):
        pq_add("wk%d" % ((c - 6) // 2), wqk[:, c, :, :], wk_d[c - 6])
    for c in range(2):
        pq_add("wu", wu_t[c][:, :, :], wu_d[c])
    for c in range(6):
        pq_add("wq", wqk[:, c, :, :], wq_d[c])
    pq_add("masks", masks[:, :, :], masks_d[:, :, :])
    for g in range(4):
        c, hf = g // 2, g % 2
        pq_add("wmix", wmixb[64 * hf:64 * hf + 64, c, 64 * hf:64 * hf + 64], wmix_d[g, :, :], waits=[ev_ms4])
    pe.wait(pq_need("ident"))

    ss_col = [0]

    rstd_mode = {"act": False}

    def rstd_chain(ev_ss, col):
        if rstd_mode["act"]:
            e1 = act.op(lambda e: e.activation(out=msA[:, col:col + 1], in_=ssA[:, col:col + 1], func=AF.Sqrt, scale=1.0 / D, bias=EPS), waits=[ev_ss])
            return dve.op(lambda e: e.reciprocal(out=rsA[:, col:col + 1], in_=msA[:, col:col + 1]), waits=[e1])
        e1 = pool.op(lambda e: e.tensor_scalar(out=msA[:, col:col + 1], in0=ssA[:, col:col + 1], scalar1=1.0 / D, scalar2=EPS,
                                               op0=ALU.mult, op1=ALU.add), waits=[ev_ss])
        e2 = pool.op(lambda e: e.tensor_tensor(out=rsA[:, col:col + 1], in0=msA[:, col:col + 1], in1=nhalf[:, :], op=ALU.pow), waits=[e1])
        return e2

    NXB = 4
    NXB_MAX = 10
    xst = [P.sb("xst%d" % i, [128, D], F32, (R_ST + i * 4 * KB) if i < 2 else (R_ZT + (i - 2) * 4 * KB)) for i in range(NXB)]
    xst += [P.sb("xst%d" % (4 + i), [128, D], F32, R_HT + i * 4 * KB) for i in range(NXB_MAX - NXB)]
    xsb = [P.sb("xsb%d" % i, [128, D], BF16, R_ST + 8 * KB + i * 2 * KB) for i in range(2)]
    junks = [P.sb("junk%d" % i, [128, D], BF16, R_ST + 12 * KB + i * 2 * KB) for i in range(2)]
    junk_st = {"i": 0, "ev": [None, None]}

    def sq_accum(in_ap, col, waits, jl=None, st=None):
        jl = junks if jl is None else jl
        st = junk_st if st is None else st
        k = st["i"] % 2
        st["i"] += 1
        ev = act.op(lambda e: e.activation(out=jl[k][:, :], in_=in_ap, func=AF.Square, accum_out=ssA[:, col:col + 1]),
                    waits=list(waits) + [st["ev"][k]])
        st["ev"][k] = ev
        return ev
    d_x = [P.dsem("d_x%d" % i) for i in range(NXB_MAX)]
    xst_free = [None] * NXB_MAX
    pa_nxb = [NXB]
    xsb_free = [None, None]
    tr_ring = Ring([0, 1])
    pa_state = {"q": [], "n": 0, "lag": 1}

    def pa_stage2(pd):
        (xin, xb, col, ev_rs, gvec, dst, dst_col, b2, n) = pd
        if n % 2 == 0 or pa_state.get("all_act"):
            ev_xs = act.op(lambda e: e.activation(out=xsb[b2][:, :], in_=xin[:, :], func=AF.Copy, scale=rsA[:, col:col + 1]),
                           waits=[ev_rs, xsb_free[b2]])
        else:
            ev_xs = dve.op(lambda e: e.tensor_scalar(out=xsb[b2][:, :], in0=xin[:, :], scalar1=rsA[:, col:col + 1], scalar2=None, op0=ALU.mult),
                           waits=[ev_rs, xsb_free[b2]])
        xst_free[xb] = ev_xs
        k_, bank, fr = tr_ring.get()
        psb = ps[:, bank, :].bitcast(BF16).rearrange("p (k t) -> p k t", k=8)
        pe.wait(ev_xs, fr, bank_free[bank])
        ev_tr = None
        for k in range(8):
            ev_tr = pe.op(lambda e, k=k: e.transpose(out=psb[:, k, :], in_=xsb[b2][:, k * 128:(k + 1) * 128], identity=ident[:, :]),
                          signal=(k == 7))
        xsb_free[b2] = ev_tr
        ev_ev = dve.op(lambda e: e.tensor_tensor(out=dst[:, :, dst_col:dst_col + 128], in0=psb,
                                                 in1=gvec[:, :].unsqueeze(2).to_broadcast([128, 8, 128]), op=ALU.mult), waits=[ev_tr])
        tr_ring.release(k_, ev_ev)
        bank_free[bank] = ev_ev
        return ev_ev

    def pa_issue_load(src_rows, idx):
        xb = idx % pa_nxb[0]
        return sp.dma(xst[xb][:, :], src_rows, d_x[xb], waits=[xst_free[xb]])

    def norm_transpose_tile(ev_ld, gvec, dst, dst_col, idx):
        xb = idx % pa_nxb[0]
        xin = xst[xb]
        col = ss_col[0] % 48
        ss_col[0] += 1
        ev_sq = sq_accum(xin[:, :], col, [ev_ld])
        ev_rs = rstd_chain(ev_sq, col)
        n = pa_state["n"]
        pa_state["n"] += 1
        pa_state["q"].append((xin, xb, col, ev_rs, gvec, dst, dst_col, n % 2, n))
        while len(pa_state["q"]) > pa_state["lag"]:
            pa_stage2(pa_state["q"].pop(0))

    def pa_flush():
        while pa_state["q"]:
            pa_stage2(pa_state["q"].pop(0))

    def phase_a(src, ntiles, gvec, dst, between=None, pq_from=0, nxb=NXB, lag=1):
        pa_nxb[0] = nxb
        pa_state["lag"] = lag
        NXB = nxb
        lds = {}
        for i in range(min(NXB - lag, ntiles)):
            lds[i] = pa_issue_load(src[i * 128:(i + 1) * 128, :], i)
        for i in range(ntiles):
            if i >= pq_from:
                pq_issue(2 if len(pq) > 12 else 1)
            norm_transpose_tile(lds[i], gvec, dst, i * 128, i)
            j = i + NXB - lag
            if j < ntiles:
                lds[j] = pa_issue_load(src[j * 128:(j + 1) * 128, :], j)
            if between is not None:
                between(i)
        pa_flush()

    pj_ring = Ring([2, 3, 4, 5, 6, 7])

    def proj_fm(wfn, rhsfn, n, extra_waits=()):
        k_, bank, fr = pj_ring.get()
        pe.wait(fr, bank_free[bank], *extra_waits)
        ev = None
        for k in range(8):
            ev = pe.op(lambda e, k=k: e.matmul(ps[:, bank, 0:n], lhsT=wfn(k), rhs=rhsfn(k), start=(k == 0), stop=(k == 7)),
                       signal=(k == 7))
        return k_, bank, ev

    RS = R_V + 8 * KB
    cs = [[P.sb("cos%d" % i, [128, 512], F32, RS + 0 * KB + i * 2 * KB), P.sb("sin%d" % i, [128, 512], F32, RS + 4 * KB + i * 2 * KB)]
          for i in range(2)]
    qb_t = [P.sb("qb%d" % i, [128, 512], BF16, RS + 8 * KB + i * KB) for i in range(2)]
    t1_t = [P.sb("t1_%d" % i, [128, 512], F32, RS + 10 * KB + i * 2 * KB) for i in range(2)]
    assert RS + 14 * KB <= R_V + 25344
    t2_t = [P.sb("t2b_%d" % i, [128, 512], F32, R_WS + 4 * KB + i * 2 * KB) for i in range(2)]
    d_cs = [P.dsem("d_cs%d" % i) for i in range(2)]
    cs_free = [None, None]
    cs_i = [0]

    def load_tables(col0, n=512):
        b = cs_i[0] % 2
        cs_i[0] += 1
        sp.dma(cs[b][0][:, 0:n], cos_d[:, col0:col0 + n], d_cs[b], waits=[cs_free[b]])
        ev = sp.dma(cs[b][1][:, 0:n], sin_d[:, col0:col0 + n], d_cs[b])
        return b, ev

    def q_dst(c, b):
        g = c // 2
        if g == 0:
            return qT[:, c, b * 512:(b + 1) * 512]
        if g == 1:
            return (qT[:, c, :].rearrange("p (r L) -> p r L", r=4)[:, :, b * 128:(b + 1) * 128], 4)
        return (qT[:, c, :].rearrange("p (r L) -> p r L", r=16)[:, :, b * 32:(b + 1) * 32], 16)

    def k_dst(g, pr, e0, n):
        if g == 0:
            return kT[0][:, pr, e0 - KOFF[0]:e0 - KOFF[0] + n]
        if g == 1:
            j0 = (e0 - KOFF[1]) // 4
            return (kT[1][:, pr, :].rearrange("p (r L) -> p r L", r=4)[:, :, j0:j0 + n // 4], 4)
        a0 = e0 // 16
        return (kT[2][:, pr, :].rearrange("p (r L) -> p r L", r=16)[:, :, a0:a0 + n // 16], 16)

    rope_state = {"pending": None, "u": 0, "qb_free": [None, None], "t1_free": [None, None], "t2_free": [None, None]}

    def rope_finish(pd):
        (ub, n, bankA, kA, evA, ev_qb, csb, ev_cs, dst, users) = pd
        kB, bankB, frB = pj_ring.get()
        ev_rot = pe.op(lambda e: e.matmul(ps[:, bankB, 0:n], lhsT=rotm[:, :], rhs=qb_t[ub][:, 0:n], start=True, stop=True),
                       waits=[ev_qb, frB, bank_free[bankB]])
        rope_state["qb_free"][ub] = ev_rot
        ev_t1 = dve.op(lambda e: e.tensor_tensor(out=t1_t[ub][:, 0:n], in0=ps[:, bankA, 0:n], in1=cs[csb][0][:, 0:n], op=ALU.mult),
                       waits=[evA, ev_cs, rope_state["t1_free"][ub]])
        ev_t2 = dve.op(lambda e: e.tensor_tensor(out=t2_t[ub][:, 0:n], in0=ps[:, bankB, 0:n], in1=cs[csb][1][:, 0:n], op=ALU.mult),
                       waits=[ev_rot, rope_state["t2_free"][ub]])
        pj_ring.release(kA, ev_t1)
        bank_free[bankA] = ev_t1
        pj_ring.release(kB, ev_t2)
        bank_free[bankB] = ev_t2
        if isinstance(dst, tuple):
            dst_ap, rr = dst
            i0 = t1_t[ub][:, 0:n].rearrange("p (j r) -> p r j", r=rr)
            i1 = t2_t[ub][:, 0:n].rearrange("p (j r) -> p r j", r=rr)
            ev_o = pool.op(lambda e: e.tensor_tensor(out=dst_ap, in0=i0, in1=i1, op=ALU.add), waits=[ev_t1, ev_t2])
        else:
            ev_o = pool.op(lambda e: e.tensor_tensor(out=dst, in0=t1_t[ub][:, 0:n], in1=t2_t[ub][:, 0:n], op=ALU.add), waits=[ev_t1, ev_t2])
        rope_state["t1_free"][ub] = ev_o
        rope_state["t2_free"][ub] = ev_o
        users.append(ev_t2)
        return ev_o

    def rope_unit(wfn, rhsfn, n, csb, ev_cs, dst, users, extra_waits=()):
        u = rope_state["u"]
        rope_state["u"] += 1
        ub = u % 2
        kA, bankA, evA = proj_fm(wfn, rhsfn, n, extra_waits)
        if rope_state.get("qb_dve"):
            ev_qb = dve.op(lambda e: e.tensor_copy(out=qb_t[ub][:, 0:n], in_=ps[:, bankA, 0:n]), waits=[evA, rope_state["qb_free"][ub]])
        else:
            ev_qb = act.op(lambda e: e.activation(out=qb_t[ub][:, 0:n], in_=ps[:, bankA, 0:n], func=AF.Copy),
                           waits=[evA, rope_state["qb_free"][ub]])
        prev = rope_state["pending"]
        rope_state["pending"] = (ub, n, bankA, kA, [evA, ev_qb], ev_qb, csb, ev_cs, dst, users)
        if prev is not None:
            return rope_finish(prev)
        return None

    def rope_flush():
        prev = rope_state["pending"]
        rope_state["pending"] = None
        if prev is not None:
            return rope_finish(prev)
        return None

    hTh = P.sb("hTh", [128, 8, 2048], BF16, R_Q)
    phase_a(xh, HT // 128, gpre, hTh, pq_from=9, nxb=NXB_MAX, lag=2)
    pe.wait(dve.last())

    vflip = [0]

    def v_block(wv_tile, g, tok_ap_fn, blk, extra_waits=(), evac=None):
        k_, bank, fr = pj_ring.get()
        pe.wait(fr, bank_free[bank], *extra_waits)
        ev = None
        for k in range(8):
            ev = pe.op(lambda e, k=k: e.matmul(ps[:, bank, 0:256], lhsT=tok_ap_fn(k), rhs=wv_tile[:, g, k, :], start=(k == 0), stop=(k == 7)),
                       signal=(k == 7))
        src = ps[:, bank, 0:256].rearrange("p (h d) -> p h d", h=4)
        vflip[0] += 1
        if evac == "act" or (evac is None and vflip[0] % 2):
            ev2 = act.op(lambda e: e.activation(out=vA[:, blk, :, 0:64], in_=src, func=AF.Copy), waits=[ev])
        else:
            ev2 = dve.op(lambda e: e.tensor_copy(out=vA[:, blk, :, 0:64], in_=src), waits=[ev])
        pj_ring.release(k_, ev2)
        bank_free[bank] = ev2
        return ev2

    def blk_idx(g, s_, m):
        if g == 0:
            return 48 if m == 0 else m - 1
        if g == 1:
            return 49 + s_ if m == 0 else 16 + 4 * s_ + (m - 1)
        return 53 + s_ if m == 0 else 32 + s_

    users = []
    items = []
    blk_state = {}

    def it_tables(col0, n=512):
        def f():
            blk_state["cs"] = load_tables(col0, n)
        return f

    def it_rope(wfn, rhsfn, n, dst, ew):
        def f():
            csb, ev_cs = blk_state["cs"]
            rope_unit(wfn, rhsfn, n, csb, ev_cs, dst, users, extra_waits=ew())
        return f

    def it_endblk():
        def f():
            rope_flush()
            cs_free[blk_state["cs"][0]] = users[-1]
        return f

    for b in range(4):
        items.append(it_tables(b * 512))
        for pr in range(2):
            items.append(it_rope(lambda k, pr=pr: wqk[:, 10 + pr, k, :], lambda k, b=b: hTh[:, k, b * 512:(b + 1) * 512], 512,
                                 k_dst(2, pr, b * 512, 512), lambda: [pq_need("wk2"), pq_need("rotm")]))
        if b == 3:
            for pr in range(2):
                items.append(it_rope(lambda k, pr=pr: wqk[:, 8 + pr, k, :], lambda k: hTh[:, k, 1536:2048], 512,
                                     k_dst(1, pr, 1536, 512), lambda: [pq_need("wk1")]))
        items.append(it_endblk())
    items.append(it_tables(1920, 128))
    for pr in range(2):
        items.append(it_rope(lambda k, pr=pr: wqk[:, 6 + pr, k, :], lambda k: hTh[:, k, 1920:2048], 128, kT[0][:, pr, 0:128], lambda: [pq_need("wk0")]))
    items.append(it_endblk())
    items.append(lambda: v_block(wvr, 0, lambda k: hTh[:, k, 1920:2048], blk_idx(0, 0, 0), extra_waits=[pq_need("wv")], evac="dve"))
    for rho in range(4):
        items.append(lambda rho=rho: v_block(wvr, 1, lambda k, rho=rho: hTh[:, k, 1536 + rho:2048:4], blk_idx(1, rho, 0), extra_waits=[pq_need("wv")], evac="dve"))
    for r in range(16):
        items.append(lambda r=r: v_block(wvr, 2, lambda k, r=r: hTh[:, k, r:2048:16], blk_idx(2, r, 0), extra_waits=[pq_need("wv")], evac="dve"))

    def it_u(c):
        def f():
            k_, bank, ev = proj_fm(lambda k, c=c: wu_t[c][:, k, :], lambda k: hTh[:, k, 2032:2048], 16, extra_waits=[pq_need("wu")])
            ev2 = dve.op(lambda e, c=c, bank=bank: e.tensor_copy(out=uh[:, c, :], in_=ps[:, bank, 0:16]), waits=[ev])
            pj_ring.release(k_, ev2)
            bank_free[bank] = ev2
        return f
    for c in range(2):
        items.append(it_u(c))

    def between(i):
        for _ in range(3):
            if items:
                items.pop(0)()

    pa_state["all_act"] = True
    rope_state["qb_dve"] = True
    rstd_mode["act"] = True
    phase_a(xo, NT, gpre, hT, between=between)
    while items:
        items.pop(0)()
    pa_state["all_act"] = False
    rope_state["qb_dve"] = False
    rstd_mode["act"] = False
    P.barrier()

    users = []
    for b in range(4):
        csb, ev_cs = load_tables(HT + b * 512)
        for c in range(12):
            if c < 6:
                dst = q_dst(c, b)
            else:
                dst = k_dst((c - 6) // 2, (c - 6) % 2, HT + b * 512, 512)
            rope_unit(lambda k, c=c: wqk[:, c, k, :], lambda k, b=b: hT[:, k, b * 512:(b + 1) * 512], 512, csb, ev_cs, dst, users,
                      extra_waits=[pq_need("wq")])
        rope_flush()
        cs_free[csb] = users[-1]
    P.barrier()

    ev_ones = dve.op(lambda e: e.memset(vA[:, 0:48, :, 64:66], 1.0))
    X = P.sb("X", [128, 2, 2064], F32, R_W)
    ev_uhc = []
    for c in range(2):
        ev_uhc.append(dve.op(lambda e, c=c: e.tensor_copy(out=X[:, c, 0:16], in_=uh[:, c, :])))
        for b in range(4):
            k_, bank, ev = proj_fm(lambda k, c=c: wu_t[c][:, k, :], lambda k, b=b: hT[:, k, b * 512:(b + 1) * 512], 512)
            ev2 = act.op(lambda e, c=c, b=b, bank=bank: e.activation(out=X[:, c, 16 + b * 512:16 + (b + 1) * 512], in_=ps[:, bank, 0:512], func=AF.Copy),
                         waits=[ev])
            pj_ring.release(k_, ev2)
            bank_free[bank] = ev2
    ev_X = act.last()
    Y = P.sb("Y", [128, 2064], F32, R_ST)
    Z = P.sb("Z", [128, 2064], F32, R_ST + 8256)
    zT = P.sb("zT", [128, 2, 2048], BF16, R_ZT)
    ypool = P.sb("ypool", [128, 2, 2048], BF16, R_YP)
    L = 2064
    pev = [ev_X] + ev_uhc
    engs2 = [dve, pool]
    for c in range(2):
        eng = engs2[c]
        e1 = eng.op(lambda e, c=c: e.tensor_tensor(out=Y[:, 1:L], in0=X[:, c, 1:L], in1=X[:, c, 0:L - 1], op=ALU.add), waits=pev)
        if c == 0:
            e2 = eng.op(lambda e: e.tensor_tensor(out=Z[64:128, 3:L], in0=Y[64:128, 3:L], in1=Y[64:128, 1:L - 2], op=ALU.add), waits=[e1])
            fin = [(0, 64, Y), (64, 128, Z)]
            elast = e2
        else:
            e2 = eng.op(lambda e: e.tensor_tensor(out=Z[:, 3:L], in0=Y[:, 3:L], in1=Y[:, 1:L - 2], op=ALU.add), waits=[e1])
            e3 = eng.op(lambda e: e.tensor_tensor(out=Y[:, 7:L], in0=Z[:, 7:L], in1=Z[:, 3:L - 4], op=ALU.add), waits=[e2])
            e4 = eng.op(lambda e: e.tensor_tensor(out=Z[64:128, 15:L], in0=Y[64:128, 15:L], in1=Y[64:128, 7:L - 8], op=ALU.add), waits=[e3])
            fin = [(0, 64, Y), (64, 128, Z)]
            elast = e4
        evz = []
        for (p0, p1, Sx) in fin:
            ez = dve.op(lambda e, c=c, p0=p0, p1=p1, Sx=Sx: e.scalar_tensor_tensor(out=zT[p0:p1, c, :], in0=Sx[p0:p1, 16:L], scalar=invw[p0:p1, c:c + 1],
                                                                                     in1=X[p0:p1, c, 16:L], op0=ALU.mult, op1=ALU.subtract),
                        waits=[elast] + pev)
            ez1 = dve.op(lambda e, c=c, p0=p0, p1=p1, Sx=Sx: e.tensor_tensor(out=Sx[p0:p1, 16:32], in0=Sx[p0:p1, 16:32], in1=invc[p0:p1, c, :], op=ALU.mult),
                         waits=[ez])
            ez2 = dve.op(lambda e, c=c, p0=p0, p1=p1, Sx=Sx: e.tensor_tensor(out=zT[p0:p1, c, 0:16], in0=Sx[p0:p1, 16:32], in1=X[p0:p1, c, 16:32], op=ALU.subtract),
                         waits=[ez1])
            evz.append(ez2)
        pev = evz
    for n in range(16):
        v_block(wvr, 0, lambda k, n=n: hT[:, k, n * 128:(n + 1) * 128], blk_idx(0, 0, 1 + n), evac="act")
    for rho in range(4):
        for n1 in range(4):
            v_block(wvr, 1, lambda k, rho=rho, n1=n1: hT[:, k, 512 * n1 + rho:512 * (n1 + 1):4], blk_idx(1, rho, 1 + n1), evac="act")
    for r in range(16):
        v_block(wvr, 2, lambda k, r=r: hT[:, k, r:2048:16], blk_idx(2, r, 1), evac="act")
    ev_z = pev
    for c in range(2):
        for b in range(4):
            k_, bank, fr = pj_ring.get()
            ev = pe.op(lambda e, c=c, b=b, bank=bank: e.matmul(ps[:, bank, 0:512], lhsT=wmixb[:, c, :], rhs=zT[:, c, b * 512:(b + 1) * 512], start=True, stop=True),
                       waits=[fr, bank_free[bank], pq_need("wmix")] + ev_z)
            ev2 = act.op(lambda e, c=c, b=b, bank=bank: e.activation(out=ypool[:, c, b * 512:(b + 1) * 512], in_=ps[:, bank, 0:512], func=AF.Copy,
                                                                      scale=pscale[:, c:c + 1]), waits=[ev])
            pj_ring.release(k_, ev2)
            bank_free[bank] = ev2
    P.barrier()

    acc = P.sb("acc", [128, 4, 2048], F32, R_W)
    PTc = [P.sb("PTc%d" % i, [128, 512], BF16, R_ST + i * KB) for i in range(3)]
    PTp = [P.sb("PTp%d" % i, [128, 512], BF16, R_ST + 4 * KB + i * KB) for i in range(3)]
    pt_free = [None, None, None]
    s_ring = Ring([(0, 1), (2, 3), (4, 5)])
    o_ring = Ring([6, 7])
    MC, MP, MH, MHP = 0, 1, 2, 3

    def qcols(g, s_, slot):
        if g == 0:
            n = 4 * s_ + slot
            return slice(n * 128, (n + 1) * 128)
        if g == 1:
            return slice(s_ * 512 + 128 * slot, s_ * 512 + 128 * (slot + 1))
        r = 4 * s_ + slot
        return slice(r * 128, (r + 1) * 128)

    def kcols(g, s_, slot, prev):
        if g == 0:
            n = 4 * s_ + slot
            kb = n + (0 if prev else 1)
            return slice(kb * 128, (kb + 1) * 128), blk_idx(0, 0, kb)
        if g == 1:
            m = slot + (0 if prev else 1)
            return slice(s_ * 640 + 128 * m, s_ * 640 + 128 * (m + 1)), blk_idx(1, s_, m)
        r = 4 * s_ + slot
        m = 0 if prev else 1
        return slice(r * 256 + 128 * m, r * 256 + 128 * (m + 1)), blk_idx(2, r, m)

    def acc_dst(g, s_, h):
        if g == 0:
            return acc[0:65, h, s_ * 512:(s_ + 1) * 512], None
        if g == 1:
            return acc[0:65, h, s_:2048:4], None
        a3 = acc[0:65, h, :].rearrange("p (a r) -> p r a", r=16)[:, 4 * s_:4 * s_ + 4, :]
        return a3, "p (r a) -> p r a"

    att_q = []
    acc_g0_ev = [None]
    dve.wait(pq_need("masks"))

    def att_finish(pd):
        (g, s_, h, ub, ev_mc, ev_mp) = pd
        ko, bankO, frO = o_ring.get()
        pe.wait(frO, bank_free[bankO], ev_mc, ev_mp)
        ev = None
        for slot in range(4):
            _, bp = kcols(g, s_, slot, True)
            _, bc = kcols(g, s_, slot, False)
            pe.op(lambda e, slot=slot, bp=bp: e.matmul(ps[0:65, bankO, slot * 128:(slot + 1) * 128], lhsT=vA[:, bp, h, 0:65],
                                                       rhs=PTp[ub][:, slot * 128:(slot + 1) * 128], start=True, stop=False), signal=False)
            ev = pe.op(lambda e, slot=slot, bc=bc: e.matmul(ps[0:65, bankO, slot * 128:(slot + 1) * 128], lhsT=vA[:, bc, h, 0:65],
                                                            rhs=PTc[ub][:, slot * 128:(slot + 1) * 128], start=False, stop=True), signal=(slot == 3))
        pt_free[ub] = ev
        dst, rr = acc_dst(g, s_, h)
        src = ps[0:65, bankO, :]
        if rr is not None:
            src = src.rearrange(rr, r=4)
        if g == 0:
            ev2 = act.op(lambda e: e.activation(out=dst, in_=src, func=AF.Copy), waits=[ev])
            acc_g0_ev[0] = ev2
        else:
            ev2 = dve.op(lambda e: e.tensor_tensor(out=dst, in0=src, in1=dst, op=ALU.add), waits=[ev, acc_g0_ev[0]])
        o_ring.release(ko, ev2)
        bank_free[bankO] = ev2
        return ev2

    att_u = 0
    for g in range(3):
        for s_ in range(4):
            for h in range(4):
                ub = att_u % 3
                att_u += 1
                pr, hf = h // 2, h % 2
                p0 = 64 * hf
                ks, (bC, bP), frS = s_ring.get()
                pe.wait(frS, bank_free[bC], bank_free[bP])
                evs = None
                for slot in range(4):
                    qs = qcols(g, s_, slot)
                    kc, _ = kcols(g, s_, slot, False)
                    kp, _ = kcols(g, s_, slot, True)
                    pe.op(lambda e, slot=slot, qs=qs, kc=kc, g=g, p0=p0, pr=pr, bC=bC: e.matmul(ps[:, bC, slot * 128:(slot + 1) * 128], lhsT=kT[g][p0:p0 + 64, pr, kc],
                                                                      rhs=qT[p0:p0 + 64, 2 * g + pr, qs], start=True, stop=True), signal=False)
                    evs = pe.op(lambda e, slot=slot, qs=qs, kp=kp, g=g, p0=p0, pr=pr, bP=bP: e.matmul(ps[:, bP, slot * 128:(slot + 1) * 128], lhsT=kT[g][p0:p0 + 64, pr, kp],
                                                                            rhs=qT[p0:p0 + 64, 2 * g + pr, qs], start=True, stop=True), signal=(slot == 3))
                ev_ec = act.op(lambda e, ub=ub, bC=bC: e.activation(out=PTc[ub][:, :], in_=ps[:, bC, :], func=AF.Exp, scale=0.125),
                               waits=[evs, pt_free[ub]])
                ev_ep = act.op(lambda e, ub=ub, bP=bP: e.activation(out=PTp[ub][:, :], in_=ps[:, bP, :], func=AF.Exp, scale=0.125))
                s_ring.release(ks, ev_ep)
                bank_free[bC] = ev_ep
                bank_free[bP] = ev_ep
                mp = MH if g == 2 else (MHP if (g == 1 or s_ == 0) else MP)
                ev_mc = dve.op(lambda e, ub=ub: e.tensor_tensor(out=PTc[ub][:, :], in0=PTc[ub][:, :], in1=masks[:, MC, :], op=ALU.mult), waits=[ev_ec])
                ev_mp = dve.op(lambda e, ub=ub, mp=mp: e.tensor_tensor(out=PTp[ub][:, :], in0=PTp[ub][:, :], in1=masks[:, mp, :], op=ALU.mult), waits=[ev_ep])
                att_q.append((g, s_, h, ub, ev_mc, ev_mp))
                while len(att_q) > 2:
                    att_finish(att_q.pop(0))
    while att_q:
        att_finish(att_q.pop(0))
    P.barrier()

    tap("hT", hT[:, :, :])
    tap("qT", qT[:, :, :])
    tap("kT0", kT[0][:, :, :])
    tap("kT1", kT[1][:, :, :])
    tap("kT2", kT[2][:, :, :])
    tap("vA", vA[:, :, :, :])
    tap("acc", acc[0:65, :, :])
    tap("ypool", ypool[:, :, :])
    if stop_after == "attn":
        return finish()
    wpa = P.sb("wpa", [128, 2, D], BF16, R_V)
    selb = P.sb("selb", [64, 2, 128], BF16, R_PT + 256)
    oT2 = P.sb("oT2", [128, 2, 2048], BF16, R_Q + 16 * KB)
    wpp = P.sb("wpp", [128, 2, D], BF16, R_V + 8 * KB)
    wgs = [P.sb("wgs%d" % i, [128, 2, 8, 128], BF16, R_V + 12 * KB + i * 4 * KB) for i in range(2)]
    d_wm2 = P.dsem("d_wm2")
    pool.dma(wpa[:, :, :], wpa_d[:, :, :], d_wm2)
    pool.dma(selb[:, :, :], sel_d[:, :, :], d_wm2)
    ev_wp = pool.dma(wpp[:, :, :], wpp_d[:, :, :], d_wm2)
    d_wg = [P.dsem("d_wg%d" % i) for i in range(2)]
    ev_wg_next = pool.dma(wgs[0][:, :, :, :], wg_d[0], d_wg[0])
    oT = P.sb("oT", [128, 4, 2048], BF16, R_Q)
    lnrow1 = P.sb("lnrow", [128, 2048], F32, R_ST)
    rrow = [P.sb("rrow0", [128, 2048], F32, R_ST + 8 * KB)] + \
           [P.sb("rrow%d" % (i + 1), [128, 2048], F32, R_K + i * 8 * KB) for i in range(3)]
    hirow = [P.sb("hirow%d" % i, [128, 2048], BF16, R_K + 24 * KB + i * 4 * KB) for i in range(2)]
    lorow = [P.sb("lorow0", [128, 2048], BF16, R_V + 4 * KB), P.sb("lorow1", [128, 2048], BF16, R_V + 28 * KB)]
    onesb = P.sb("onesb", [128, 64], BF16, R_PT)
    ev_ob = dve.op(lambda e: e.memset(onesb[:, :], 1.0))
    row_free = [None, None]
    lo_ev = [None, None]
    ex_prev = [None]
    oT_ev = {}
    ev_hl = []
    for h in range(4):
        hb = h % 2
        ev_ln = act.op(lambda e, h=h: e.activation(out=lnrow1[64:65, :], in_=acc[64:65, h, :], func=AF.Ln), waits=[ex_prev[0]])
        ev_ex = act.op(lambda e, h=h: e.activation(out=rrow[h][64:65, :], in_=lnrow1[64:65, :], func=AF.Exp, scale=-1.0), waits=[ev_ln])
        ex_prev[0] = ev_ex
        ev_hi = dve.op(lambda e, hb=hb, h=h: e.tensor_copy(out=hirow[hb][64:65, :], in_=rrow[h][64:65, :]), waits=[ev_ex, row_free[hb]])
        ev_lo = pool.op(lambda e, hb=hb, h=h: e.tensor_tensor(out=lorow[hb][64:65, :], in0=rrow[h][64:65, :], in1=hirow[hb][64:65, :], op=ALU.subtract),
                        waits=[ev_hi, row_free[hb]])
        lo_ev[hb] = ev_lo
        evu = None
        for b in range(4):
            k_, bank, fr = pj_ring.get()
            pe.op(lambda e, hb=hb, b=b, bank=bank: e.matmul(ps[0:64, bank, 0:512], lhsT=onesb[64:65, 0:64], rhs=hirow[hb][64:65, b * 512:(b + 1) * 512],
                                                            start=True, stop=False), waits=[fr, bank_free[bank], ev_hi, ev_lo, ev_ob], signal=False)
            ev = pe.op(lambda e, hb=hb, b=b, bank=bank: e.matmul(ps[0:64, bank, 0:512], lhsT=onesb[64:65, 0:64], rhs=lorow[hb][64:65, b * 512:(b + 1) * 512],
                                                                 start=False, stop=True))
            ev2 = dve.op(lambda e, h=h, b=b, bank=bank: e.tensor_tensor(out=oT[0:64, h, b * 512:(b + 1) * 512], in0=ps[0:64, bank, 0:512],
                                                                        in1=acc[0:64, h, b * 512:(b + 1) * 512], op=ALU.mult), waits=[ev])
            pj_ring.release(k_, ev2)
            bank_free[bank] = ev2
            evu = ev
            oT_ev[(h, b)] = ev2
        row_free[hb] = evu
        if h % 2 == 1:
            p_ = h // 2
            for b in range(4):
                k_, bank, fr = pj_ring.get()
                pe.op(lambda e, p_=p_, b=b, bank=bank: e.matmul(ps[:, bank, 0:512], lhsT=selb[0:64, 0, :], rhs=oT[0:64, 2 * p_, b * 512:(b + 1) * 512],
                                                                start=True, stop=False),
                      waits=[fr, bank_free[bank], ev_wp, oT_ev[(2 * p_, b)], oT_ev[(2 * p_ + 1, b)]], signal=False)
                ev = pe.op(lambda e, p_=p_, b=b, bank=bank: e.matmul(ps[:, bank, 0:512], lhsT=selb[0:64, 1, :], rhs=oT[0:64, 2 * p_ + 1, b * 512:(b + 1) * 512],
                                                                     start=False, stop=True))
                ev2 = dve.op(lambda e, p_=p_, b=b, bank=bank: e.tensor_copy(out=oT2[:, p_, b * 512:(b + 1) * 512], in_=ps[:, bank, 0:512]), waits=[ev])
                pj_ring.release(k_, ev2)
                bank_free[bank] = ev2
    P.barrier()

    tap("oT", oT[0:64, :, :])
    if stop_after == "norm":
        return finish()

    mT = P.sb("mT", [128, 8, 2048], BF16, R_K)
    sga = [P.sb("sga%d" % i, [128, 512], F32, R_V + 20 * KB + i * 2 * KB) for i in range(2)]
    sgp = [P.sb("sgp%d" % i, [128, 512], F32, R_V + 24 * KB + i * 2 * KB) for i in range(2)]
    m1 = [P.sb("m1_%d" % i, [128, 512], F32, R_ST + i * 2 * KB) for i in range(2)]
    m2 = [P.sb("m2_%d" % i, [128, 512], F32, R_ST + 4 * KB + i * 2 * KB) for i in range(2)]
    wout = P.sb("wout", [128, 8, D], BF16, R_W)
    gpost = P.sb("gpost", [128, D], F32, R_W + 16 * KB)
    wg_free = [None, None]
    d_wo = P.dsem("d_wo")
    mring = Ring([(0, 1, 2, 3), (4, 5, 6, 7)])
    sg_free = [None, None]
    m_free = [None, None]
    mu = 0
    for c in range(8):
        wb = c % 2
        ev_wg = ev_wg_next
        if c + 1 < 8:
            ev_wg_next = pool.dma(wgs[(c + 1) % 2][:, :, :, :], wg_d[c + 1], d_wg[(c + 1) % 2], waits=[wg_free[(c + 1) % 2]])
        if c == 1:
            for k in range(8):
                ev_wo = pool.dma(wout[:, k, :], wout_d[:, k, :], d_wo)
            ev_gp = sp.dma(gpost[:, :], gpost_d[:, :], P.dsem("d_gpost"))
        for b in range(4):
            ub = mu % 2
            mu += 1
            km, (b1, b2, b3, b4), frm = mring.get()
            tok = slice(b * 512, (b + 1) * 512)
            pe.wait(frm, bank_free[b1], bank_free[b2], bank_free[b3], bank_free[b4], ev_wg, ev_wp)
            for k in range(8):
                e1 = pe.op(lambda e, k=k, wb=wb, b1=b1, tok=tok: e.matmul(ps[:, b1, :], lhsT=wgs[wb][:, 0, k, :], rhs=hT[:, k, tok], start=(k == 0), stop=(k == 7)),
                           signal=(k == 7))
            for k in range(8):
                e2 = pe.op(lambda e, k=k, wb=wb, b2=b2, tok=tok: e.matmul(ps[:, b2, :], lhsT=wgs[wb][:, 1, k, :], rhs=hT[:, k, tok], start=(k == 0), stop=(k == 7)),
                           signal=(k == 7))
            for p_ in range(2):
                e3 = pe.op(lambda e, p_=p_, c=c, b3=b3, tok=tok: e.matmul(ps[:, b3, :], lhsT=wpa[:, p_, c * 128:(c + 1) * 128], rhs=oT2[:, p_, tok],
                                                                          start=(p_ == 0), stop=(p_ == 1)), signal=(p_ == 1))
            for j in range(2):
                e4 = pe.op(lambda e, j=j, c=c, b4=b4, tok=tok: e.matmul(ps[:, b4, :], lhsT=wpp[:, j, c * 128:(c + 1) * 128], rhs=ypool[:, j, tok],
                                                                        start=(j == 0), stop=(j == 1)), signal=(j == 1))
            if b == 3:
                wg_free[wb] = e2
            ea = act.op(lambda e, ub=ub, b1=b1: e.activation(out=sga[ub][:, :], in_=ps[:, b1, :], func=AF.Sigmoid), waits=[e1, sg_free[ub]])
            eb = act.op(lambda e, ub=ub, b2=b2: e.activation(out=sgp[ub][:, :], in_=ps[:, b2, :], func=AF.Sigmoid), waits=[e2])
            em1 = dve.op(lambda e, ub=ub, b3=b3: e.tensor_tensor(out=m1[ub][:, :], in0=ps[:, b3, :], in1=sga[ub][:, :], op=ALU.mult), waits=[ea, e3, m_free[ub]])
            em2 = dve.op(lambda e, ub=ub, b4=b4: e.tensor_tensor(out=m2[ub][:, :], in0=ps[:, b4, :], in1=sgp[ub][:, :], op=ALU.mult), waits=[eb, e4])
            sg_free[ub] = em2
            mring.release(km, em2)
            for bb in (b1, b2, b3, b4):
                bank_free[bb] = em2
            eo = pool.op(lambda e, ub=ub, c=c, tok=tok: e.tensor_tensor(out=mT[:, c, tok], in0=m1[ub][:, :], in1=m2[ub][:, :], op=ALU.add), waits=[em1, em2])
            m_free[ub] = eo
    P.barrier()

    tap("mT", mT[:, :, :])
    if stop_after == "merge":
        return finish()
    wdn = P.sb("wdn", [128, NJ, D], BF16, R_Q + 10 * KB)
    assert R_Q + 54 * KB <= R_W and R_K + 44 * KB <= R_Q + 10 * KB
    gfin = P.sb("gfin", [128, D], F32, R_W + 20 * KB)
    xin = [P.sb("xin%d" % i, [128, D], F32, R_ST + i * 4 * KB) for i in range(4)]
    x1b = [P.sb("x1b0", [128, D], F32, R_PT), P.sb("x1b1", [128, D], F32, R_WS), P.sb("x1b2", [128, D], F32, R_WS + 4 * KB)]
    tb = [P.sb("tb%d" % i, [128, D], F32, R_ZT + i * 4 * KB) for i in range(2)] + [P.sb("tb2", [128, D], F32, R_Q + 54 * KB)]
    assert R_Q + 58 * KB <= R_W
    xs2 = [P.sb("xs2_%d" % i, [128, D], BF16, R_YP + i * 2 * KB) for i in range(2)]
    junks2 = [P.sb("junk2_%d" % i, [128, D], BF16, R_YP + 4 * KB + i * 2 * KB) for i in range(2)]
    junk_st2 = {"i": 0, "ev": [None, None]}
    d_xi = [P.dsem("d_xi%d" % i) for i in range(4)]
    d_x1o = [P.dsem("d_x1o%d" % i) for i in range(3)]
    xin_free = [None, None, None, None]
    x1b_free = [None, None, None]
    tb_free = [None, None, None]
    xs2_free = [None, None]
    pair_ring = Ring([(2, 3), (4, 5), (6, 7)])
    tr_ring2 = Ring([0, 1])

    def pair_ap(pr_):
        return ps[:, pr_[0]:pr_[0] + 2, :].rearrange("p a b -> p (a b)")

    x1_store_ev = {}

    def wout_stage1(i):
        b = i % 4
        ev_x = sp.dma(xin[b][:, :], xo[i * 128:(i + 1) * 128, :], d_xi[b], waits=[xin_free[b]])
        kp, pr_, frp = pair_ring.get()
        pe.wait(frp, bank_free[pr_[0]], bank_free[pr_[1]], ev_wo)
        ev = None
        for hf in range(2):
            for k in range(8):
                ev = pe.op(lambda e, k=k, hf=hf, pr_=pr_: e.matmul(ps[:, pr_[0] + hf, :], lhsT=mT[:, k, i * 128:(i + 1) * 128], rhs=wout[:, k, hf * 512:(hf + 1) * 512],
                                                                  start=(k == 0), stop=(k == 7)), signal=(k == 7 and hf == 1))
        return (i, b, kp, pr_, ev, ev_x)

    def norm_part(pr_, ev_mm, col, gvec, ev_g, b):
        yap = pair_ap(pr_)
        ev_sq = sq_accum(yap, col, [ev_mm], junks2, junk_st2)
        ev_rs = rstd_chain(ev_sq, col)
        ev_t = dve.op(lambda e: e.scalar_tensor_tensor(out=tb[b][:, :], in0=yap, scalar=rsA[:, col:col + 1], in1=gvec[:, :], op0=ALU.mult, op1=ALU.mult),
                      waits=[ev_rs, tb_free[b], ev_g])
        return ev_t

    def add_part(ev_t, xres, ev_xres, outbuf, out_free, b):
        ev_o = pool.op(lambda e: e.tensor_tensor(out=outbuf[:, :], in0=tb[b][:, :], in1=xres[:, :], op=ALU.add), waits=[ev_t, ev_xres, out_free])
        tb_free[b] = ev_o
        return ev_o

    def norm_res(pr_, ev_mm, col, gvec, ev_g, xres, ev_xres, outbuf, out_free, b):
        ev_t = norm_part(pr_, ev_mm, col, gvec, ev_g, b)
        ev_o = add_part(ev_t, xres, ev_xres, outbuf, out_free, b)
        return ev_t, ev_o

    def wout_stage2a(st):
        (i, b, kp, pr_, ev_mm, ev_x) = st
        col = ss_col[0] % 48
        ss_col[0] += 1
        ev_t = norm_part(pr_, ev_mm, col, gpost, ev_gp, i % 3)
        pair_ring.release(kp, ev_t)
        bank_free[pr_[0]] = ev_t
        bank_free[pr_[1]] = ev_t
        return (i, b, ev_t, ev_x)

    def wout_stage2add(st):
        (i, b, ev_t, ev_x) = st
        xb = i % 3
        ev_o = add_part(ev_t, xin[b], ev_x, x1b[xb], x1b_free[xb], i % 3)
        xin_free[b] = ev_o
        ev_st = sp.dma(x1s_d[i * 128:(i + 1) * 128, :], x1b[xb][:, :], d_x1o[xb], waits=[ev_o])
        x1_store_ev[i] = ev_st
        return (i, xb, ev_o, ev_st)

    def wout_stage2b1(st):
        (i, b, ev_o, ev_st) = st
        col2 = ss_col[0] % 48
        ss_col[0] += 1
        ev_sq = sq_accum(x1b[b][:, :], col2, [ev_o], junks2, junk_st2)
        ev_rs = rstd_chain(ev_sq, col2)
        return (i, b, ev_st, col2, ev_rs)

    def wout_stage2b2(st):
        (i, b, ev_st, col2, ev_rs) = st
        b2 = i % 2
        ev_xs = act.op(lambda e: e.activation(out=xs2[b2][:, :], in_=x1b[b][:, :], func=AF.Copy, scale=rsA[:, col2:col2 + 1]), waits=[ev_rs, xs2_free[b2]])
        x1b_free[b] = [ev_xs, ev_st]
        k_, bank, fr = tr_ring2.get()
        psb = ps[:, bank, :].bitcast(BF16).rearrange("p (k t) -> p k t", k=8)
        pe.wait(ev_xs, fr, bank_free[bank])
        ev_tr = None
        for k in range(8):
            ev_tr = pe.op(lambda e, k=k: e.transpose(out=psb[:, k, :], in_=xs2[b2][:, k * 128:(k + 1) * 128], identity=ident[:, :]), signal=(k == 7))
        xs2_free[b2] = ev_tr
        ev_ev = dve.op(lambda e: e.tensor_tensor(out=hT[:, :, i * 128:(i + 1) * 128], in0=psb, in1=gffn[:, :].unsqueeze(2).to_broadcast([128, 8, 128]), op=ALU.mult),
                       waits=[ev_tr])
        tr_ring2.release(k_, ev_ev)
        bank_free[bank] = ev_ev

    wgu_boot = [P.sb("wgub%d" % i, [128, 2, 8, 128], BF16, R_Q + i * 4 * KB) for i in range(2)]
    d_gub = [P.dsem("d_gub%d" % i) for i in range(2)]
    boot_ev = [pool.dma(wgu_boot[i][:, :, :, :], wgu_d[i], d_gub[i]) for i in range(2)]
    d_wd = P.dsem("d_wd")
    ev_gf = sp.dma(gfin[:, :], gfin_d[:, :], P.dsem("d_gfin"))
    st1, st2, st3, st4 = {}, {}, {}, {}
    for it in range(NT + 4):
        if it < NT:
            st1[it] = wout_stage1(it)
        if 0 <= it - 3 < NT:
            st3[it - 3] = wout_stage2add(st2[it - 3])
        if 0 <= it - 4 < NT:
            st4[it - 4] = wout_stage2b1(st3[it - 4])
        if 0 <= it - 1 < NT:
            st2[it - 1] = wout_stage2a(st1[it - 1])
        if 0 <= it - 4 < NT:
            wout_stage2b2(st4[it - 4])
    P.barrier()

    tap("h2T", hT[:, :, :])
    if stop_after == "wout":
        return finish()
    ffT = P.sb("ffT", [128, NJ, 1024], BF16, R_K)
    wgu = [P.sb("wgu%d" % i, [128, 2, 8, 128], BF16, R_W + i * 4 * KB) for i in range(3)]
    sa = [P.sb("sa%d" % i, [128, 512], F32, R_W + 12 * KB + i * 2 * KB) for i in range(2)]
    d_gu = [P.dsem("d_gu%d" % i) for i in range(3)]
    gu_free = [None, None, None]
    sa_free = [None, None]
    d_x1i = [P.dsem("d_x1i%d" % i) for i in range(2)]
    d_out = [P.dsem("d_out%d" % i) for i in range(2)]
    obuf = x1b
    obuf_free = [x1b_free[0], x1b_free[1]]
    tb_free = [pool.last(), pool.last(), pool.last()]
    ab_ring = Ring([(0, 1), (2, 3)])
    pair_ring2 = Ring([(4, 5), (6, 7)])
    out_evs = []
    fu = 0
    gl = 0
    loads = [(hf, j) for hf in range(2) for j in range(NJ)]
    ld_ev = {}

    def issue_gu(idx):
        hf, j = loads[idx]
        b = idx % 3
        ld_ev[idx] = pool.dma(wgu[b][:, :, :, :], wgu_d[j], d_gu[b], waits=[gu_free[b]])

    ld_ev[0], ld_ev[1] = boot_ev

    def wgu_buf(idx):
        return wgu_boot[idx] if idx < 2 else wgu[idx % 3]
    for hf in range(2):
        for j in range(NJ):
            idx = hf * NJ + j
            if idx + 2 < len(loads):
                issue_gu(idx + 2)
            if hf == 0:
                ev_wd = pool.dma(wdn[:, j, :], wdn_d[:, j, :], d_wd)
            wb = idx % 3
            wbuf = wgu_buf(idx)
            for b in range(2):
                ub = fu % 2
                fu += 1
                tok = slice(hf * 1024 + b * 512, hf * 1024 + (b + 1) * 512)
                ka, (ba, bb), fra = ab_ring.get()
                pe.wait(fra, bank_free[ba], bank_free[bb], ld_ev[idx])
                for k in range(8):
                    e1 = pe.op(lambda e, k=k, wbuf=wbuf, ba=ba, tok=tok: e.matmul(ps[:, ba, :], lhsT=wbuf[:, 0, k, :], rhs=hT[:, k, tok], start=(k == 0), stop=(k == 7)),
                               signal=(k == 7))
                for k in range(8):
                    e2 = pe.op(lambda e, k=k, wbuf=wbuf, bb=bb, tok=tok: e.matmul(ps[:, bb, :], lhsT=wbuf[:, 1, k, :], rhs=hT[:, k, tok], start=(k == 0), stop=(k == 7)),
                               signal=(k == 7))
                if b == 1:
                    gu_free[wb] = e2
                es = act.op(lambda e, ub=ub, ba=ba: e.activation(out=sa[ub][:, :], in_=ps[:, ba, :], func=AF.Silu), waits=[e1, sa_free[ub]])
                ef = dve.op(lambda e, ub=ub, bb=bb, j=j, b=b: e.tensor_tensor(out=ffT[:, j, b * 512:(b + 1) * 512], in0=ps[:, bb, :], in1=sa[ub][:, :], op=ALU.mult),
                            waits=[es, e2])
                sa_free[ub] = ef
                ab_ring.release(ka, ef)
                bank_free[ba] = ef
                bank_free[bb] = ef
        ev_ff = dve.last()
        for il in range(8):
            i = hf * 8 + il
            b = i % 2
            ev_x1 = sp.dma(xin[b][:, :], x1s_d[i * 128:(i + 1) * 128, :], d_x1i[b], waits=[xin_free[b], x1_store_ev[i]])
            kp, pr_, frp = pair_ring2.get()
            pe.wait(frp, bank_free[pr_[0]], bank_free[pr_[1]], ev_wd, ev_ff)
            ev = None
            for h2 in range(2):
                for j in range(NJ):
                    ev = pe.op(lambda e, j=j, h2=h2, pr_=pr_, il=il: e.matmul(ps[:, pr_[0] + h2, :], lhsT=ffT[:, j, il * 128:(il + 1) * 128],
                                                                               rhs=wdn[:, j, h2 * 512:(h2 + 1) * 512], start=(j == 0), stop=(j == NJ - 1)),
                               signal=(j == NJ - 1 and h2 == 1))
            col = ss_col[0] % 48
            ss_col[0] += 1
            ev_t, ev_o = norm_res(pr_, ev, col, gfin, ev_gf, xin[b], ev_x1, obuf[b], obuf_free[b], b)
            pair_ring2.release(kp, ev_t)
            bank_free[pr_[0]] = ev_t
            bank_free[pr_[1]] = ev_t
            xin_free[b] = ev_o
            ev_out = sp.dma(out_d[i * 128:(i + 1) * 128, :], obuf[b][:, :], d_out[b], waits=[ev_o])
            obuf_free[b] = ev_out
            out_evs.append(ev_out)
        dve.wait(pe.last())
    sp.wait(out_evs[-1], out_evs[-2])
    for e_ in P.cengs:
        e_.wait(out_evs[-1], out_evs[-2])
    P.run()
    pscm.__exit__(None, None, None)
    return nc


def _rope_tables():
    half = 32
    inv_freq = 10000.0 ** (-(np.arange(half, dtype=np.float64) * 2.0 / 64.0))
    pos = np.arange(-HT, S, dtype=np.float64)
    ang = pos[:, None] * inv_freq[None, :]
    cos = np.cos(ang).astype(np.float32)
    sin = np.sin(ang).astype(np.float32)
    d = np.arange(128) % 64
    i = d % 32
    sgn = np.where(d < 32, -1.0, 1.0).astype(np.float32)
    C = np.ascontiguousarray(cos[:, i].T)
    Sn = np.ascontiguousarray((sin[:, i] * sgn[None, :]).T)
    return C, Sn


_PROG_CACHE = {}


def _host_inputs(x, g_pre_mix, w_in, w_pool_mix, pool_scale, w_proj_attn, w_proj_pool, w_out,
                 g_post_mix, g_pre_ffn, w_gate_up, w_down, g_post_ffn):
    f = np.float32
    x = np.asarray(x, f).reshape(S, D)
    w_in = np.asarray(w_in, f)[0]
    def chunks(wcols, n):
        nc_ = wcols.shape[1] // n
        return np.ascontiguousarray(wcols.reshape(8, 128, nc_, n).transpose(2, 1, 0, 3))
    wq = chunks(w_in[:, 0:768], 128)
    wk = chunks(w_in[:, 768:1536], 128)
    wv = chunks(w_in[:, 1536:2304], 256)
    wu = chunks(w_in[:, 2304:2560], 128)
    wga = chunks(w_in[:, 2560:3584], 128)
    wgp = chunks(w_in[:, 3584:4608], 128)
    wg = np.ascontiguousarray(np.stack([wga, wgp], axis=2))
    wmix = np.ascontiguousarray(np.asarray(w_pool_mix, f)[0])
    wpa = np.ascontiguousarray(np.asarray(w_proj_attn, f)[0].reshape(2, 128, D).transpose(1, 0, 2))
    sel = np.zeros((64, 2, 128), f)
    sel[np.arange(64), 0, np.arange(64)] = 1.0
    sel[np.arange(64), 1, np.arange(64) + 64] = 1.0
    wpp = np.ascontiguousarray(np.asarray(w_proj_pool, f)[0].reshape(2, 128, D).transpose(1, 0, 2))
    wout = np.ascontiguousarray(np.asarray(w_out, f)[0].reshape(8, 128, D).transpose(1, 0, 2))
    wgu_full = np.asarray(w_gate_up, f)[0]
    wa = chunks(wgu_full[:, 0:FF], 128)
    wb = chunks(wgu_full[:, FF:2 * FF], 128)
    wgu = np.ascontiguousarray(np.stack([wa, wb], axis=2))
    wdn = np.ascontiguousarray(np.asarray(w_down, f)[0].reshape(NJ, 128, D).transpose(1, 0, 2))
    def fm(g):
        return np.ascontiguousarray(np.asarray(g, f).reshape(8, 128).T)
    gpre = fm(g_pre_mix[0])
    gffn = fm(g_pre_ffn[0])
    gpost = np.ascontiguousarray(np.broadcast_to(np.asarray(g_post_mix, f)[0][None, :], (128, D)))
    gfin = np.ascontiguousarray(np.broadcast_to(np.asarray(g_post_ffn, f)[0][None, :], (128, D)))
    pscale = np.ascontiguousarray(np.asarray(pool_scale, f)[0].reshape(2, 128).T)
    wins = np.array([2, 4, 8, 16], dtype=f)
    wpart = wins[(np.arange(256) // 64)]
    invw = np.ascontiguousarray((1.0 / wpart).astype(f).reshape(2, 128).T)
    C, Sn = _rope_tables()
    ident = np.eye(128, dtype=f)
    rotm = np.zeros((128, 128), f)
    for dd in range(128):
        base = (dd // 64) * 64
        rotm[base + ((dd % 64) + 32) % 64, dd] = 1.0
    kk = np.arange(128)[:, None]
    qq = np.arange(128)[None, :]
    mC = (kk <= qq).astype(f)
    mP = (kk >= qq).astype(f)
    common = dict(wq=wq, wk=wk, wv=wv, wu=wu, wg=wg, wmix=wmix, wpa=wpa, sel=sel, wpp=wpp, wout=wout, wgu=wgu, wdn=wdn,
                  gpre=gpre, gffn=gffn, gpost=gpost, gfin=gfin, pscale=pscale, invw=invw, ident=ident, rotm=rotm)
    in_maps = []
    for c in range(NCORES):
        mH = mP if c > 0 else np.zeros_like(mP)
        masks = np.zeros((128, 4, 512), f)
        masks[:, 0] = np.tile(mC, (1, 4))
        masks[:, 1] = np.tile(mP, (1, 4))
        masks[:, 2] = np.tile(mH, (1, 4))
        masks[:, 3] = np.concatenate([mH, mP, mP, mP], axis=1)
        xo = x[c * T:(c + 1) * T]
        xh_ = x[c * T - HT:c * T] if c > 0 else np.zeros((HT, D), f)
        tpos = np.arange(16, dtype=f) + c * T
        cnt = np.minimum(tpos[None, :] + 1.0, wpart[:, None])
        invc = np.ascontiguousarray((1.0 / cnt).astype(f).reshape(2, 128, 16).transpose(1, 0, 2))
        m = dict(common)
        m.update(xh=np.ascontiguousarray(xh_), xo=np.ascontiguousarray(xo), masks=masks, invc=invc,
                 cost=np.ascontiguousarray(C[:, c * T:c * T + HT + T]), sint=np.ascontiguousarray(Sn[:, c * T:c * T + HT + T]))
        in_maps.append(m)
    return in_maps


def kernel(x, g_pre_mix, w_in, w_pool_mix, pool_scale, w_proj_attn, w_proj_pool, w_out,
           g_post_mix, g_pre_ffn, w_gate_up, w_down, g_post_ffn):
    in_maps = _host_inputs(x, g_pre_mix, w_in, w_pool_mix, pool_scale, w_proj_attn, w_proj_pool, w_out,
                           g_post_mix, g_pre_ffn, w_gate_up, w_down, g_post_ffn)
    nc = build_program()
    res = run_bass_kernel_spmd(nc, in_maps, core_ids=list(range(NCORES)))
    out = np.concatenate([np.asarray(r["out"], np.float32) for r in res.results], axis=0)
    return out.reshape(1, S, D)
```

```python
import numpy as np
import concourse.bass as bass
import concourse.mybir as mybir
from concourse.bass_utils import run_bass_kernel_spmd

F32 = mybir.dt.float32
BF16 = mybir.dt.bfloat16
AF = mybir.ActivationFunctionType
ALU = mybir.AluOpType

NCORES = 8
S = 16384
D = 1024
T = S // NCORES
NT = T // 128
HT = 2048
FF = 2816
NJ = FF // 128
EPS = 1e-6
KB = 1024

DEBUG_TAPS = False


class Ev:
    __slots__ = ("sem", "val")

    def __init__(self, sem, val):
        self.sem = sem
        self.val = val


class Eng:
    def __init__(self, name):
        self.name = name
        self.ops = []
        self.meta = []
        self.sem = None
        self.count = 0
        self.seen = {}

    def wait(self, *evs):
        for ev in evs:
            if ev is None:
                continue
            if isinstance(ev, (list, tuple)):
                self.wait(*ev)
                continue
            key = id(ev.sem)
            if self.seen.get(key, 0) >= ev.val:
                continue
            self.seen[key] = ev.val
            sem, val = ev.sem, ev.val
            self.meta.append(("wait", id(sem), val))
            self.ops.append(lambda e, sem=sem, val=val: e.wait_ge(sem, val))

    def op(self, fn, waits=(), signal=True):
        self.wait(*waits)
        if signal:
            self.count += 1
            sem, val = self.sem, self.count
            self.meta.append(("inc", id(sem), 1))
            self.ops.append(lambda e, fn=fn, sem=sem: fn(e).then_inc(sem, 1))
            return Ev(sem, val)
        self.ops.append(lambda e, fn=fn: fn(e))
        return None

    def last(self):
        return Ev(self.sem, self.count) if self.count else None

    def dma(self, out, in_, dsem, waits=(), **kw):
        self.wait(*waits)
        dsem.count += 16
        sem = dsem.sem
        self.meta.append(("inc", id(sem), 16))
        self.ops.append(lambda e, out=out, in_=in_, sem=sem, kw=kw: e.dma_start(out=out, in_=in_, **kw).then_inc(sem, 16))
        return Ev(sem, dsem.count)


class DSem:
    def __init__(self, sem):
        self.sem = sem
        self.count = 0


class Prog:
    def __init__(self, nc):
        self.nc = nc
        self.pe = Eng("tensor")
        self.act = Eng("scalar")
        self.dve = Eng("vector")
        self.pool = Eng("gpsimd")
        self.sp = Eng("sync")
        self.cengs = [self.pe, self.act, self.dve, self.pool]
        self.engs = self.cengs + [self.sp]
        self._ctx = []
        for e in self.engs:
            e.sem = self.new_sem("p_" + e.name)
        self.nsb = 0

    def new_sem(self, name):
        cm = self.nc.semaphore(name)
        h = cm.__enter__()
        self._ctx.append(cm)
        return h

    def dsem(self, name):
        return DSem(self.new_sem(name))

    def sb(self, name, shape, dtype, off):
        esz = 2 if dtype == BF16 else 4
        n = 1
        for s_ in shape[1:]:
            n *= s_
        assert off % 32 == 0, (name, off)
        assert off + n * esz <= ARENA_BYTES, (name, off, n * esz)
        self.nsb += 1
        return self.nc.alloc_sbuf_tensor_at("%s_%d" % (name, self.nsb), list(shape), dtype, offset=ARENA_BASE + off)

    def barrier(self):
        evs = [e.last() for e in self.cengs]
        for e in self.engs:
            e.wait(*evs)

    def check_deadlock(self):
        semv = {}
        pos = {e.name: 0 for e in self.engs}
        progress = True
        while progress:
            progress = False
            for e in self.engs:
                while pos[e.name] < len(e.meta):
                    kind, sid, val = e.meta[pos[e.name]]
                    if kind == "wait":
                        if semv.get(sid, 0) >= val:
                            pos[e.name] += 1
                            progress = True
                        else:
                            break
                    else:
                        semv[sid] = semv.get(sid, 0) + val
                        pos[e.name] += 1
                        progress = True
        stuck = {e.name: (pos[e.name], len(e.meta), e.meta[pos[e.name]]) for e in self.engs if pos[e.name] < len(e.meta)}
        if stuck:
            raise RuntimeError("static deadlock: %r" % (stuck,))

    def run(self):
        self.check_deadlock()
        nc = self.nc
        with nc.Block() as block:
            @block.tensor
            def _(e):
                for f in self.pe.ops:
                    f(e)

            @block.scalar
            def _(e):
                for f in self.act.ops:
                    f(e)

            @block.vector
            def _(e):
                for f in self.dve.ops:
                    f(e)

            @block.gpsimd
            def _(e):
                for f in self.pool.ops:
                    f(e)

            @block.sync
            def _(e):
                for f in self.sp.ops:
                    f(e)
        for cm in reversed(self._ctx):
            cm.__exit__(None, None, None)


ARENA_BASE = 18432
ARENA_BYTES = 229376 - ARENA_BASE


class Ring:
    def __init__(self, items):
        self.items = list(items)
        self.free = [None] * len(self.items)
        self.i = 0

    def get(self):
        k = self.i % len(self.items)
        self.i += 1
        return k, self.items[k], self.free[k]

    def release(self, k, ev):
        self.free[k] = ev


def build_program(taps=(), stop_after=None):
    nc = bass.Bass("TRN2", target_bir_lowering=False)

    def din(name, shape, dt=F32):
        return nc.dram_tensor(name, list(shape), dt, kind="ExternalInput").ap()

    xh = din("xh", [HT, D])
    xo = din("xo", [T, D])
    wq_d = din("wq", [6, 128, 8, 128])
    wk_d = din("wk", [6, 128, 8, 128])
    wv_d = din("wv", [3, 128, 8, 256])
    wu_d = din("wu", [2, 128, 8, 128])
    wg_d = din("wg", [8, 128, 2, 8, 128])
    wmix_d = din("wmix", [4, 64, 64])
    wpa_d = din("wpa", [128, 2, D])
    sel_d = din("sel", [64, 2, 128])
    wpp_d = din("wpp", [128, 2, D])
    wout_d = din("wout", [128, 8, D])
    wgu_d = din("wgu", [NJ, 128, 2, 8, 128])
    wdn_d = din("wdn", [128, NJ, D])
    gpre_d = din("gpre", [128, 8])
    gffn_d = din("gffn", [128, 8])
    gpost_d = din("gpost", [128, D])
    gfin_d = din("gfin", [128, D])
    pscale_d = din("pscale", [128, 2])
    invw_d = din("invw", [128, 2])
    invc_d = din("invc", [128, 2, 16])
    cos_d = din("cost", [128, HT + T])
    sin_d = din("sint", [128, HT + T])
    ident_d = din("ident", [128, 128])
    rotm_d = din("rotm", [128, 128])
    masks_d = din("masks", [128, 4, 512])
    out_d = nc.dram_tensor("out", [T, D], F32, kind="ExternalOutput").ap()
    x1s_d = nc.dram_tensor("x1s", [T, D], F32, kind="Internal").ap()
    tap_d = {}
    for name, shape, dt in taps:
        tap_d[name] = nc.dram_tensor("tap_" + name, list(shape), dt, kind="ExternalOutput").ap()

    P = Prog(nc)
    pe, act, dve, pool, sp = P.pe, P.act, P.dve, P.pool, P.sp
    d_tap = P.dsem("d_tap")

    def tap(name, ap):
        if name not in tap_d:
            return
        P.barrier()
        nd = len(tap_d[name].shape)
        idx = tuple(slice(None) for _ in range(nd))
        evt = sp.dma(tap_d[name][idx], ap, d_tap)
        for e_ in P.engs:
            e_.wait(evt)

    def finish():
        P.barrier()
        P.run()
        pscm.__exit__(None, None, None)
        return nc
    pscm = nc.psum_tensor("ps", [128, 8, 512], F32)
    ps = pscm.__enter__()
    bank_free = [None] * 8

    o = 0

    def take(nbytes):
        nonlocal o
        r = o
        o += (nbytes + 63) // 64 * 64
        return r

    ident = P.sb("ident", [128, 128], BF16, take(256))
    rotm = P.sb("rotm", [128, 128], BF16, take(256))
    masks = P.sb("masks", [128, 4, 512], BF16, take(4096))
    gpre = P.sb("gpre", [128, 8], F32, take(32))
    gffn = P.sb("gffn", [128, 8], F32, take(32))
    pscale = P.sb("pscale", [128, 2], F32, take(8))
    invw = P.sb("invw", [128, 2], F32, take(8))
    invc = P.sb("invc", [128, 2, 16], F32, take(128))
    uh = P.sb("uh", [128, 2, 16], F32, take(128))
    ssA = P.sb("ssA", [128, 48], F32, take(192))
    msA = P.sb("msA", [128, 48], F32, take(192))
    rsA = P.sb("rsA", [128, 48], F32, take(192))
    nhalf = P.sb("nhalf", [128, 1], F32, take(64))
    onesf = P.sb("onesf", [128, 64], F32, take(256))
    wmixb = P.sb("wmixb", [128, 2, 128], BF16, take(512))
    assert o <= 6 * KB + 512, o
    o = 7 * KB
    R_HT = take(32 * KB)
    R_K = take(35328)
    R_Q = take(24 * KB)
    R_V = take(69 * 4 * 66 * 2)
    R_W = take(24 * KB)
    R_WS = take(8 * KB)
    R_ST = take(17 * KB)
    R_ZT = take(8 * KB)
    R_YP = take(8 * KB)
    R_PT = take(4 * KB)
    assert o <= ARENA_BYTES, o

    hT = P.sb("hT", [128, 8, 2048], BF16, R_HT)
    kT = [P.sb("kT0", [128, 2, 2176], BF16, R_K),
          P.sb("kT1", [128, 2, 2560], BF16, R_K + 8704),
          P.sb("kT2", [128, 2, 4096], BF16, R_K + 8704 + 10240)]
    KOFF = [1920, 1536, 0]
    qT = P.sb("qT", [128, 6, 2048], BF16, R_Q)
    vA = P.sb("vA", [128, 69, 4, 66], BF16, R_V)

    d_c = P.dsem("d_const")
    act.dma(gpre[:, :], gpre_d[:, :], d_c)
    act.dma(gffn[:, :], gffn_d[:, :], d_c)
    act.dma(pscale[:, :], pscale_d[:, :], d_c)
    act.dma(invw[:, :], invw_d[:, :], d_c)
    ev_const = act.dma(invc[:, :, :], invc_d[:, :, :], d_c)
    ev_ms1 = dve.op(lambda e: e.memset(nhalf[:, :], -0.5))
    ev_ms2 = dve.op(lambda e: e.memset(onesf[:, :], 1.0))
    ev_ms3 = dve.op(lambda e: e.memset(vA[:, 48:69, :, 64:66], 1.0))
    ev_ms4 = dve.op(lambda e: e.memset(wmixb[:, :, :], 0.0))
    pool.wait(ev_ms1)
    dve.wait(ev_const)

    wqk = P.sb("wqk", [128, 12, 8, 128], BF16, R_W)
    wvr = P.sb("wvr", [128, 3, 8, 256], BF16, R_YP)
    wu_t = [P.sb("wu%d" % i, [128, 8, 128], BF16, R_WS + i * 2 * KB) for i in range(2)]
    pq = []
    pq_sem = {}
    pq_ev = {}
    pq_left = {}

    def pq_add(group, out, in_, waits=()):
        if group not in pq_sem:
            pq_sem[group] = P.dsem("d_pq_" + group)
            pq_left[group] = 0
        pq_left[group] += 1
        pq.append((group, out, in_, waits))

    def pq_issue(n=1):
        for _ in range(n):
            if not pq:
                return
            group, out, in_, waits = pq.pop(0)
            pq_ev[group] = pool.dma(out, in_, pq_sem[group], waits=list(waits))
            pq_left[group] -= 1

    def pq_need(group):
        while pq_left[group] > 0:
            pq_issue(1)
        return pq_ev[group]

    pq_add("ident", ident[:, :], ident_d[:, :])
    pq_add("rotm", rotm[:, :], rotm_d[:, :])
    for c in (10, 11):
        pq_add("wk2", wqk[:, c, :, :], wk_d[c - 6])
    for g in range(3):
        pq_add("wv", wvr[:, g, :, :], wv_d[g])
    for c in (8, 9, 6, 7):
        pq_add("wk%d" % ((c - 6) // 2), wqk[:, c, :, :], wk_d[c - 6])
    for c in range(2):
        pq_add("wu", wu_t[c][:, :, :], wu_d[c])
    for c in range(6):
        pq_add("wq", wqk[:, c, :, :], wq_d[c])
    pq_add("masks", masks[:, :, :], masks_d[:, :, :])
    for g in range(4):
        c, hf = g // 2, g % 2
        pq_add("wmix", wmixb[64 * hf:64 * hf + 64, c, 64 * hf:64 * hf + 64], wmix_d[g, :, :], waits=[ev_ms4])
    pe.wait(pq_need("ident"))

    ss_col = [0]

    rstd_mode = {"act": False}

    def rstd_chain(ev_ss, col):
        if rstd_mode["act"]:
            e1 = act.op(lambda e: e.activation(out=msA[:, col:col + 1], in_=ssA[:, col:col + 1], func=AF.Sqrt, scale=1.0 / D, bias=EPS), waits=[ev_ss])
            return dve.op(lambda e: e.reciprocal(out=rsA[:, col:col + 1], in_=msA[:, col:col + 1]), waits=[e1])
        e1 = pool.op(lambda e: e.tensor_scalar(out=msA[:, col:col + 1], in0=ssA[:, col:col + 1], scalar1=1.0 / D, scalar2=EPS,
                                               op0=ALU.mult, op1=ALU.add), waits=[ev_ss])
        e2 = pool.op(lambda e: e.tensor_tensor(out=rsA[:, col:col + 1], in0=msA[:, col:col + 1], in1=nhalf[:, :], op=ALU.pow), waits=[e1])
        return e2

    NXB = 4
    NXB_MAX = 10
    xst = [P.sb("xst%d" % i, [128, D], F32, (R_ST + i * 4 * KB) if i < 2 else (R_ZT + (i - 2) * 4 * KB)) for i in range(NXB)]
    xst += [P.sb("xst%d" % (4 + i), [128, D], F32, R_HT + i * 4 * KB) for i in range(NXB_MAX - NXB)]
    xsb = [P.sb("xsb%d" % i, [128, D], BF16, R_ST + 8 * KB + i * 2 * KB) for i in range(2)]
    junks = [P.sb("junk%d" % i, [128, D], BF16, R_ST + 12 * KB + i * 2 * KB) for i in range(2)]
    junk_st = {"i": 0, "ev": [None, None]}

    def sq_accum(in_ap, col, waits, jl=None, st=None):
        jl = junks if jl is None else jl
        st = junk_st if st is None else st
        k = st["i"] % 2
        st["i"] += 1
        ev = act.op(lambda e: e.activation(out=jl[k][:, :], in_=in_ap, func=AF.Square, accum_out=ssA[:, col:col + 1]),
                    waits=list(waits) + [st["ev"][k]])
        st["ev"][k] = ev
        return ev
    d_x = [P.dsem("d_x%d" % i) for i in range(NXB_MAX)]
    xst_free = [None] * NXB_MAX
    pa_nxb = [NXB]
    xsb_free = [None, None]
    tr_ring = Ring([0, 1])
    pa_state = {"q": [], "n": 0, "lag": 1}

    def pa_stage2(pd):
        (xin, xb, col, ev_rs, gvec, dst, dst_col, b2, n) = pd
        if n % 2 == 0 or pa_state.get("all_act"):
            ev_xs = act.op(lambda e: e.activation(out=xsb[b2][:, :], in_=xin[:, :], func=AF.Copy, scale=rsA[:, col:col + 1]),
                           waits=[ev_rs, xsb_free[b2]])
        else:
            ev_xs = dve.op(lambda e: e.tensor_scalar(out=xsb[b2][:, :], in0=xin[:, :], scalar1=rsA[:, col:col + 1], scalar2=None, op0=ALU.mult),
                           waits=[ev_rs, xsb_free[b2]])
        xst_free[xb] = ev_xs
        k_, bank, fr = tr_ring.get()
        psb = ps[:, bank, :].bitcast(BF16).rearrange("p (k t) -> p k t", k=8)
        pe.wait(ev_xs, fr, bank_free[bank])
        ev_tr = None
        for k in range(8):
            ev_tr = pe.op(lambda e, k=k: e.transpose(out=psb[:, k, :], in_=xsb[b2][:, k * 128:(k + 1) * 128], identity=ident[:, :]),
                          signal=(k == 7))
        xsb_free[b2] = ev_tr
        ev_ev = dve.op(lambda e: e.tensor_tensor(out=dst[:, :, dst_col:dst_col + 128], in0=psb,
                                                 in1=gvec[:, :].unsqueeze(2).to_broadcast([128, 8, 128]), op=ALU.mult), waits=[ev_tr])
        tr_ring.release(k_, ev_ev)
        bank_free[bank] = ev_ev
        return ev_ev

    def pa_issue_load(src_rows, idx):
        xb = idx % pa_nxb[0]
        return sp.dma(xst[xb][:, :], src_rows, d_x[xb], waits=[xst_free[xb]])

    def norm_transpose_tile(ev_ld, gvec, dst, dst_col, idx):
        xb = idx % pa_nxb[0]
        xin = xst[xb]
        col = ss_col[0] % 48
        ss_col[0] += 1
        ev_sq = sq_accum(xin[:, :], col, [ev_ld])
        ev_rs = rstd_chain(ev_sq, col)
        n = pa_state["n"]
        pa_state["n"] += 1
        pa_state["q"].append((xin, xb, col, ev_rs, gvec, dst, dst_col, n % 2, n))
        while len(pa_state["q"]) > pa_state["lag"]:
            pa_stage2(pa_state["q"].pop(0))

    def pa_flush():
        while pa_state["q"]:
            pa_stage2(pa_state["q"].pop(0))

    def phase_a(src, ntiles, gvec, dst, between=None, pq_from=0, nxb=NXB, lag=1):
        pa_nxb[0] = nxb
        pa_state["lag"] = lag
        NXB = nxb
        lds = {}
        for i in range(min(NXB - lag, ntiles)):
            lds[i] = pa_issue_load(src[i * 128:(i + 1) * 128, :], i)
        for i in range(ntiles):
            if i >= pq_from:
                pq_issue(2 if len(pq) > 12 else 1)
            norm_transpose_tile(lds[i], gvec, dst, i * 128, i)
            j = i + NXB - lag
            if j < ntiles:
                lds[j] = pa_issue_load(src[j * 128:(j + 1) * 128, :], j)
            if between is not None:
                between(i)
        pa_flush()

    pj_ring = Ring([2, 3, 4, 5, 6, 7])

    def proj_fm(wfn, rhsfn, n, extra_waits=()):
        k_, bank, fr = pj_ring.get()
        pe.wait(fr, bank_free[bank], *extra_waits)
        ev = None
        for k in range(8):
            ev = pe.op(lambda e, k=k: e.matmul(ps[:, bank, 0:n], lhsT=wfn(k), rhs=rhsfn(k), start=(k == 0), stop=(k == 7)),
                       signal=(k == 7))
        return k_, bank, ev

    RS = R_V + 8 * KB
    cs = [[P.sb("cos%d" % i, [128, 512], F32, RS + 0 * KB + i * 2 * KB), P.sb("sin%d" % i, [128, 512], F32, RS + 4 * KB + i * 2 * KB)]
          for i in range(2)]
    qb_t = [P.sb("qb%d" % i, [128, 512], BF16, RS + 8 * KB + i * KB) for i in range(2)]
    t1_t = [P.sb("t1_%d" % i, [128, 512], F32, RS + 10 * KB + i * 2 * KB) for i in range(2)]
    assert RS + 14 * KB <= R_V + 25344
    t2_t = [P.sb("t2b_%d" % i, [128, 512], F32, R_WS + 4 * KB + i * 2 * KB) for i in range(2)]
    d_cs = [P.dsem("d_cs%d" % i) for i in range(2)]
    cs_free = [None, None]
    cs_i = [0]

    def load_tables(col0, n=512):
        b = cs_i[0] % 2
        cs_i[0] += 1
        sp.dma(cs[b][0][:, 0:n], cos_d[:, col0:col0 + n], d_cs[b], waits=[cs_free[b]])
        ev = sp.dma(cs[b][1][:, 0:n], sin_d[:, col0:col0 + n], d_cs[b])
        return b, ev

    def q_dst(c, b):
        g = c // 2
        if g == 0:
            return qT[:, c, b * 512:(b + 1) * 512]
        if g == 1:
            return (qT[:, c, :].rearrange("p (r L) -> p r L", r=4)[:, :, b * 128:(b + 1) * 128], 4)
        return (qT[:, c, :].rearrange("p (r L) -> p r L", r=16)[:, :, b * 32:(b + 1) * 32], 16)

    def k_dst(g, pr, e0, n):
        if g == 0:
            return kT[0][:, pr, e0 - KOFF[0]:e0 - KOFF[0] + n]
        if g == 1:
            j0 = (e0 - KOFF[1]) // 4
            return (kT[1][:, pr, :].rearrange("p (r L) -> p r L", r=4)[:, :, j0:j0 + n // 4], 4)
        a0 = e0 // 16
        return (kT[2][:, pr, :].rearrange("p (r L) -> p r L", r=16)[:, :, a0:a0 + n // 16], 16)

    rope_state = {"pending": None, "u": 0, "qb_free": [None, None], "t1_free": [None, None], "t2_free": [None, None]}

    def rope_finish(pd):
        (ub, n, bankA, kA, evA, ev_qb, csb, ev_cs, dst, users) = pd
        kB, bankB, frB = pj_ring.get()
        ev_rot = pe.op(lambda e: e.matmul(ps[:, bankB, 0:n], lhsT=rotm[:, :], rhs=qb_t[ub][:, 0:n], start=True, stop=True),
                       waits=[ev_qb, frB, bank_free[bankB]])
        rope_state["qb_free"][ub] = ev_rot
        ev_t1 = dve.op(lambda e: e.tensor_tensor(out=t1_t[ub][:, 0:n], in0=ps[:, bankA, 0:n], in1=cs[csb][0][:, 0:n], op=ALU.mult),
                       waits=[evA, ev_cs, rope_state["t1_free"][ub]])
        ev_t2 = dve.op(lambda e: e.tensor_tensor(out=t2_t[ub][:, 0:n], in0=ps[:, bankB, 0:n], in1=cs[csb][1][:, 0:n], op=ALU.mult),
                       waits=[ev_rot, rope_state["t2_free"][ub]])
        pj_ring.release(kA, ev_t1)
        bank_free[bankA] = ev_t1
        pj_ring.release(kB, ev_t2)
        bank_free[bankB] = ev_t2
        if isinstance(dst, tuple):
            dst_ap, rr = dst
            i0 = t1_t[ub][:, 0:n].rearrange("p (j r) -> p r j", r=rr)
            i1 = t2_t[ub][:, 0:n].rearrange("p (j r) -> p r j", r=rr)
            ev_o = pool.op(lambda e: e.tensor_tensor(out=dst_ap, in0=i0, in1=i1, op=ALU.add), waits=[ev_t1, ev_t2])
        else:
            ev_o = pool.op(lambda e: e.tensor_tensor(out=dst, in0=t1_t[ub][:, 0:n], in1=t2_t[ub][:, 0:n], op=ALU.add), waits=[ev_t1, ev_t2])
        rope_state["t1_free"][ub] = ev_o
        rope_state["t2_free"][ub] = ev_o
        users.append(ev_t2)
        return ev_o

    def rope_unit(wfn, rhsfn, n, csb, ev_cs, dst, users, extra_waits=()):
        u = rope_state["u"]
        rope_state["u"] += 1
        ub = u % 2
        kA, bankA, evA = proj_fm(wfn, rhsfn, n, extra_waits)
        if rope_state.get("qb_dve"):
            ev_qb = dve.op(lambda e: e.tensor_copy(out=qb_t[ub][:, 0:n], in_=ps[:, bankA, 0:n]), waits=[evA, rope_state["qb_free"][ub]])
        else:
            ev_qb = act.op(lambda e: e.activation(out=qb_t[ub][:, 0:n], in_=ps[:, bankA, 0:n], func=AF.Copy),
                           waits=[evA, rope_state["qb_free"][ub]])
        prev = rope_state["pending"]
        rope_state["pending"] = (ub, n, bankA, kA, [evA, ev_qb], ev_qb, csb, ev_cs, dst, users)
        if prev is not None:
            return rope_finish(prev)
        return None

    def rope_flush():
        prev = rope_state["pending"]
        rope_state["pending"] = None
        if prev is not None:
            return rope_finish(prev)
        return None

    hTh = P.sb("hTh", [128, 8, 2048], BF16, R_Q)
    phase_a(xh, HT // 128, gpre, hTh, pq_from=9, nxb=NXB_MAX, lag=2)
    pe.wait(dve.last())

    vflip = [0]

    def v_block(wv_tile, g, tok_ap_fn, blk, extra_waits=(), evac=None):
        k_, bank, fr = pj_ring.get()
        pe.wait(fr, bank_free[bank], *extra_waits)
        ev = None
        for k in range(8):
            ev = pe.op(lambda e, k=k: e.matmul(ps[:, bank, 0:256], lhsT=tok_ap_fn(k), rhs=wv_tile[:, g, k, :], start=(k == 0), stop=(k == 7)),
                       signal=(k == 7))
        src = ps[:, bank, 0:256].rearrange("p (h d) -> p h d", h=4)
        vflip[0] += 1
        if evac == "act" or (evac is None and vflip[0] % 2):
            ev2 = act.op(lambda e: e.activation(out=vA[:, blk, :, 0:64], in_=src, func=AF.Copy), waits=[ev])
        else:
            ev2 = dve.op(lambda e: e.tensor_copy(out=vA[:, blk, :, 0:64], in_=src), waits=[ev])
        pj_ring.release(k_, ev2)
        bank_free[bank] = ev2
        return ev2

    def blk_idx(g, s_, m):
        if g == 0:
            return 48 if m == 0 else m - 1
        if g == 1:
            return 49 + s_ if m == 0 else 16 + 4 * s_ + (m - 1)
        return 53 + s_ if m == 0 else 32 + s_

    users = []
    items = []
    blk_state = {}

    def it_tables(col0, n=512):
        def f():
            blk_state["cs"] = load_tables(col0, n)
        return f

    def it_rope(wfn, rhsfn, n, dst, ew):
        def f():
            csb, ev_cs = blk_state["cs"]
            rope_unit(wfn, rhsfn, n, csb, ev_cs, dst, users, extra_waits=ew())
        return f

    def it_endblk():
        def f():
            rope_flush()
            cs_free[blk_state["cs"][0]] = users[-1]
        return f

    for b in range(4):
        items.append(it_tables(b * 512))
        for pr in range(2):
            items.append(it_rope(lambda k, pr=pr: wqk[:, 10 + pr, k, :], lambda k, b=b: hTh[:, k, b * 512:(b + 1) * 512], 512,
                                 k_dst(2, pr, b * 512, 512), lambda: [pq_need("wk2"), pq_need("rotm")]))
        if b == 3:
            for pr in range(2):
                items.append(it_rope(lambda k, pr=pr: wqk[:, 8 + pr, k, :], lambda k: hTh[:, k, 1536:2048], 512,
                                     k_dst(1, pr, 1536, 512), lambda: [pq_need("wk1")]))
        items.append(it_endblk())
    items.append(it_tables(1920, 128))
    for pr in range(2):
        items.append(it_rope(lambda k, pr=pr: wqk[:, 6 + pr, k, :], lambda k: hTh[:, k, 1920:2048], 128, kT[0][:, pr, 0:128], lambda: [pq_need("wk0")]))
    items.append(it_endblk())
    items.append(lambda: v_block(wvr, 0, lambda k: hTh[:, k, 1920:2048], blk_idx(0, 0, 0), extra_waits=[pq_need("wv")], evac="dve"))
    for rho in range(4):
        items.append(lambda rho=rho: v_block(wvr, 1, lambda k, rho=rho: hTh[:, k, 1536 + rho:2048:4], blk_idx(1, rho, 0), extra_waits=[pq_need("wv")], evac="dve"))
    for r in range(16):
        items.append(lambda r=r: v_block(wvr, 2, lambda k, r=r: hTh[:, k, r:2048:16], blk_idx(2, r, 0), extra_waits=[pq_need("wv")], evac="dve"))

    def it_u(c):
        def f():
            k_, bank, ev = proj_fm(lambda k, c=c: wu_t[c][:, k, :], lambda k: hTh[:, k, 2032:2048], 16, extra_waits=[pq_need("wu")])
            ev2 = dve.op(lambda e, c=c, bank=bank: e.tensor_copy(out=uh[:, c, :], in_=ps[:, bank, 0:16]), waits=[ev])
            pj_ring.release(k_, ev2)
            bank_free[bank] = ev2
        return f
    for c in range(2):
        items.append(it_u(c))

    def between(i):
        for _ in range(3):
            if items:
                items.pop(0)()

    pa_state["all_act"] = True
    rope_state["qb_dve"] = True
    rstd_mode["act"] = True
    phase_a(xo, NT, gpre, hT, between=between)
    while items:
        items.pop(0)()
    pa_state["all_act"] = False
    rope_state["qb_dve"] = False
    rstd_mode["act"] = False
    P.barrier()

    users = []
    for b in range(4):
        csb, ev_cs = load_tables(HT + b * 512)
        for c in range(12):
            if c < 6:
                dst = q_dst(c, b)
            else:
                dst = k_dst((c - 6) // 2, (c - 6) % 2, HT + b * 512, 512)
            rope_unit(lambda k, c=c: wqk[:, c, k, :], lambda k, b=b: hT[:, k, b * 512:(b + 1) * 512], 512, csb, ev_cs, dst, users,
                      extra_waits=[pq_need("wq")])
        rope_flush()
        cs_free[csb] = users[-1]
    P.barrier()

    ev_ones = dve.op(lambda e: e.memset(vA[:, 0:48, :, 64:66], 1.0))
    X = P.sb("X", [128, 2, 2064], F32, R_W)
    ev_uhc = []
    for c in range(2):
        ev_uhc.append(dve.op(lambda e, c=c: e.tensor_copy(out=X[:, c, 0:16], in_=uh[:, c, :])))
        for b in range(4):
            k_, bank, ev = proj_fm(lambda k, c=c: wu_t[c][:, k, :], lambda k, b=b: hT[:, k, b * 512:(b + 1) * 512], 512)
            ev2 = act.op(lambda e, c=c, b=b, bank=bank: e.activation(out=X[:, c, 16 + b * 512:16 + (b + 1) * 512], in_=ps[:, bank, 0:512], func=AF.Copy),
                         waits=[ev])
            pj_ring.release(k_, ev2)
            bank_free[bank] = ev2
    ev_X = act.last()
    Y = P.sb("Y", [128, 2064], F32, R_ST)
    Z = P.sb("Z", [128, 2064], F32, R_ST + 8256)
    zT = P.sb("zT", [128, 2, 2048], BF16, R_ZT)
    ypool = P.sb("ypool", [128, 2, 2048], BF16, R_YP)
    L = 2064
    pev = [ev_X] + ev_uhc
    engs2 = [dve, pool]
    for c in range(2):
        eng = engs2[c]
        e1 = eng.op(lambda e, c=c: e.tensor_tensor(out=Y[:, 1:L], in0=X[:, c, 1:L], in1=X[:, c, 0:L - 1], op=ALU.add), waits=pev)
        if c == 0:
            e2 = eng.op(lambda e: e.tensor_tensor(out=Z[64:128, 3:L], in0=Y[64:128, 3:L], in1=Y[64:128, 1:L - 2], op=ALU.add), waits=[e1])
            fin = [(0, 64, Y), (64, 128, Z)]
            elast = e2
        else:
            e2 = eng.op(lambda e: e.tensor_tensor(out=Z[:, 3:L], in0=Y[:, 3:L], in1=Y[:, 1:L - 2], op=ALU.add), waits=[e1])
            e3 = eng.op(lambda e: e.tensor_tensor(out=Y[:, 7:L], in0=Z[:, 7:L], in1=Z[:, 3:L - 4], op=ALU.add), waits=[e2])
            e4 = eng.op(lambda e: e.tensor_tensor(out=Z[64:128, 15:L], in0=Y[64:128, 15:L], in1=Y[64:128, 7:L - 8], op=ALU.add), waits=[e3])
            fin = [(0, 64, Y), (64, 128, Z)]
            elast = e4
        evz = []
        for (p0, p1, Sx) in fin:
            ez = dve.op(lambda e, c=c, p0=p0, p1=p1, Sx=Sx: e.scalar_tensor_tensor(out=zT[p0:p1, c, :], in0=Sx[p0:p1, 16:L], scalar=invw[p0:p1, c:c + 1],
                                                                                     in1=X[p0:p1, c, 16:L], op0=ALU.mult, op1=ALU.subtract),
                        waits=[elast] + pev)
            ez1 = dve.op(lambda e, c=c, p0=p0, p1=p1, Sx=Sx: e.tensor_tensor(out=Sx[p0:p1, 16:32], in0=Sx[p0:p1, 16:32], in1=invc[p0:p1, c, :], op=ALU.mult),
                         waits=[ez])
            ez2 = dve.op(lambda e, c=c, p0=p0, p1=p1, Sx=Sx: e.tensor_tensor(out=zT[p0:p1, c, 0:16], in0=Sx[p0:p1, 16:32], in1=X[p0:p1, c, 16:32], op=ALU.subtract),
                         waits=[ez1])
            evz.append(ez2)
        pev = evz
    for n in range(16):
        v_block(wvr, 0, lambda k, n=n: hT[:, k, n * 128:(n + 1) * 128], blk_idx(0, 0, 1 + n), evac="act")
    for rho in range(4):
        for n1 in range(4):
            v_block(wvr, 1, lambda k, rho=rho, n1=n1: hT[:, k, 512 * n1 + rho:512 * (n1 + 1):4], blk_idx(1, rho, 1 + n1), evac="act")
    for r in range(16):
        v_block(wvr, 2, lambda k, r=r: hT[:, k, r:2048:16], blk_idx(2, r, 1), evac="act")
    ev_z = pev
    for c in range(2):
        for b in range(4):
            k_, bank, fr = pj_ring.get()
            ev = pe.op(lambda e, c=c, b=b, bank=bank: e.matmul(ps[:, bank, 0:512], lhsT=wmixb[:, c, :], rhs=zT[:, c, b * 512:(b + 1) * 512], start=True, stop=True),
                       waits=[fr, bank_free[bank], pq_need("wmix")] + ev_z)
            ev2 = act.op(lambda e, c=c, b=b, bank=bank: e.activation(out=ypool[:, c, b * 512:(b + 1) * 512], in_=ps[:, bank, 0:512], func=AF.Copy,
                                                                      scale=pscale[:, c:c + 1]), waits=[ev])
            pj_ring.release(k_, ev2)
            bank_free[bank] = ev2
    P.barrier()

    acc = P.sb("acc", [128, 4, 2048], F32, R_W)
    PTc = [P.sb("PTc%d" % i, [128, 512], BF16, R_ST + i * KB) for i in range(3)]
    PTp = [P.sb("PTp%d" % i, [128, 512], BF16, R_ST + 4 * KB + i * KB) for i in range(3)]
    pt_free = [None, None, None]
    s_ring = Ring([(0, 1), (2, 3), (4, 5)])
    o_ring = Ring([6, 7])
    MC, MP, MH, MHP = 0, 1, 2, 3

    def qcols(g, s_, slot):
        if g == 0:
            n = 4 * s_ + slot
            return slice(n * 128, (n + 1) * 128)
        if g == 1:
            return slice(s_ * 512 + 128 * slot, s_ * 512 + 128 * (slot + 1))
        r = 4 * s_ + slot
        return slice(r * 128, (r + 1) * 128)

    def kcols(g, s_, slot, prev):
        if g == 0:
            n = 4 * s_ + slot
            kb = n + (0 if prev else 1)
            return slice(kb * 128, (kb + 1) * 128), blk_idx(0, 0, kb)
        if g == 1:
            m = slot + (0 if prev else 1)
            return slice(s_ * 640 + 128 * m, s_ * 640 + 128 * (m + 1)), blk_idx(1, s_, m)
        r = 4 * s_ + slot
        m = 0 if prev else 1
        return slice(r * 256 + 128 * m, r * 256 + 128 * (m + 1)), blk_idx(2, r, m)

    def acc_dst(g, s_, h):
        if g == 0:
            return acc[0:65, h, s_ * 512:(s_ + 1) * 512], None
        if g == 1:
            return acc[0:65, h, s_:2048:4], None
        a3 = acc[0:65, h, :].rearrange("p (a r) -> p r a", r=16)[:, 4 * s_:4 * s_ + 4, :]
        return a3, "p (r a) -> p r a"

    att_q = []
    acc_g0_ev = [None]
    dve.wait(pq_need("masks"))

    def att_finish(pd):
        (g, s_, h, ub, ev_mc, ev_mp) = pd
        ko, bankO, frO = o_ring.get()
        pe.wait(frO, bank_free[bankO], ev_mc, ev_mp)
        ev = None
        for slot in range(4):
            _, bp = kcols(g, s_, slot, True)
            _, bc = kcols(g, s_, slot, False)
            pe.op(lambda e, slot=slot, bp=bp: e.matmul(ps[0:65, bankO, slot * 128:(slot + 1) * 128], lhsT=vA[:, bp, h, 0:65],
                                                       rhs=PTp[ub][:, slot * 128:(slot + 1) * 128], start=True, stop=False), signal=False)
            ev = pe.op(lambda e, slot=slot, bc=bc: e.matmul(ps[0:65, bankO, slot * 128:(slot + 1) * 128], lhsT=vA[:, bc, h, 0:65],
                                                            rhs=PTc[ub][:, slot * 128:(slot + 1) * 128], start=False, stop=True), signal=(slot == 3))
        pt_free[ub] = ev
        dst, rr = acc_dst(g, s_, h)
        src = ps[0:65, bankO, :]
        if rr is not None:
            src = src.rearrange(rr, r=4)
        if g == 0:
            ev2 = act.op(lambda e: e.activation(out=dst, in_=src, func=AF.Copy), waits=[ev])
            acc_g0_ev[0] = ev2
        else:
            ev2 = dve.op(lambda e: e.tensor_tensor(out=dst, in0=src, in1=dst, op=ALU.add), waits=[ev, acc_g0_ev[0]])
        o_ring.release(ko, ev2)
        bank_free[bankO] = ev2
        return ev2

    att_u = 0
    for g in range(3):
        for s_ in range(4):
            for h in range(4):
                ub = att_u % 3
                att_u += 1
                pr, hf = h // 2, h % 2
                p0 = 64 * hf
                ks, (bC, bP), frS = s_ring.get()
                pe.wait(frS, bank_free[bC], bank_free[bP])
                evs = None
                for slot in range(4):
                    qs = qcols(g, s_, slot)
                    kc, _ = kcols(g, s_, slot, False)
                    kp, _ = kcols(g, s_, slot, True)
                    pe.op(lambda e, slot=slot, qs=qs, kc=kc, g=g, p0=p0, pr=pr, bC=bC: e.matmul(ps[:, bC, slot * 128:(slot + 1) * 128], lhsT=kT[g][p0:p0 + 64, pr, kc],
                                                                      rhs=qT[p0:p0 + 64, 2 * g + pr, qs], start=True, stop=True), signal=False)
                    evs = pe.op(lambda e, slot=slot, qs=qs, kp=kp, g=g, p0=p0, pr=pr, bP=bP: e.matmul(ps[:, bP, slot * 128:(slot + 1) * 128], lhsT=kT[g][p0:p0 + 64, pr, kp],
                                                                            rhs=qT[p0:p0 + 64, 2 * g + pr, qs], start=True, stop=True), signal=(slot == 3))
                ev_ec = act.op(lambda e, ub=ub, bC=bC: e.activation(out=PTc[ub][:, :], in_=ps[:, bC, :], func=AF.Exp, scale=0.125),
                               waits=[evs, pt_free[ub]])
                ev_ep = act.op(lambda e, ub=ub, bP=bP: e.activation(out=PTp[ub][:, :], in_=ps[:, bP, :], func=AF.Exp, scale=0.125))
                s_ring.release(ks, ev_ep)
                bank_free[bC] = ev_ep
                bank_free[bP] = ev_ep
                mp = MH if g == 2 else (MHP if (g == 1 or s_ == 0) else MP)
                ev_mc = dve.op(lambda e, ub=ub: e.tensor_tensor(out=PTc[ub][:, :], in0=PTc[ub][:, :], in1=masks[:, MC, :], op=ALU.mult), waits=[ev_ec])
                ev_mp = dve.op(lambda e, ub=ub, mp=mp: e.tensor_tensor(out=PTp[ub][:, :], in0=PTp[ub][:, :], in1=masks[:, mp, :], op=ALU.mult), waits=[ev_ep])
                att_q.append((g, s_, h, ub, ev_mc, ev_mp))
                while len(att_q) > 2:
                    att_finish(att_q.pop(0))
    while att_q:
        att_finish(att_q.pop(0))
    P.barrier()

    tap("hT", hT[:, :, :])
    tap("qT", qT[:, :, :])
    tap("kT0", kT[0][:, :, :])
    tap("kT1", kT[1][:, :, :])
    tap("kT2", kT[2][:, :, :])
    tap("vA", vA[:, :, :, :])
    tap("acc", acc[0:65, :, :])
    tap("ypool", ypool[:, :, :])
    if stop_after == "attn":
        return finish()
    wpa = P.sb("wpa", [128, 2, D], BF16, R_V)
    selb = P.sb("selb", [64, 2, 128], BF16, R_PT + 256)
    oT2 = P.sb("oT2", [128, 2, 2048], BF16, R_Q + 16 * KB)
    wpp = P.sb("wpp", [128, 2, D], BF16, R_V + 8 * KB)
    wgs = [P.sb("wgs%d" % i, [128, 2, 8, 128], BF16, R_V + 12 * KB + i * 4 * KB) for i in range(2)]
    d_wm2 = P.dsem("d_wm2")
    pool.dma(wpa[:, :, :], wpa_d[:, :, :], d_wm2)
    pool.dma(selb[:, :, :], sel_d[:, :, :], d_wm2)
    ev_wp = pool.dma(wpp[:, :, :], wpp_d[:, :, :], d_wm2)
    d_wg = [P.dsem("d_wg%d" % i) for i in range(2)]
    ev_wg_next = pool.dma(wgs[0][:, :, :, :], wg_d[0], d_wg[0])
    oT = P.sb("oT", [128, 4, 2048], BF16, R_Q)
    lnrow1 = P.sb("lnrow", [128, 2048], F32, R_ST)
    rrow = [P.sb("rrow0", [128, 2048], F32, R_ST + 8 * KB)] + \
           [P.sb("rrow%d" % (i + 1), [128, 2048], F32, R_K + i * 8 * KB) for i in range(3)]
    hirow = [P.sb("hirow%d" % i, [128, 2048], BF16, R_K + 24 * KB + i * 4 * KB) for i in range(2)]
    lorow = [P.sb("lorow0", [128, 2048], BF16, R_V + 4 * KB), P.sb("lorow1", [128, 2048], BF16, R_V + 28 * KB)]
    onesb = P.sb("onesb", [128, 64], BF16, R_PT)
    ev_ob = dve.op(lambda e: e.memset(onesb[:, :], 1.0))
    row_free = [None, None]
    lo_ev = [None, None]
    ex_prev = [None]
    oT_ev = {}
    rows_ev = {}

    def norm_rows(h):
        hb = h % 2
        ev_ln = act.op(lambda e: e.activation(out=lnrow1[64:65, :], in_=acc[64:65, h, :], func=AF.Ln), waits=[ex_prev[0]])
        ev_ex = act.op(lambda e: e.activation(out=rrow[h][64:65, :], in_=lnrow1[64:65, :], func=AF.Exp, scale=-1.0), waits=[ev_ln])
        ex_prev[0] = ev_ex
        ev_hi = dve.op(lambda e: e.tensor_copy(out=hirow[hb][64:65, :], in_=rrow[h][64:65, :]), waits=[ev_ex, row_free[hb]])
        ev_lo = pool.op(lambda e: e.tensor_tensor(out=lorow[hb][64:65, :], in0=rrow[h][64:65, :], in1=hirow[hb][64:65, :], op=ALU.subtract),
                        waits=[ev_hi, row_free[hb]])
        rows_ev[h] = (ev_hi, ev_lo)

    def norm_apply(h):
        hb = h % 2
        ev_hi, ev_lo = rows_ev[h]
        evu = None
        for b in range(4):
            k_, bank, fr = pj_ring.get()
            pe.op(lambda e, b=b, bank=bank: e.matmul(ps[0:64, bank, 0:512], lhsT=onesb[64:65, 0:64], rhs=hirow[hb][64:65, b * 512:(b + 1) * 512],
                                                     start=True, stop=False), waits=[fr, bank_free[bank], ev_hi, ev_lo, ev_ob], signal=False)
            ev = pe.op(lambda e, b=b, bank=bank: e.matmul(ps[0:64, bank, 0:512], lhsT=onesb[64:65, 0:64], rhs=lorow[hb][64:65, b * 512:(b + 1) * 512],
                                                          start=False, stop=True))
            ev2 = dve.op(lambda e, b=b, bank=bank: e.tensor_tensor(out=oT[0:64, h, b * 512:(b + 1) * 512], in0=ps[0:64, bank, 0:512],
                                                                   in1=acc[0:64, h, b * 512:(b + 1) * 512], op=ALU.mult), waits=[ev])
            pj_ring.release(k_, ev2)
            bank_free[bank] = ev2
            evu = ev
            oT_ev[(h, b)] = ev2
        row_free[hb] = evu
        if h % 2 == 1:
            p_ = h // 2
            for b in range(4):
                k_, bank, fr = pj_ring.get()
                pe.op(lambda e, b=b, bank=bank: e.matmul(ps[:, bank, 0:512], lhsT=selb[0:64, 0, :], rhs=oT[0:64, 2 * p_, b * 512:(b + 1) * 512],
                                                         start=True, stop=False),
                      waits=[fr, bank_free[bank], ev_wp, oT_ev[(2 * p_, b)], oT_ev[(2 * p_ + 1, b)]], signal=False)
                ev = pe.op(lambda e, b=b, bank=bank: e.matmul(ps[:, bank, 0:512], lhsT=selb[0:64, 1, :], rhs=oT[0:64, 2 * p_ + 1, b * 512:(b + 1) * 512],
                                                              start=False, stop=True))
                ev2 = dve.op(lambda e, b=b, bank=bank: e.tensor_copy(out=oT2[:, p_, b * 512:(b + 1) * 512], in_=ps[:, bank, 0:512]), waits=[ev])
                pj_ring.release(k_, ev2)
                bank_free[bank] = ev2

    norm_rows(0)
    for h in range(4):
        if h + 1 < 4:
            norm_rows(h + 1)
        norm_apply(h)
    P.barrier()

    tap("oT", oT[0:64, :, :])
    if stop_after == "norm":
        return finish()

    mT = P.sb("mT", [128, 8, 2048], BF16, R_K)
    sga = [P.sb("sga%d" % i, [128, 512], F32, R_V + 20 * KB + i * 2 * KB) for i in range(2)]
    sgp = [P.sb("sgp%d" % i, [128, 512], F32, R_V + 24 * KB + i * 2 * KB) for i in range(2)]
    m1 = [P.sb("m1_%d" % i, [128, 512], F32, R_ST + i * 2 * KB) for i in range(2)]
    m2 = [P.sb("m2_%d" % i, [128, 512], F32, R_ST + 4 * KB + i * 2 * KB) for i in range(2)]
    wout = P.sb("wout", [128, 8, D], BF16, R_W)
    gpost = P.sb("gpost", [128, D], F32, R_W + 16 * KB)
    wg_free = [None, None]
    d_wo = P.dsem("d_wo")
    mring = Ring([(0, 1, 2, 3), (4, 5, 6, 7)])
    sg_free = [None, None]
    m_free = [None, None]
    mu = 0
    for c in range(8):
        wb = c % 2
        ev_wg = ev_wg_next
        if c + 1 < 8:
            ev_wg_next = pool.dma(wgs[(c + 1) % 2][:, :, :, :], wg_d[c + 1], d_wg[(c + 1) % 2], waits=[wg_free[(c + 1) % 2]])
        if c == 1:
            for k in range(8):
                ev_wo = pool.dma(wout[:, k, :], wout_d[:, k, :], d_wo)
            ev_gp = sp.dma(gpost[:, :], gpost_d[:, :], P.dsem("d_gpost"))
        for b in range(4):
            ub = mu % 2
            mu += 1
            km, (b1, b2, b3, b4), frm = mring.get()
            tok = slice(b * 512, (b + 1) * 512)
            pe.wait(frm, bank_free[b1], bank_free[b2], bank_free[b3], bank_free[b4], ev_wg, ev_wp)
            for k in range(8):
                e1 = pe.op(lambda e, k=k, wb=wb, b1=b1, tok=tok: e.matmul(ps[:, b1, :], lhsT=wgs[wb][:, 0, k, :], rhs=hT[:, k, tok], start=(k == 0), stop=(k == 7)),
                           signal=(k == 7))
            for k in range(8):
                e2 = pe.op(lambda e, k=k, wb=wb, b2=b2, tok=tok: e.matmul(ps[:, b2, :], lhsT=wgs[wb][:, 1, k, :], rhs=hT[:, k, tok], start=(k == 0), stop=(k == 7)),
                           signal=(k == 7))
            for p_ in range(2):
                e3 = pe.op(lambda e, p_=p_, c=c, b3=b3, tok=tok: e.matmul(ps[:, b3, :], lhsT=wpa[:, p_, c * 128:(c + 1) * 128], rhs=oT2[:, p_, tok],
                                                                          start=(p_ == 0), stop=(p_ == 1)), signal=(p_ == 1))
            for j in range(2):
                e4 = pe.op(lambda e, j=j, c=c, b4=b4, tok=tok: e.matmul(ps[:, b4, :], lhsT=wpp[:, j, c * 128:(c + 1) * 128], rhs=ypool[:, j, tok],
                                                                        start=(j == 0), stop=(j == 1)), signal=(j == 1))
            if b == 3:
                wg_free[wb] = e2
            ea = act.op(lambda e, ub=ub, b1=b1: e.activation(out=sga[ub][:, :], in_=ps[:, b1, :], func=AF.Sigmoid), waits=[e1, sg_free[ub]])
            eb = act.op(lambda e, ub=ub, b2=b2: e.activation(out=sgp[ub][:, :], in_=ps[:, b2, :], func=AF.Sigmoid), waits=[e2])
            em1 = dve.op(lambda e, ub=ub, b3=b3: e.tensor_tensor(out=m1[ub][:, :], in0=ps[:, b3, :], in1=sga[ub][:, :], op=ALU.mult), waits=[ea, e3, m_free[ub]])
            em2 = dve.op(lambda e, ub=ub, b4=b4: e.tensor_tensor(out=m2[ub][:, :], in0=ps[:, b4, :], in1=sgp[ub][:, :], op=ALU.mult), waits=[eb, e4])
            sg_free[ub] = em2
            mring.release(km, em2)
            for bb in (b1, b2, b3, b4):
                bank_free[bb] = em2
            eo = pool.op(lambda e, ub=ub, c=c, tok=tok: e.tensor_tensor(out=mT[:, c, tok], in0=m1[ub][:, :], in1=m2[ub][:, :], op=ALU.add), waits=[em1, em2])
            m_free[ub] = eo
    P.barrier()

    tap("mT", mT[:, :, :])
    if stop_after == "merge":
        return finish()
    wdn = P.sb("wdn", [128, NJ, D], BF16, R_Q + 10 * KB)
    assert R_Q + 54 * KB <= R_W and R_K + 44 * KB <= R_Q + 10 * KB
    gfin = P.sb("gfin", [128, D], F32, R_W + 20 * KB)
    xin = [P.sb("xin%d" % i, [128, D], F32, R_ST + i * 4 * KB) for i in range(4)]
    x1b = [P.sb("x1b0", [128, D], F32, R_PT), P.sb("x1b1", [128, D], F32, R_WS), P.sb("x1b2", [128, D], F32, R_WS + 4 * KB)]
    tb = [P.sb("tb%d" % i, [128, D], F32, R_ZT + i * 4 * KB) for i in range(2)] + [P.sb("tb2", [128, D], F32, R_Q + 54 * KB)]
    assert R_Q + 58 * KB <= R_W
    xs2 = [P.sb("xs2_%d" % i, [128, D], BF16, R_YP + i * 2 * KB) for i in range(2)]
    junks2 = [P.sb("junk2_%d" % i, [128, D], BF16, R_YP + 4 * KB + i * 2 * KB) for i in range(2)]
    junk_st2 = {"i": 0, "ev": [None, None]}
    d_xi = [P.dsem("d_xi%d" % i) for i in range(4)]
    d_x1o = [P.dsem("d_x1o%d" % i) for i in range(3)]
    xin_free = [None, None, None, None]
    x1b_free = [None, None, None]
    tb_free = [None, None, None]
    xs2_free = [None, None]
    pair_ring = Ring([(2, 3), (4, 5), (6, 7)])
    tr_ring2 = Ring([0, 1])

    def pair_ap(pr_):
        return ps[:, pr_[0]:pr_[0] + 2, :].rearrange("p a b -> p (a b)")

    x1_store_ev = {}

    def wout_stage1(i):
        b = i % 4
        ev_x = sp.dma(xin[b][:, :], xo[i * 128:(i + 1) * 128, :], d_xi[b], waits=[xin_free[b]])
        kp, pr_, frp = pair_ring.get()
        pe.wait(frp, bank_free[pr_[0]], bank_free[pr_[1]], ev_wo)
        ev = None
        for hf in range(2):
            for k in range(8):
                ev = pe.op(lambda e, k=k, hf=hf, pr_=pr_: e.matmul(ps[:, pr_[0] + hf, :], lhsT=mT[:, k, i * 128:(i + 1) * 128], rhs=wout[:, k, hf * 512:(hf + 1) * 512],
                                                                  start=(k == 0), stop=(k == 7)), signal=(k == 7 and hf == 1))
        return (i, b, kp, pr_, ev, ev_x)

    def norm_part(pr_, ev_mm, col, gvec, ev_g, b):
        yap = pair_ap(pr_)
        ev_sq = sq_accum(yap, col, [ev_mm], junks2, junk_st2)
        ev_rs = rstd_chain(ev_sq, col)
        ev_t = dve.op(lambda e: e.scalar_tensor_tensor(out=tb[b][:, :], in0=yap, scalar=rsA[:, col:col + 1], in1=gvec[:, :], op0=ALU.mult, op1=ALU.mult),
                      waits=[ev_rs, tb_free[b], ev_g])
        return ev_t

    def add_part(ev_t, xres, ev_xres, outbuf, out_free, b):
        ev_o = pool.op(lambda e: e.tensor_tensor(out=outbuf[:, :], in0=tb[b][:, :], in1=xres[:, :], op=ALU.add), waits=[ev_t, ev_xres, out_free])
        tb_free[b] = ev_o
        return ev_o

    def norm_res(pr_, ev_mm, col, gvec, ev_g, xres, ev_xres, outbuf, out_free, b):
        ev_t = norm_part(pr_, ev_mm, col, gvec, ev_g, b)
        ev_o = add_part(ev_t, xres, ev_xres, outbuf, out_free, b)
        return ev_t, ev_o

    def wout_stage2a(st):
        (i, b, kp, pr_, ev_mm, ev_x) = st
        col = ss_col[0] % 48
        ss_col[0] += 1
        ev_t = norm_part(pr_, ev_mm, col, gpost, ev_gp, i % 3)
        pair_ring.release(kp, ev_t)
        bank_free[pr_[0]] = ev_t
        bank_free[pr_[1]] = ev_t
        return (i, b, ev_t, ev_x)

    def wout_stage2add(st):
        (i, b, ev_t, ev_x) = st
        xb = i % 3
        ev_o = add_part(ev_t, xin[b], ev_x, x1b[xb], x1b_free[xb], i % 3)
        xin_free[b] = ev_o
        ev_st = sp.dma(x1s_d[i * 128:(i + 1) * 128, :], x1b[xb][:, :], d_x1o[xb], waits=[ev_o])
        x1_store_ev[i] = ev_st
        return (i, xb, ev_o, ev_st)

    def wout_stage2b1(st):
        (i, b, ev_o, ev_st) = st
        col2 = ss_col[0] % 48
        ss_col[0] += 1
        ev_sq = sq_accum(x1b[b][:, :], col2, [ev_o], junks2, junk_st2)
        ev_rs = rstd_chain(ev_sq, col2)
        return (i, b, ev_st, col2, ev_rs)

    def wout_stage2b2(st):
        (i, b, ev_st, col2, ev_rs) = st
        b2 = i % 2
        ev_xs = act.op(lambda e: e.activation(out=xs2[b2][:, :], in_=x1b[b][:, :], func=AF.Copy, scale=rsA[:, col2:col2 + 1]), waits=[ev_rs, xs2_free[b2]])
        x1b_free[b] = [ev_xs, ev_st]
        k_, bank, fr = tr_ring2.get()
        psb = ps[:, bank, :].bitcast(BF16).rearrange("p (k t) -> p k t", k=8)
        pe.wait(ev_xs, fr, bank_free[bank])
        ev_tr = None
        for k in range(8):
            ev_tr = pe.op(lambda e, k=k: e.transpose(out=psb[:, k, :], in_=xs2[b2][:, k * 128:(k + 1) * 128], identity=ident[:, :]), signal=(k == 7))
        xs2_free[b2] = ev_tr
        ev_ev = dve.op(lambda e: e.tensor_tensor(out=hT[:, :, i * 128:(i + 1) * 128], in0=psb, in1=gffn[:, :].unsqueeze(2).to_broadcast([128, 8, 128]), op=ALU.mult),
                       waits=[ev_tr])
        tr_ring2.release(k_, ev_ev)
        bank_free[bank] = ev_ev

    wgu_boot = [P.sb("wgub%d" % i, [128, 2, 8, 128], BF16, R_Q + i * 4 * KB) for i in range(2)]
    d_gub = [P.dsem("d_gub%d" % i) for i in range(2)]
    boot_ev = [pool.dma(wgu_boot[i][:, :, :, :], wgu_d[i], d_gub[i]) for i in range(2)]
    d_wd = P.dsem("d_wd")
    ev_gf = sp.dma(gfin[:, :], gfin_d[:, :], P.dsem("d_gfin"))
    st1, st2, st3, st4 = {}, {}, {}, {}
    for it in range(NT + 4):
        if it < NT:
            st1[it] = wout_stage1(it)
        if 0 <= it - 3 < NT:
            st3[it - 3] = wout_stage2add(st2[it - 3])
        if 0 <= it - 4 < NT:
            st4[it - 4] = wout_stage2b1(st3[it - 4])
        if 0 <= it - 1 < NT:
            st2[it - 1] = wout_stage2a(st1[it - 1])
        if 0 <= it - 4 < NT:
            wout_stage2b2(st4[it - 4])
    P.barrier()

    tap("h2T", hT[:, :, :])
    if stop_after == "wout":
        return finish()
    ffT = P.sb("ffT", [128, NJ, 1024], BF16, R_K)
    wgu = [P.sb("wgu%d" % i, [128, 2, 8, 128], BF16, R_W + i * 4 * KB) for i in range(3)]
    sa = [P.sb("sa%d" % i, [128, 512], F32, R_W + 12 * KB + i * 2 * KB) for i in range(2)]
    d_gu = [P.dsem("d_gu%d" % i) for i in range(3)]
    gu_free = [None, None, None]
    sa_free = [None, None]
    d_x1i = [P.dsem("d_x1i%d" % i) for i in range(2)]
    d_out = [P.dsem("d_out%d" % i) for i in range(2)]
    obuf = x1b
    obuf_free = [x1b_free[0], x1b_free[1]]
    tb_free = [pool.last(), pool.last(), pool.last()]
    ab_ring = Ring([(0, 1), (2, 3)])
    pair_ring2 = Ring([(4, 5), (6, 7)])
    out_evs = []
    fu = 0
    gl = 0
    loads = [(hf, j) for hf in range(2) for j in range(NJ)]
    ld_ev = {}

    def issue_gu(idx):
        hf, j = loads[idx]
        b = idx % 3
        ld_ev[idx] = pool.dma(wgu[b][:, :, :, :], wgu_d[j], d_gu[b], waits=[gu_free[b]])

    ld_ev[0], ld_ev[1] = boot_ev

    def wgu_buf(idx):
        return wgu_boot[idx] if idx < 2 else wgu[idx % 3]
    for hf in range(2):
        for j in range(NJ):
            idx = hf * NJ + j
            if idx + 2 < len(loads):
                issue_gu(idx + 2)
            if hf == 0:
                ev_wd = pool.dma(wdn[:, j, :], wdn_d[:, j, :], d_wd)
            wb = idx % 3
            wbuf = wgu_buf(idx)
            for b in range(2):
                ub = fu % 2
                fu += 1
                tok = slice(hf * 1024 + b * 512, hf * 1024 + (b + 1) * 512)
                ka, (ba, bb), fra = ab_ring.get()
                pe.wait(fra, bank_free[ba], bank_free[bb], ld_ev[idx])
                for k in range(8):
                    e1 = pe.op(lambda e, k=k, wbuf=wbuf, ba=ba, tok=tok: e.matmul(ps[:, ba, :], lhsT=wbuf[:, 0, k, :], rhs=hT[:, k, tok], start=(k == 0), stop=(k == 7)),
                               signal=(k == 7))
                for k in range(8):
                    e2 = pe.op(lambda e, k=k, wbuf=wbuf, bb=bb, tok=tok: e.matmul(ps[:, bb, :], lhsT=wbuf[:, 1, k, :], rhs=hT[:, k, tok], start=(k == 0), stop=(k == 7)),
                               signal=(k == 7))
                if b == 1:
                    gu_free[wb] = e2
                es = act.op(lambda e, ub=ub, ba=ba: e.activation(out=sa[ub][:, :], in_=ps[:, ba, :], func=AF.Silu), waits=[e1, sa_free[ub]])
                ef = dve.op(lambda e, ub=ub, bb=bb, j=j, b=b: e.tensor_tensor(out=ffT[:, j, b * 512:(b + 1) * 512], in0=ps[:, bb, :], in1=sa[ub][:, :], op=ALU.mult),
                            waits=[es, e2])
                sa_free[ub] = ef
                ab_ring.release(ka, ef)
                bank_free[ba] = ef
                bank_free[bb] = ef
        ev_ff = dve.last()
        for il in range(8):
            i = hf * 8 + il
            b = i % 2
            ev_x1 = sp.dma(xin[b][:, :], x1s_d[i * 128:(i + 1) * 128, :], d_x1i[b], waits=[xin_free[b], x1_store_ev[i]])
            kp, pr_, frp = pair_ring2.get()
            pe.wait(frp, bank_free[pr_[0]], bank_free[pr_[1]], ev_wd, ev_ff)
            ev = None
            for h2 in range(2):
                for j in range(NJ):
                    ev = pe.op(lambda e, j=j, h2=h2, pr_=pr_, il=il: e.matmul(ps[:, pr_[0] + h2, :], lhsT=ffT[:, j, il * 128:(il + 1) * 128],
                                                                               rhs=wdn[:, j, h2 * 512:(h2 + 1) * 512], start=(j == 0), stop=(j == NJ - 1)),
                               signal=(j == NJ - 1 and h2 == 1))
            col = ss_col[0] % 48
            ss_col[0] += 1
            ev_t, ev_o = norm_res(pr_, ev, col, gfin, ev_gf, xin[b], ev_x1, obuf[b], obuf_free[b], b)
            pair_ring2.release(kp, ev_t)
            bank_free[pr_[0]] = ev_t
            bank_free[pr_[1]] = ev_t
            xin_free[b] = ev_o
            ev_out = sp.dma(out_d[i * 128:(i + 1) * 128, :], obuf[b][:, :], d_out[b], waits=[ev_o])
            obuf_free[b] = ev_out
            out_evs.append(ev_out)
        dve.wait(pe.last())
    sp.wait(out_evs[-1], out_evs[-2])
    for e_ in P.cengs:
        e_.wait(out_evs[-1], out_evs[-2])
    P.run()
    pscm.__exit__(None, None, None)
    return nc


def _rope_tables():
    half = 32
    inv_freq = 10000.0 ** (-(np.arange(half, dtype=np.float64) * 2.0 / 64.0))
    pos = np.arange(-HT, S, dtype=np.float64)
    ang = pos[:, None] * inv_freq[None, :]
    cos = np.cos(ang).astype(np.float32)
    sin = np.sin(ang).astype(np.float32)
    d = np.arange(128) % 64
    i = d % 32
    sgn = np.where(d < 32, -1.0, 1.0).astype(np.float32)
    C = np.ascontiguousarray(cos[:, i].T)
    Sn = np.ascontiguousarray((sin[:, i] * sgn[None, :]).T)
    return C, Sn


_PROG_CACHE = {}


def _host_inputs(x, g_pre_mix, w_in, w_pool_mix, pool_scale, w_proj_attn, w_proj_pool, w_out,
                 g_post_mix, g_pre_ffn, w_gate_up, w_down, g_post_ffn):
    f = np.float32
    x = np.asarray(x, f).reshape(S, D)
    w_in = np.asarray(w_in, f)[0]
    def chunks(wcols, n):
        nc_ = wcols.shape[1] // n
        return np.ascontiguousarray(wcols.reshape(8, 128, nc_, n).transpose(2, 1, 0, 3))
    wq = chunks(w_in[:, 0:768], 128)
    wk = chunks(w_in[:, 768:1536], 128)
    wv = chunks(w_in[:, 1536:2304], 256)
    wu = chunks(w_in[:, 2304:2560], 128)
    wga = chunks(w_in[:, 2560:3584], 128)
    wgp = chunks(w_in[:, 3584:4608], 128)
    wg = np.ascontiguousarray(np.stack([wga, wgp], axis=2))
    wmix = np.ascontiguousarray(np.asarray(w_pool_mix, f)[0])
    wpa = np.ascontiguousarray(np.asarray(w_proj_attn, f)[0].reshape(2, 128, D).transpose(1, 0, 2))
    sel = np.zeros((64, 2, 128), f)
    sel[np.arange(64), 0, np.arange(64)] = 1.0
    sel[np.arange(64), 1, np.arange(64) + 64] = 1.0
    wpp = np.ascontiguousarray(np.asarray(w_proj_pool, f)[0].reshape(2, 128, D).transpose(1, 0, 2))
    wout = np.ascontiguousarray(np.asarray(w_out, f)[0].reshape(8, 128, D).transpose(1, 0, 2))
    wgu_full = np.asarray(w_gate_up, f)[0]
    wa = chunks(wgu_full[:, 0:FF], 128)
    wb = chunks(wgu_full[:, FF:2 * FF], 128)
    wgu = np.ascontiguousarray(np.stack([wa, wb], axis=2))
    wdn = np.ascontiguousarray(np.asarray(w_down, f)[0].reshape(NJ, 128, D).transpose(1, 0, 2))
    def fm(g):
        return np.ascontiguousarray(np.asarray(g, f).reshape(8, 128).T)
    gpre = fm(g_pre_mix[0])
    gffn = fm(g_pre_ffn[0])
    gpost = np.ascontiguousarray(np.broadcast_to(np.asarray(g_post_mix, f)[0][None, :], (128, D)))
    gfin = np.ascontiguousarray(np.broadcast_to(np.asarray(g_post_ffn, f)[0][None, :], (128, D)))
    pscale = np.ascontiguousarray(np.asarray(pool_scale, f)[0].reshape(2, 128).T)
    wins = np.array([2, 4, 8, 16], dtype=f)
    wpart = wins[(np.arange(256) // 64)]
    invw = np.ascontiguousarray((1.0 / wpart).astype(f).reshape(2, 128).T)
    C, Sn = _rope_tables()
    ident = np.eye(128, dtype=f)
    rotm = np.zeros((128, 128), f)
    for dd in range(128):
        base = (dd // 64) * 64
        rotm[base + ((dd % 64) + 32) % 64, dd] = 1.0
    kk = np.arange(128)[:, None]
    qq = np.arange(128)[None, :]
    mC = (kk <= qq).astype(f)
    mP = (kk >= qq).astype(f)
    common = dict(wq=wq, wk=wk, wv=wv, wu=wu, wg=wg, wmix=wmix, wpa=wpa, sel=sel, wpp=wpp, wout=wout, wgu=wgu, wdn=wdn,
                  gpre=gpre, gffn=gffn, gpost=gpost, gfin=gfin, pscale=pscale, invw=invw, ident=ident, rotm=rotm)
    in_maps = []
    for c in range(NCORES):
        mH = mP if c > 0 else np.zeros_like(mP)
        masks = np.zeros((128, 4, 512), f)
        masks[:, 0] = np.tile(mC, (1, 4))
        masks[:, 1] = np.tile(mP, (1, 4))
        masks[:, 2] = np.tile(mH, (1, 4))
        masks[:, 3] = np.concatenate([mH, mP, mP, mP], axis=1)
        xo = x[c * T:(c + 1) * T]
        xh_ = x[c * T - HT:c * T] if c > 0 else np.zeros((HT, D), f)
        tpos = np.arange(16, dtype=f) + c * T
        cnt = np.minimum(tpos[None, :] + 1.0, wpart[:, None])
        invc = np.ascontiguousarray((1.0 / cnt).astype(f).reshape(2, 128, 16).transpose(1, 0, 2))
        m = dict(common)
        m.update(xh=np.ascontiguousarray(xh_), xo=np.ascontiguousarray(xo), masks=masks, invc=invc,
                 cost=np.ascontiguousarray(C[:, c * T:c * T + HT + T]), sint=np.ascontiguousarray(Sn[:, c * T:c * T + HT + T]))
        in_maps.append(m)
    return in_maps


def kernel(x, g_pre_mix, w_in, w_pool_mix, pool_scale, w_proj_attn, w_proj_pool, w_out,
           g_post_mix, g_pre_ffn, w_gate_up, w_down, g_post_ffn):
    in_maps = _host_inputs(x, g_pre_mix, w_in, w_pool_mix, pool_scale, w_proj_attn, w_proj_pool, w_out,
                           g_post_mix, g_pre_ffn, w_gate_up, w_down, g_post_ffn)
    nc = build_program()
    res = run_bass_kernel_spmd(nc, in_maps, core_ids=list(range(NCORES)))
    out = np.concatenate([np.asarray(r["out"], np.float32) for r in res.results], axis=0)
    return out.reshape(1, S, D)
```

```python
import numpy as np
import concourse.bass as bass
import concourse.mybir as mybir
from concourse.bass_utils import run_bass_kernel_spmd

F32 = mybir.dt.float32
BF16 = mybir.dt.bfloat16
AF = mybir.ActivationFunctionType
ALU = mybir.AluOpType

NCORES = 8
S = 16384
D = 1024
T = S // NCORES
NT = T // 128
HT = 2048
FF = 2816
NJ = FF // 128
EPS = 1e-6
KB = 1024

DEBUG_TAPS = False


class Ev:
    __slots__ = ("sem", "val")

    def __init__(self, sem, val):
        self.sem = sem
        self.val = val


class Eng:
    def __init__(self, name):
        self.name = name
        self.ops = []
        self.meta = []
        self.sem = None
        self.count = 0
        self.seen = {}

    def wait(self, *evs):
        for ev in evs:
            if ev is None:
                continue
            if isinstance(ev, (list, tuple)):
                self.wait(*ev)
                continue
            key = id(ev.sem)
            if self.seen.get(key, 0) >= ev.val:
                continue
            self.seen[key] = ev.val
            sem, val = ev.sem, ev.val
            self.meta.append(("wait", id(sem), val))
            self.ops.append(lambda e, sem=sem, val=val: e.wait_ge(sem, val))

    def op(self, fn, waits=(), signal=True):
        self.wait(*waits)
        if signal:
            self.count += 1
            sem, val = self.sem, self.count
            self.meta.append(("inc", id(sem), 1))
            self.ops.append(lambda e, fn=fn, sem=sem: fn(e).then_inc(sem, 1))
            return Ev(sem, val)
        self.ops.append(lambda e, fn=fn: fn(e))
        return None

    def last(self):
        return Ev(self.sem, self.count) if self.count else None

    def dma(self, out, in_, dsem, waits=(), **kw):
        self.wait(*waits)
        dsem.count += 16
        sem = dsem.sem
        self.meta.append(("inc", id(sem), 16))
        self.ops.append(lambda e, out=out, in_=in_, sem=sem, kw=kw: e.dma_start(out=out, in_=in_, **kw).then_inc(sem, 16))
        return Ev(sem, dsem.count)


class DSem:
    def __init__(self, sem):
        self.sem = sem
        self.count = 0


class Prog:
    def __init__(self, nc):
        self.nc = nc
        self.pe = Eng("tensor")
        self.act = Eng("scalar")
        self.dve = Eng("vector")
        self.pool = Eng("gpsimd")
        self.sp = Eng("sync")
        self.cengs = [self.pe, self.act, self.dve, self.pool]
        self.engs = self.cengs + [self.sp]
        self._ctx = []
        for e in self.engs:
            e.sem = self.new_sem("p_" + e.name)
        self.nsb = 0

    def new_sem(self, name):
        cm = self.nc.semaphore(name)
        h = cm.__enter__()
        self._ctx.append(cm)
        return h

    def dsem(self, name):
        return DSem(self.new_sem(name))

    def sb(self, name, shape, dtype, off):
        esz = 2 if dtype == BF16 else 4
        n = 1
        for s_ in shape[1:]:
            n *= s_
        assert off % 32 == 0, (name, off)
        assert off + n * esz <= ARENA_BYTES, (name, off, n * esz)
        self.nsb += 1
        return self.nc.alloc_sbuf_tensor_at("%s_%d" % (name, self.nsb), list(shape), dtype, offset=ARENA_BASE + off)

    def barrier(self):
        evs = [e.last() for e in self.cengs]
        for e in self.engs:
            e.wait(*evs)

    def check_deadlock(self):
        semv = {}
        pos = {e.name: 0 for e in self.engs}
        progress = True
        while progress:
            progress = False
            for e in self.engs:
                while pos[e.name] < len(e.meta):
                    kind, sid, val = e.meta[pos[e.name]]
                    if kind == "wait":
                        if semv.get(sid, 0) >= val:
                            pos[e.name] += 1
                            progress = True
                        else:
                            break
                    else:
                        semv[sid] = semv.get(sid, 0) + val
                        pos[e.name] += 1
                        progress = True
        stuck = {e.name: (pos[e.name], len(e.meta), e.meta[pos[e.name]]) for e in self.engs if pos[e.name] < len(e.meta)}
        if stuck:
            raise RuntimeError("static deadlock: %r" % (stuck,))

    def run(self):
        self.check_deadlock()
        nc = self.nc
        with nc.Block() as block:
            @block.tensor
            def _(e):
                for f in self.pe.ops:
                    f(e)

            @block.scalar
            def _(e):
                for f in self.act.ops:
                    f(e)

            @block.vector
            def _(e):
                for f in self.dve.ops:
                    f(e)

            @block.gpsimd
            def _(e):
                for f in self.pool.ops:
                    f(e)

            @block.sync
            def _(e):
                for f in self.sp.ops:
                    f(e)
        for cm in reversed(self._ctx):
            cm.__exit__(None, None, None)


ARENA_BASE = 18432
ARENA_BYTES = 229376 - ARENA_BASE


class Ring:
    def __init__(self, items):
        self.items = list(items)
        self.free = [None] * len(self.items)
        self.i = 0

    def get(self):
        k = self.i % len(self.items)
        self.i += 1
        return k, self.items[k], self.free[k]

    def release(self, k, ev):
        self.free[k] = ev


def build_program(taps=(), stop_after=None):
    nc = bass.Bass("TRN2", target_bir_lowering=False)

    def din(name, shape, dt=F32):
        return nc.dram_tensor(name, list(shape), dt, kind="ExternalInput").ap()

    xh = din("xh", [HT, D])
    xo = din("xo", [T, D])
    wq_d = din("wq", [6, 128, 8, 128])
    wk_d = din("wk", [6, 128, 8, 128])
    wv_d = din("wv", [3, 128, 8, 256])
    wu_d = din("wu", [2, 128, 8, 128])
    wg_d = din("wg", [8, 128, 2, 8, 128])
    wmix_d = din("wmix", [4, 64, 64])
    wpa_d = din("wpa", [128, 2, D])
    sel_d = din("sel", [64, 2, 128])
    wpp_d = din("wpp", [128, 2, D])
    wout_d = din("wout", [128, 8, D])
    wgu_d = din("wgu", [NJ, 128, 2, 8, 128])
    wdn_d = din("wdn", [128, NJ, D])
    gpre_d = din("gpre", [128, 8])
    gffn_d = din("gffn", [128, 8])
    gpost_d = din("gpost", [128, D])
    gfin_d = din("gfin", [128, D])
    pscale_d = din("pscale", [128, 2])
    invw_d = din("invw", [128, 2])
    invc_d = din("invc", [128, 2, 16])
    cos_d = din("cost", [128, HT + T])
    sin_d = din("sint", [128, HT + T])
    ident_d = din("ident", [128, 128])
    rotm_d = din("rotm", [128, 128])
    masks_d = din("masks", [128, 4, 512])
    out_d = nc.dram_tensor("out", [T, D], F32, kind="ExternalOutput").ap()
    x1s_d = nc.dram_tensor("x1s", [T, D], F32, kind="Internal").ap()
    tap_d = {}
    for name, shape, dt in taps:
        tap_d[name] = nc.dram_tensor("tap_" + name, list(shape), dt, kind="ExternalOutput").ap()

    P = Prog(nc)
    pe, act, dve, pool, sp = P.pe, P.act, P.dve, P.pool, P.sp
    d_tap = P.dsem("d_tap")

    def tap(name, ap):
        if name not in tap_d:
            return
        P.barrier()
        nd = len(tap_d[name].shape)
        idx = tuple(slice(None) for _ in range(nd))
        evt = sp.dma(tap_d[name][idx], ap, d_tap)
        for e_ in P.engs:
            e_.wait(evt)

    def finish():
        P.barrier()
        P.run()
        pscm.__exit__(None, None, None)
        return nc
    pscm = nc.psum_tensor("ps", [128, 8, 512], F32)
    ps = pscm.__enter__()
    bank_free = [None] * 8

    o = 0

    def take(nbytes):
        nonlocal o
        r = o
        o += (nbytes + 63) // 64 * 64
        return r

    ident = P.sb("ident", [128, 128], BF16, take(256))
    rotm = P.sb("rotm", [128, 128], BF16, take(256))
    masks = P.sb("masks", [128, 4, 512], BF16, take(4096))
    gpre = P.sb("gpre", [128, 8], F32, take(32))
    gffn = P.sb("gffn", [128, 8], F32, take(32))
    pscale = P.sb("pscale", [128, 2], F32, take(8))
    invw = P.sb("invw", [128, 2], F32, take(8))
    invc = P.sb("invc", [128, 2, 16], F32, take(128))
    uh = P.sb("uh", [128, 2, 16], F32, take(128))
    ssA = P.sb("ssA", [128, 48], F32, take(192))
    msA = P.sb("msA", [128, 48], F32, take(192))
    rsA = P.sb("rsA", [128, 48], F32, take(192))
    nhalf = P.sb("nhalf", [128, 1], F32, take(64))
    onesf = P.sb("onesf", [128, 64], F32, take(256))
    wmixb = P.sb("wmixb", [128, 2, 128], BF16, take(512))
    assert o <= 6 * KB + 512, o
    o = 7 * KB
    R_HT = take(32 * KB)
    R_K = take(35328)
    R_Q = take(24 * KB)
    R_V = take(69 * 4 * 66 * 2)
    R_W = take(24 * KB)
    R_WS = take(8 * KB)
    R_ST = take(17 * KB)
    R_ZT = take(8 * KB)
    R_YP = take(8 * KB)
    R_PT = take(4 * KB)
    assert o <= ARENA_BYTES, o

    hT = P.sb("hT", [128, 8, 2048], BF16, R_HT)
    kT = [P.sb("kT0", [128, 2, 2176], BF16, R_K),
          P.sb("kT1", [128, 2, 2560], BF16, R_K + 8704),
          P.sb("kT2", [128, 2, 4096], BF16, R_K + 8704 + 10240)]
    KOFF = [1920, 1536, 0]
    qT = P.sb("qT", [128, 6, 2048], BF16, R_Q)
    vA = P.sb("vA", [128, 69, 4, 66], BF16, R_V)

    d_c = P.dsem("d_const")
    act.dma(gpre[:, :], gpre_d[:, :], d_c)
    act.dma(gffn[:, :], gffn_d[:, :], d_c)
    act.dma(pscale[:, :], pscale_d[:, :], d_c)
    act.dma(invw[:, :], invw_d[:, :], d_c)
    ev_const = act.dma(invc[:, :, :], invc_d[:, :, :], d_c)
    ev_ms1 = dve.op(lambda e: e.memset(nhalf[:, :], -0.5))
    ev_ms2 = dve.op(lambda e: e.memset(onesf[:, :], 1.0))
    ev_ms3 = dve.op(lambda e: e.memset(vA[:, 48:69, :, 64:66], 1.0))
    ev_ms4 = dve.op(lambda e: e.memset(wmixb[:, :, :], 0.0))
    pool.wait(ev_ms1)
    dve.wait(ev_const)

    wqk = P.sb("wqk", [128, 12, 8, 128], BF16, R_W)
    wvr = P.sb("wvr", [128, 3, 8, 256], BF16, R_YP)
    wu_t = [P.sb("wu%d" % i, [128, 8, 128], BF16, R_WS + i * 2 * KB) for i in range(2)]
    pq = []
    pq_sem = {}
    pq_ev = {}
    pq_left = {}

    def pq_add(group, out, in_, waits=()):
        if group not in pq_sem:
            pq_sem[group] = P.dsem("d_pq_" + group)
            pq_left[group] = 0
        pq_left[group] += 1
        pq.append((group, out, in_, waits))

    def pq_issue(n=1):
        for _ in range(n):
            if not pq:
                return
            group, out, in_, waits = pq.pop(0)
            pq_ev[group] = pool.dma(out, in_, pq_sem[group], waits=list(waits))
            pq_left[group] -= 1

    def pq_need(group):
        while pq_left[group] > 0:
            pq_issue(1)
        return pq_ev[group]

    pq_add("ident", ident[:, :], ident_d[:, :])
    pq_add("rotm", rotm[:, :], rotm_d[:, :])
    for c in (10, 11):
        pq_add("wk2", wqk[:, c, :, :], wk_d[c - 6])
    for g in range(3):
        pq_add("wv", wvr[:, g, :, :], wv_d[g])
    for c in (8, 9, 6, 7):
        pq_add("wk%d" % ((c - 6) // 2), wqk[:, c, :, :], wk_d[c - 6])
    for c in range(2):
        pq_add("wu", wu_t[c][:, :, :], wu_d[c])
    for c in range(6):
        pq_add("wq", wqk[:, c, :, :], wq_d[c])
    pq_add("masks", masks[:, :, :], masks_d[:, :, :])
    for g in range(4):
        c, hf = g // 2, g % 2
        pq_add("wmix", wmixb[64 * hf:64 * hf + 64, c, 64 * hf:64 * hf + 64], wmix_d[g, :, :], waits=[ev_ms4])
    pe.wait(pq_need("ident"))

    ss_col = [0]

    rstd_mode = {"act": False}

    def rstd_chain(ev_ss, col):
        if rstd_mode["act"]:
            e1 = act.op(lambda e: e.activation(out=msA[:, col:col + 1], in_=ssA[:, col:col + 1], func=AF.Sqrt, scale=1.0 / D, bias=EPS), waits=[ev_ss])
            return dve.op(lambda e: e.reciprocal(out=rsA[:, col:col + 1], in_=msA[:, col:col + 1]), waits=[e1])
        e1 = pool.op(lambda e: e.tensor_scalar(out=msA[:, col:col + 1], in0=ssA[:, col:col + 1], scalar1=1.0 / D, scalar2=EPS,
                                               op0=ALU.mult, op1=ALU.add), waits=[ev_ss])
        e2 = pool.op(lambda e: e.tensor_tensor(out=rsA[:, col:col + 1], in0=msA[:, col:col + 1], in1=nhalf[:, :], op=ALU.pow), waits=[e1])
        return e2

    NXB = 4
    NXB_MAX = 10
    xst = [P.sb("xst%d" % i, [128, D], F32, (R_ST + i * 4 * KB) if i < 2 else (R_ZT + (i - 2) * 4 * KB)) for i in range(NXB)]
    xst += [P.sb("xst%d" % (4 + i), [128, D], F32, R_HT + i * 4 * KB) for i in range(NXB_MAX - NXB)]
    xsb = [P.sb("xsb%d" % i, [128, D], BF16, R_ST + 8 * KB + i * 2 * KB) for i in range(2)]
    junks = [P.sb("junk%d" % i, [128, D], BF16, R_ST + 12 * KB + i * 2 * KB) for i in range(2)]
    junk_st = {"i": 0, "ev": [None, None]}

    def sq_accum(in_ap, col, waits, jl=None, st=None):
        jl = junks if jl is None else jl
        st = junk_st if st is None else st
        k = st["i"] % 2
        st["i"] += 1
        ev = act.op(lambda e: e.activation(out=jl[k][:, :], in_=in_ap, func=AF.Square, accum_out=ssA[:, col:col + 1]),
                    waits=list(waits) + [st["ev"][k]])
        st["ev"][k] = ev
        return ev
    d_x = [P.dsem("d_x%d" % i) for i in range(NXB_MAX)]
    xst_free = [None] * NXB_MAX
    pa_nxb = [NXB]
    xsb_free = [None, None]
    tr_ring = Ring([0, 1])
    pa_state = {"q": [], "n": 0, "lag": 1}

    def pa_stage2(pd):
        (xin, xb, col, ev_rs, gvec, dst, dst_col, b2, n) = pd
        if n % 2 == 0 or pa_state.get("all_act"):
            ev_xs = act.op(lambda e: e.activation(out=xsb[b2][:, :], in_=xin[:, :], func=AF.Copy, scale=rsA[:, col:col + 1]),
                           waits=[ev_rs, xsb_free[b2]])
        else:
            ev_xs = dve.op(lambda e: e.tensor_scalar(out=xsb[b2][:, :], in0=xin[:, :], scalar1=rsA[:, col:col + 1], scalar2=None, op0=ALU.mult),
                           waits=[ev_rs, xsb_free[b2]])
        xst_free[xb] = ev_xs
        k_, bank, fr = tr_ring.get()
        psb = ps[:, bank, :].bitcast(BF16).rearrange("p (k t) -> p k t", k=8)
        pe.wait(ev_xs, fr, bank_free[bank])
        ev_tr = None
        for k in range(8):
            ev_tr = pe.op(lambda e, k=k: e.transpose(out=psb[:, k, :], in_=xsb[b2][:, k * 128:(k + 1) * 128], identity=ident[:, :]),
                          signal=(k == 7))
        xsb_free[b2] = ev_tr
        ev_ev = dve.op(lambda e: e.tensor_tensor(out=dst[:, :, dst_col:dst_col + 128], in0=psb,
                                                 in1=gvec[:, :].unsqueeze(2).to_broadcast([128, 8, 128]), op=ALU.mult), waits=[ev_tr])
        tr_ring.release(k_, ev_ev)
        bank_free[bank] = ev_ev
        return ev_ev

    def pa_issue_load(src_rows, idx):
        xb = idx % pa_nxb[0]
        return sp.dma(xst[xb][:, :], src_rows, d_x[xb], waits=[xst_free[xb]])

    def norm_transpose_tile(ev_ld, gvec, dst, dst_col, idx):
        xb = idx % pa_nxb[0]
        xin = xst[xb]
        col = ss_col[0] % 48
        ss_col[0] += 1
        ev_sq = sq_accum(xin[:, :], col, [ev_ld])
        ev_rs = rstd_chain(ev_sq, col)
        n = pa_state["n"]
        pa_state["n"] += 1
        pa_state["q"].append((xin, xb, col, ev_rs, gvec, dst, dst_col, n % 2, n))
        while len(pa_state["q"]) > pa_state["lag"]:
            pa_stage2(pa_state["q"].pop(0))

    def pa_flush():
        while pa_state["q"]:
            pa_stage2(pa_state["q"].pop(0))

    def phase_a(src, ntiles, gvec, dst, between=None, pq_from=0, nxb=NXB, lag=1):
        pa_nxb[0] = nxb
        pa_state["lag"] = lag
        NXB = nxb
        lds = {}
        for i in range(min(NXB - lag, ntiles)):
            lds[i] = pa_issue_load(src[i * 128:(i + 1) * 128, :], i)
        for i in range(ntiles):
            if i >= pq_from:
                pq_issue(2 if len(pq) > 12 else 1)
            norm_transpose_tile(lds[i], gvec, dst, i * 128, i)
            j = i + NXB - lag
            if j < ntiles:
                lds[j] = pa_issue_load(src[j * 128:(j + 1) * 128, :], j)
            if between is not None:
                between(i)
        pa_flush()

    pj_ring = Ring([2, 3, 4, 5, 6, 7])

    def proj_fm(wfn, rhsfn, n, extra_waits=()):
        k_, bank, fr = pj_ring.get()
        pe.wait(fr, bank_free[bank], *extra_waits)
        ev = None
        for k in range(8):
            ev = pe.op(lambda e, k=k: e.matmul(ps[:, bank, 0:n], lhsT=wfn(k), rhs=rhsfn(k), start=(k == 0), stop=(k == 7)),
                       signal=(k == 7))
        return k_, bank, ev

    RS = R_V + 8 * KB
    cs = [[P.sb("cos%d" % i, [128, 512], F32, RS + 0 * KB + i * 2 * KB), P.sb("sin%d" % i, [128, 512], F32, RS + 4 * KB + i * 2 * KB)]
          for i in range(2)]
    qb_t = [P.sb("qb%d" % i, [128, 512], BF16, RS + 8 * KB + i * KB) for i in range(2)]
    t1_t = [P.sb("t1_%d" % i, [128, 512], F32, RS + 10 * KB + i * 2 * KB) for i in range(2)]
    assert RS + 14 * KB <= R_V + 25344
    t2_t = [P.sb("t2b_%d" % i, [128, 512], F32, R_WS + 4 * KB + i * 2 * KB) for i in range(2)]
    d_cs = [P.dsem("d_cs%d" % i) for i in range(2)]
    cs_free = [None, None]
    cs_i = [0]

    def load_tables(col0, n=512):
        b = cs_i[0] % 2
        cs_i[0] += 1
        sp.dma(cs[b][0][:, 0:n], cos_d[:, col0:col0 + n], d_cs[b], waits=[cs_free[b]])
        ev = sp.dma(cs[b][1][:, 0:n], sin_d[:, col0:col0 + n], d_cs[b])
        return b, ev

    def q_dst(c, b):
        g = c // 2
        if g == 0:
            return qT[:, c, b * 512:(b + 1) * 512]
        if g == 1:
            return (qT[:, c, :].rearrange("p (r L) -> p r L", r=4)[:, :, b * 128:(b + 1) * 128], 4)
        return (qT[:, c, :].rearrange("p (r L) -> p r L", r=16)[:, :, b * 32:(b + 1) * 32], 16)

    def k_dst(g, pr, e0, n):
        if g == 0:
            return kT[0][:, pr, e0 - KOFF[0]:e0 - KOFF[0] + n]
        if g == 1:
            j0 = (e0 - KOFF[1]) // 4
            return (kT[1][:, pr, :].rearrange("p (r L) -> p r L", r=4)[:, :, j0:j0 + n // 4], 4)
        a0 = e0 // 16
        return (kT[2][:, pr, :].rearrange("p (r L) -> p r L", r=16)[:, :, a0:a0 + n // 16], 16)

    rope_state = {"pending": None, "u": 0, "qb_free": [None, None], "t1_free": [None, None], "t2_free": [None, None]}

    def rope_finish(pd):
        (ub, n, bankA, kA, evA, ev_qb, csb, ev_cs, dst, users) = pd
        kB, bankB, frB = pj_ring.get()
        ev_rot = pe.op(lambda e: e.matmul(ps[:, bankB, 0:n], lhsT=rotm[:, :], rhs=qb_t[ub][:, 0:n], start=True, stop=True),
                       waits=[ev_qb, frB, bank_free[bankB]])
        rope_state["qb_free"][ub] = ev_rot
        ev_t1 = dve.op(lambda e: e.tensor_tensor(out=t1_t[ub][:, 0:n], in0=ps[:, bankA, 0:n], in1=cs[csb][0][:, 0:n], op=ALU.mult),
                       waits=[evA, ev_cs, rope_state["t1_free"][ub]])
        ev_t2 = dve.op(lambda e: e.tensor_tensor(out=t2_t[ub][:, 0:n], in0=ps[:, bankB, 0:n], in1=cs[csb][1][:, 0:n], op=ALU.mult),
                       waits=[ev_rot, rope_state["t2_free"][ub]])
        pj_ring.release(kA, ev_t1)
        bank_free[bankA] = ev_t1
        pj_ring.release(kB, ev_t2)
        bank_free[bankB] = ev_t2
        if isinstance(dst, tuple):
            dst_ap, rr = dst
            i0 = t1_t[ub][:, 0:n].rearrange("p (j r) -> p r j", r=rr)
            i1 = t2_t[ub][:, 0:n].rearrange("p (j r) -> p r j", r=rr)
            ev_o = pool.op(lambda e: e.tensor_tensor(out=dst_ap, in0=i0, in1=i1, op=ALU.add), waits=[ev_t1, ev_t2])
        else:
            ev_o = pool.op(lambda e: e.tensor_tensor(out=dst, in0=t1_t[ub][:, 0:n], in1=t2_t[ub][:, 0:n], op=ALU.add), waits=[ev_t1, ev_t2])
        rope_state["t1_free"][ub] = ev_o
        rope_state["t2_free"][ub] = ev_o
        users.append(ev_t2)
        return ev_o

    def rope_unit(wfn, rhsfn, n, csb, ev_cs, dst, users, extra_waits=()):
        u = rope_state["u"]
        rope_state["u"] += 1
        ub = u % 2
        kA, bankA, evA = proj_fm(wfn, rhsfn, n, extra_waits)
        if rope_state.get("qb_dve"):
            ev_qb = dve.op(lambda e: e.tensor_copy(out=qb_t[ub][:, 0:n], in_=ps[:, bankA, 0:n]), waits=[evA, rope_state["qb_free"][ub]])
        else:
            ev_qb = act.op(lambda e: e.activation(out=qb_t[ub][:, 0:n], in_=ps[:, bankA, 0:n], func=AF.Copy),
                           waits=[evA, rope_state["qb_free"][ub]])
        prev = rope_state["pending"]
        rope_state["pending"] = (ub, n, bankA, kA, [evA, ev_qb], ev_qb, csb, ev_cs, dst, users)
        if prev is not None:
            return rope_finish(prev)
        return None

    def rope_flush():
        prev = rope_state["pending"]
        rope_state["pending"] = None
        if prev is not None:
            return rope_finish(prev)
        return None

    hTh = P.sb("hTh", [128, 8, 2048], BF16, R_Q)
    phase_a(xh, HT // 128, gpre, hTh, pq_from=9, nxb=NXB_MAX, lag=2)
    pe.wait(dve.last())

    vflip = [0]

    def v_block(wv_tile, g, tok_ap_fn, blk, extra_waits=(), evac=None):
        k_, bank, fr = pj_ring.get()
        pe.wait(fr, bank_free[bank], *extra_waits)
        ev = None
        for k in range(8):
            ev = pe.op(lambda e, k=k: e.matmul(ps[:, bank, 0:256], lhsT=tok_ap_fn(k), rhs=wv_tile[:, g, k, :], start=(k == 0), stop=(k == 7)),
                       signal=(k == 7))
        src = ps[:, bank, 0:256].rearrange("p (h d) -> p h d", h=4)
        vflip[0] += 1
        if evac == "act" or (evac is None and vflip[0] % 2):
            ev2 = act.op(lambda e: e.activation(out=vA[:, blk, :, 0:64], in_=src, func=AF.Copy), waits=[ev])
        else:
            ev2 = dve.op(lambda e: e.tensor_copy(out=vA[:, blk, :, 0:64], in_=src), waits=[ev])
        pj_ring.release(k_, ev2)
        bank_free[bank] = ev2
        return ev2

    def blk_idx(g, s_, m):
        if g == 0:
            return 48 if m == 0 else m - 1
        if g == 1:
            return 49 + s_ if m == 0 else 16 + 4 * s_ + (m - 1)
        return 53 + s_ if m == 0 else 32 + s_

    users = []
    items = []
    blk_state = {}

    def it_tables(col0, n=512):
        def f():
            blk_state["cs"] = load_tables(col0, n)
        return f

    def it_rope(wfn, rhsfn, n, dst, ew):
        def f():
            csb, ev_cs = blk_state["cs"]
            rope_unit(wfn, rhsfn, n, csb, ev_cs, dst, users, extra_waits=ew())
        return f

    def it_endblk():
        def f():
            rope_flush()
            cs_free[blk_state["cs"][0]] = users[-1]
        return f

    for b in range(4):
        items.append(it_tables(b * 512))
        for pr in range(2):
            items.append(it_rope(lambda k, pr=pr: wqk[:, 10 + pr, k, :], lambda k, b=b: hTh[:, k, b * 512:(b + 1) * 512], 512,
                                 k_dst(2, pr, b * 512, 512), lambda: [pq_need("wk2"), pq_need("rotm")]))
        if b == 3:
            for pr in range(2):
                items.append(it_rope(lambda k, pr=pr: wqk[:, 8 + pr, k, :], lambda k: hTh[:, k, 1536:2048], 512,
                                     k_dst(1, pr, 1536, 512), lambda: [pq_need("wk1")]))
        items.append(it_endblk())
    items.append(it_tables(1920, 128))
    for pr in range(2):
        items.append(it_rope(lambda k, pr=pr: wqk[:, 6 + pr, k, :], lambda k: hTh[:, k, 1920:2048], 128, kT[0][:, pr, 0:128], lambda: [pq_need("wk0")]))
    items.append(it_endblk())
    items.append(lambda: v_block(wvr, 0, lambda k: hTh[:, k, 1920:2048], blk_idx(0, 0, 0), extra_waits=[pq_need("wv")], evac="dve"))
    for rho in range(4):
        items.append(lambda rho=rho: v_block(wvr, 1, lambda k, rho=rho: hTh[:, k, 1536 + rho:2048:4], blk_idx(1, rho, 0), extra_waits=[pq_need("wv")], evac="dve"))
    for r in range(16):
        items.append(lambda r=r: v_block(wvr, 2, lambda k, r=r: hTh[:, k, r:2048:16], blk_idx(2, r, 0), extra_waits=[pq_need("wv")], evac="dve"))

    def it_u(c):
        def f():
            k_, bank, ev = proj_fm(lambda k, c=c: wu_t[c][:, k, :], lambda k: hTh[:, k, 2032:2048], 16, extra_waits=[pq_need("wu")])
            ev2 = dve.op(lambda e, c=c, bank=bank: e.tensor_copy(out=uh[:, c, :], in_=ps[:, bank, 0:16]), waits=[ev])
            pj_ring.release(k_, ev2)
            bank_free[bank] = ev2
        return f
    for c in range(2):
        items.append(it_u(c))

    def between(i):
        for _ in range(3):
            if items:
                items.pop(0)()

    pa_state["all_act"] = True
    rope_state["qb_dve"] = True
    rstd_mode["act"] = True
    phase_a(xo, NT, gpre, hT, between=between)
    while items:
        items.pop(0)()
    pa_state["all_act"] = False
    rope_state["qb_dve"] = False
    rstd_mode["act"] = False
    P.barrier()

    users = []
    for b in range(4):
        csb, ev_cs = load_tables(HT + b * 512)
        for c in range(12):
            if c < 6:
                dst = q_dst(c, b)
            else:
                dst = k_dst((c - 6) // 2, (c - 6) % 2, HT + b * 512, 512)
            rope_unit(lambda k, c=c: wqk[:, c, k, :], lambda k, b=b: hT[:, k, b * 512:(b + 1) * 512], 512, csb, ev_cs, dst, users,
                      extra_waits=[pq_need("wq")])
        rope_flush()
        cs_free[csb] = users[-1]
    P.barrier()

    ev_ones = dve.op(lambda e: e.memset(vA[:, 0:48, :, 64:66], 1.0))
    X = P.sb("X", [128, 2, 2064], F32, R_W)
    ev_uhc = []
    for c in range(2):
        ev_uhc.append(dve.op(lambda e, c=c: e.tensor_copy(out=X[:, c, 0:16], in_=uh[:, c, :])))
        for b in range(4):
            k_, bank, ev = proj_fm(lambda k, c=c: wu_t[c][:, k, :], lambda k, b=b: hT[:, k, b * 512:(b + 1) * 512], 512)
            ev2 = act.op(lambda e, c=c, b=b, bank=bank: e.activation(out=X[:, c, 16 + b * 512:16 + (b + 1) * 512], in_=ps[:, bank, 0:512], func=AF.Copy),
                         waits=[ev])
            pj_ring.release(k_, ev2)
            bank_free[bank] = ev2
    ev_X = act.last()
    Y = P.sb("Y", [128, 2064], F32, R_ST)
    Z = P.sb("Z", [128, 2064], F32, R_ST + 8256)
    zT = P.sb("zT", [128, 2, 2048], BF16, R_ZT)
    ypool = P.sb("ypool", [128, 2, 2048], BF16, R_YP)
    L = 2064
    pev = [ev_X] + ev_uhc
    engs2 = [dve, pool]
    for c in range(2):
        eng = engs2[c]
        e1 = eng.op(lambda e, c=c: e.tensor_tensor(out=Y[:, 1:L], in0=X[:, c, 1:L], in1=X[:, c, 0:L - 1], op=ALU.add), waits=pev)
        if c == 0:
            e2 = eng.op(lambda e: e.tensor_tensor(out=Z[64:128, 3:L], in0=Y[64:128, 3:L], in1=Y[64:128, 1:L - 2], op=ALU.add), waits=[e1])
            fin = [(0, 64, Y), (64, 128, Z)]
            elast = e2
        else:
            e2 = eng.op(lambda e: e.tensor_tensor(out=Z[:, 3:L], in0=Y[:, 3:L], in1=Y[:, 1:L - 2], op=ALU.add), waits=[e1])
            e3 = eng.op(lambda e: e.tensor_tensor(out=Y[:, 7:L], in0=Z[:, 7:L], in1=Z[:, 3:L - 4], op=ALU.add), waits=[e2])
            e4 = eng.op(lambda e: e.tensor_tensor(out=Z[64:128, 15:L], in0=Y[64:128, 15:L], in1=Y[64:128, 7:L - 8], op=ALU.add), waits=[e3])
            fin = [(0, 64, Y), (64, 128, Z)]
            elast = e4
        evz = []
        for (p0, p1, Sx) in fin:
            ez = dve.op(lambda e, c=c, p0=p0, p1=p1, Sx=Sx: e.scalar_tensor_tensor(out=zT[p0:p1, c, :], in0=Sx[p0:p1, 16:L], scalar=invw[p0:p1, c:c + 1],
                                                                                     in1=X[p0:p1, c, 16:L], op0=ALU.mult, op1=ALU.subtract),
                        waits=[elast] + pev)
            ez1 = dve.op(lambda e, c=c, p0=p0, p1=p1, Sx=Sx: e.tensor_tensor(out=Sx[p0:p1, 16:32], in0=Sx[p0:p1, 16:32], in1=invc[p0:p1, c, :], op=ALU.mult),
                         waits=[ez])
            ez2 = dve.op(lambda e, c=c, p0=p0, p1=p1, Sx=Sx: e.tensor_tensor(out=zT[p0:p1, c, 0:16], in0=Sx[p0:p1, 16:32], in1=X[p0:p1, c, 16:32], op=ALU.subtract),
                         waits=[ez1])
            evz.append(ez2)
        pev = evz
    for n in range(16):
        v_block(wvr, 0, lambda k, n=n: hT[:, k, n * 128:(n + 1) * 128], blk_idx(0, 0, 1 + n), evac="act")
    for rho in range(4):
        for n1 in range(4):
            v_block(wvr, 1, lambda k, rho=rho, n1=n1: hT[:, k, 512 * n1 + rho:512 * (n1 + 1):4], blk_idx(1, rho, 1 + n1), evac="act")
    for r in range(16):
        v_block(wvr, 2, lambda k, r=r: hT[:, k, r:2048:16], blk_idx(2, r, 1), evac="act")
    ev_z = pev
    for c in range(2):
        for b in range(4):
            k_, bank, fr = pj_ring.get()
            ev = pe.op(lambda e, c=c, b=b, bank=bank: e.matmul(ps[:, bank, 0:512], lhsT=wmixb[:, c, :], rhs=zT[:, c, b * 512:(b + 1) * 512], start=True, stop=True),
                       waits=[fr, bank_free[bank], pq_need("wmix")] + ev_z)
            ev2 = act.op(lambda e, c=c, b=b, bank=bank: e.activation(out=ypool[:, c, b * 512:(b + 1) * 512], in_=ps[:, bank, 0:512], func=AF.Copy,
                                                                      scale=pscale[:, c:c + 1]), waits=[ev])
            pj_ring.release(k_, ev2)
            bank_free[bank] = ev2
    P.barrier()

    acc = P.sb("acc", [128, 4, 2048], F32, R_W)
    PTc = [P.sb("PTc%d" % i, [128, 512], BF16, R_ST + i * KB) for i in range(3)]
    PTp = [P.sb("PTp%d" % i, [128, 512], BF16, R_ST + 4 * KB + i * KB) for i in range(3)]
    pt_free = [None, None, None]
    s_ring = Ring([(0, 1), (2, 3), (4, 5)])
    o_ring = Ring([6, 7])
    MC, MP, MH, MHP = 0, 1, 2, 3

    def qcols(g, s_, slot):
        if g == 0:
            n = 4 * s_ + slot
            return slice(n * 128, (n + 1) * 128)
        if g == 1:
            return slice(s_ * 512 + 128 * slot, s_ * 512 + 128 * (slot + 1))
        r = 4 * s_ + slot
        return slice(r * 128, (r + 1) * 128)

    def kcols(g, s_, slot, prev):
        if g == 0:
            n = 4 * s_ + slot
            kb = n + (0 if prev else 1)
            return slice(kb * 128, (kb + 1) * 128), blk_idx(0, 0, kb)
        if g == 1:
            m = slot + (0 if prev else 1)
            return slice(s_ * 640 + 128 * m, s_ * 640 + 128 * (m + 1)), blk_idx(1, s_, m)
        r = 4 * s_ + slot
        m = 0 if prev else 1
        return slice(r * 256 + 128 * m, r * 256 + 128 * (m + 1)), blk_idx(2, r, m)

    def acc_dst(g, s_, h):
        if g == 0:
            return acc[0:65, h, s_ * 512:(s_ + 1) * 512], None
        if g == 1:
            return acc[0:65, h, s_:2048:4], None
        a3 = acc[0:65, h, :].rearrange("p (a r) -> p r a", r=16)[:, 4 * s_:4 * s_ + 4, :]
        return a3, "p (r a) -> p r a"

    att_q = []
    acc_g0_ev = [None]
    dve.wait(pq_need("masks"))

    def att_finish(pd):
        (g, s_, h, ub, ev_mc, ev_mp) = pd
        ko, bankO, frO = o_ring.get()
        pe.wait(frO, bank_free[bankO], ev_mc, ev_mp)
        ev = None
        for slot in range(4):
            _, bp = kcols(g, s_, slot, True)
            _, bc = kcols(g, s_, slot, False)
            pe.op(lambda e, slot=slot, bp=bp: e.matmul(ps[0:65, bankO, slot * 128:(slot + 1) * 128], lhsT=vA[:, bp, h, 0:65],
                                                       rhs=PTp[ub][:, slot * 128:(slot + 1) * 128], start=True, stop=False), signal=False)
            ev = pe.op(lambda e, slot=slot, bc=bc: e.matmul(ps[0:65, bankO, slot * 128:(slot + 1) * 128], lhsT=vA[:, bc, h, 0:65],
                                                            rhs=PTc[ub][:, slot * 128:(slot + 1) * 128], start=False, stop=True), signal=(slot == 3))
        pt_free[ub] = ev
        dst, rr = acc_dst(g, s_, h)
        src = ps[0:65, bankO, :]
        if rr is not None:
            src = src.rearrange(rr, r=4)
        if g == 0:
            ev2 = act.op(lambda e: e.activation(out=dst, in_=src, func=AF.Copy), waits=[ev])
            acc_g0_ev[0] = ev2
        else:
            ev2 = dve.op(lambda e: e.tensor_tensor(out=dst, in0=src, in1=dst, op=ALU.add), waits=[ev, acc_g0_ev[0]])
        o_ring.release(ko, ev2)
        bank_free[bankO] = ev2
        return ev2

    att_u = 0
    for g in range(3):
        for s_ in range(4):
            for h in range(4):
                ub = att_u % 3
                att_u += 1
                pr, hf = h // 2, h % 2
                p0 = 64 * hf
                ks, (bC, bP), frS = s_ring.get()
                pe.wait(frS, bank_free[bC], bank_free[bP])
                evs = None
                for slot in range(4):
                    qs = qcols(g, s_, slot)
                    kc, _ = kcols(g, s_, slot, False)
                    kp, _ = kcols(g, s_, slot, True)
                    pe.op(lambda e, slot=slot, qs=qs, kc=kc, g=g, p0=p0, pr=pr, bC=bC: e.matmul(ps[:, bC, slot * 128:(slot + 1) * 128], lhsT=kT[g][p0:p0 + 64, pr, kc],
                                                                      rhs=qT[p0:p0 + 64, 2 * g + pr, qs], start=True, stop=True), signal=False)
                    evs = pe.op(lambda e, slot=slot, qs=qs, kp=kp, g=g, p0=p0, pr=pr, bP=bP: e.matmul(ps[:, bP, slot * 128:(slot + 1) * 128], lhsT=kT[g][p0:p0 + 64, pr, kp],
                                                                            rhs=qT[p0:p0 + 64, 2 * g + pr, qs], start=True, stop=True), signal=(slot == 3))
                ev_ec = act.op(lambda e, ub=ub, bC=bC: e.activation(out=PTc[ub][:, :], in_=ps[:, bC, :], func=AF.Exp, scale=0.125),
                               waits=[evs, pt_free[ub]])
                ev_ep = act.op(lambda e, ub=ub, bP=bP: e.activation(out=PTp[ub][:, :], in_=ps[:, bP, :], func=AF.Exp, scale=0.125))
                s_ring.release(ks, ev_ep)
                bank_free[bC] = ev_ep
                bank_free[bP] = ev_ep
                mp = MH if g == 2 else (MHP if (g == 1 or s_ == 0) else MP)
                ev_mc = dve.op(lambda e, ub=ub: e.tensor_tensor(out=PTc[ub][:, :], in0=PTc[ub][:, :], in1=masks[:, MC, :], op=ALU.mult), waits=[ev_ec])
                ev_mp = dve.op(lambda e, ub=ub, mp=mp: e.tensor_tensor(out=PTp[ub][:, :], in0=PTp[ub][:, :], in1=masks[:, mp, :], op=ALU.mult), waits=[ev_ep])
                att_q.append((g, s_, h, ub, ev_mc, ev_mp))
                while len(att_q) > 2:
                    att_finish(att_q.pop(0))
    while att_q:
        att_finish(att_q.pop(0))
    P.barrier()

    tap("hT", hT[:, :, :])
    tap("qT", qT[:, :, :])
    tap("kT0", kT[0][:, :, :])
    tap("kT1", kT[1][:, :, :])
    tap("kT2", kT[2][:, :, :])
    tap("vA", vA[:, :, :, :])
    tap("acc", acc[0:65, :, :])
    tap("ypool", ypool[:, :, :])
    if stop_after == "attn":
        return finish()
    wpa = P.sb("wpa", [128, 2, D], BF16, R_V)
    selb = P.sb("selb", [64, 2, 128], BF16, R_PT + 256)
    oT2 = P.sb("oT2", [128, 2, 2048], BF16, R_Q + 16 * KB)
    wpp = P.sb("wpp", [128, 2, D], BF16, R_V + 8 * KB)
    wgs = [P.sb("wgs%d" % i, [128, 2, 8, 128], BF16, R_V + 12 * KB + i * 4 * KB) for i in range(2)]
    d_wm2 = P.dsem("d_wm2")
    pool.dma(wpa[:, :, :], wpa_d[:, :, :], d_wm2)
    pool.dma(selb[:, :, :], sel_d[:, :, :], d_wm2)
    ev_wp = pool.dma(wpp[:, :, :], wpp_d[:, :, :], d_wm2)
    d_wg = [P.dsem("d_wg%d" % i) for i in range(2)]
    ev_wg_next = pool.dma(wgs[0][:, :, :, :], wg_d[0], d_wg[0])
    oT = P.sb("oT", [128, 4, 2048], BF16, R_Q)
    lnrow1 = P.sb("lnrow", [128, 2048], F32, R_ST)
    rrow = [P.sb("rrow0", [128, 2048], F32, R_ST + 8 * KB)] + \
           [P.sb("rrow%d" % (i + 1), [128, 2048], F32, R_K + i * 8 * KB) for i in range(3)]
    hirow = [P.sb("hirow%d" % i, [128, 2048], BF16, R_K + 24 * KB + i * 4 * KB) for i in range(2)]
    lorow = [P.sb("lorow0", [128, 2048], BF16, R_V + 4 * KB), P.sb("lorow1", [128, 2048], BF16, R_V + 28 * KB)]
    onesb = P.sb("onesb", [128, 64], BF16, R_PT)
    ev_ob = dve.op(lambda e: e.memset(onesb[:, :], 1.0))
    row_free = [None, None]
    lo_ev = [None, None]
    ex_prev = [None]
    oT_ev = {}
    rows_ev = {}

    def norm_rows(h):
        hb = h % 2
        ev_ln = act.op(lambda e: e.activation(out=lnrow1[64:65, :], in_=acc[64:65, h, :], func=AF.Ln), waits=[ex_prev[0]])
        ev_ex = act.op(lambda e: e.activation(out=rrow[h][64:65, :], in_=lnrow1[64:65, :], func=AF.Exp, scale=-1.0), waits=[ev_ln])
        ex_prev[0] = ev_ex
        ev_hi = dve.op(lambda e: e.tensor_copy(out=hirow[hb][64:65, :], in_=rrow[h][64:65, :]), waits=[ev_ex, row_free[hb]])
        ev_lo = pool.op(lambda e: e.tensor_tensor(out=lorow[hb][64:65, :], in0=rrow[h][64:65, :], in1=hirow[hb][64:65, :], op=ALU.subtract),
                        waits=[ev_hi, row_free[hb]])
        rows_ev[h] = (ev_hi, ev_lo)

    def norm_apply(h):
        hb = h % 2
        ev_hi, ev_lo = rows_ev[h]
        evu = None
        for b in range(4):
            k_, bank, fr = pj_ring.get()
            pe.op(lambda e, b=b, bank=bank: e.matmul(ps[0:64, bank, 0:512], lhsT=onesb[64:65, 0:64], rhs=hirow[hb][64:65, b * 512:(b + 1) * 512],
                                                     start=True, stop=False), waits=[fr, bank_free[bank], ev_hi, ev_lo, ev_ob], signal=False)
            ev = pe.op(lambda e, b=b, bank=bank: e.matmul(ps[0:64, bank, 0:512], lhsT=onesb[64:65, 0:64], rhs=lorow[hb][64:65, b * 512:(b + 1) * 512],
                                                          start=False, stop=True))
            ev2 = dve.op(lambda e, b=b, bank=bank: e.tensor_tensor(out=oT[0:64, h, b * 512:(b + 1) * 512], in0=ps[0:64, bank, 0:512],
                                                                   in1=acc[0:64, h, b * 512:(b + 1) * 512], op=ALU.mult), waits=[ev])
            pj_ring.release(k_, ev2)
            bank_free[bank] = ev2
            evu = ev
            oT_ev[(h, b)] = ev2
        row_free[hb] = evu

    def norm_pack(p_):
        if True:
            for b in range(4):
                k_, bank, fr = pj_ring.get()
                pe.op(lambda e, b=b, bank=bank: e.matmul(ps[:, bank, 0:512], lhsT=selb[0:64, 0, :], rhs=oT[0:64, 2 * p_, b * 512:(b + 1) * 512],
                                                         start=True, stop=False),
                      waits=[fr, bank_free[bank], ev_wp, oT_ev[(2 * p_, b)], oT_ev[(2 * p_ + 1, b)]], signal=False)
                ev = pe.op(lambda e, b=b, bank=bank: e.matmul(ps[:, bank, 0:512], lhsT=selb[0:64, 1, :], rhs=oT[0:64, 2 * p_ + 1, b * 512:(b + 1) * 512],
                                                              start=False, stop=True))
                ev2 = dve.op(lambda e, b=b, bank=bank: e.tensor_copy(out=oT2[:, p_, b * 512:(b + 1) * 512], in_=ps[:, bank, 0:512]), waits=[ev])
                pj_ring.release(k_, ev2)
                bank_free[bank] = ev2

    norm_rows(0)
    norm_rows(1)
    norm_apply(0)
    norm_rows(2)
    norm_apply(1)
    norm_rows(3)
    norm_pack(0)
    norm_apply(2)
    norm_apply(3)
    norm_pack(1)
    P.barrier()

    tap("oT", oT[0:64, :, :])
    if stop_after == "norm":
        return finish()

    mT = P.sb("mT", [128, 8, 2048], BF16, R_K)
    sga = [P.sb("sga%d" % i, [128, 512], F32, R_V + 20 * KB + i * 2 * KB) for i in range(2)]
    sgp = [P.sb("sgp%d" % i, [128, 512], F32, R_V + 24 * KB + i * 2 * KB) for i in range(2)]
    m1 = [P.sb("m1_%d" % i, [128, 512], F32, R_ST + i * 2 * KB) for i in range(2)]
    m2 = [P.sb("m2_%d" % i, [128, 512], F32, R_ST + 4 * KB + i * 2 * KB) for i in range(2)]
    wout = P.sb("wout", [128, 8, D], BF16, R_W)
    gpost = P.sb("gpost", [128, D], F32, R_W + 16 * KB)
    wg_free = [None, None]
    d_wo = P.dsem("d_wo")
    mring = Ring([(0, 1, 2, 3), (4, 5, 6, 7)])
    sg_free = [None, None]
    m_free = [None, None]
    mu = 0
    for c in range(8):
        wb = c % 2
        ev_wg = ev_wg_next
        if c + 1 < 8:
            ev_wg_next = pool.dma(wgs[(c + 1) % 2][:, :, :, :], wg_d[c + 1], d_wg[(c + 1) % 2], waits=[wg_free[(c + 1) % 2]])
        if c == 1:
            for k in range(8):
                ev_wo = pool.dma(wout[:, k, :], wout_d[:, k, :], d_wo)
            ev_gp = sp.dma(gpost[:, :], gpost_d[:, :], P.dsem("d_gpost"))
        for b in range(4):
            ub = mu % 2
            mu += 1
            km, (b1, b2, b3, b4), frm = mring.get()
            tok = slice(b * 512, (b + 1) * 512)
            pe.wait(frm, bank_free[b1], bank_free[b2], bank_free[b3], bank_free[b4], ev_wg, ev_wp)
            for k in range(8):
                e1 = pe.op(lambda e, k=k, wb=wb, b1=b1, tok=tok: e.matmul(ps[:, b1, :], lhsT=wgs[wb][:, 0, k, :], rhs=hT[:, k, tok], start=(k == 0), stop=(k == 7)),
                           signal=(k == 7))
            for k in range(8):
                e2 = pe.op(lambda e, k=k, wb=wb, b2=b2, tok=tok: e.matmul(ps[:, b2, :], lhsT=wgs[wb][:, 1, k, :], rhs=hT[:, k, tok], start=(k == 0), stop=(k == 7)),
                           signal=(k == 7))
            for p_ in range(2):
                e3 = pe.op(lambda e, p_=p_, c=c, b3=b3, tok=tok: e.matmul(ps[:, b3, :], lhsT=wpa[:, p_, c * 128:(c + 1) * 128], rhs=oT2[:, p_, tok],
                                                                          start=(p_ == 0), stop=(p_ == 1)), signal=(p_ == 1))
            for j in range(2):
                e4 = pe.op(lambda e, j=j, c=c, b4=b4, tok=tok: e.matmul(ps[:, b4, :], lhsT=wpp[:, j, c * 128:(c + 1) * 128], rhs=ypool[:, j, tok],
                                                                        start=(j == 0), stop=(j == 1)), signal=(j == 1))
            if b == 3:
                wg_free[wb] = e2
            ea = act.op(lambda e, ub=ub, b1=b1: e.activation(out=sga[ub][:, :], in_=ps[:, b1, :], func=AF.Sigmoid), waits=[e1, sg_free[ub]])
            eb = act.op(lambda e, ub=ub, b2=b2: e.activation(out=sgp[ub][:, :], in_=ps[:, b2, :], func=AF.Sigmoid), waits=[e2])
            em1 = dve.op(lambda e, ub=ub, b3=b3: e.tensor_tensor(out=m1[ub][:, :], in0=ps[:, b3, :], in1=sga[ub][:, :], op=ALU.mult), waits=[ea, e3, m_free[ub]])
            em2 = dve.op(lambda e, ub=ub, b4=b4: e.tensor_tensor(out=m2[ub][:, :], in0=ps[:, b4, :], in1=sgp[ub][:, :], op=ALU.mult), waits=[eb, e4])
            sg_free[ub] = em2
            mring.release(km, em2)
            for bb in (b1, b2, b3, b4):
                bank_free[bb] = em2
            eo = pool.op(lambda e, ub=ub, c=c, tok=tok: e.tensor_tensor(out=mT[:, c, tok], in0=m1[ub][:, :], in1=m2[ub][:, :], op=ALU.add), waits=[em1, em2])
            m_free[ub] = eo
    P.barrier()

    tap("mT", mT[:, :, :])
    if stop_after == "merge":
        return finish()
    wdn = P.sb("wdn", [128, NJ, D], BF16, R_Q + 10 * KB)
    assert R_Q + 54 * KB <= R_W and R_K + 44 * KB <= R_Q + 10 * KB
    gfin = P.sb("gfin", [128, D], F32, R_W + 20 * KB)
    xin = [P.sb("xin%d" % i, [128, D], F32, R_ST + i * 4 * KB) for i in range(4)]
    x1b = [P.sb("x1b0", [128, D], F32, R_PT), P.sb("x1b1", [128, D], F32, R_WS), P.sb("x1b2", [128, D], F32, R_WS + 4 * KB)]
    tb = [P.sb("tb%d" % i, [128, D], F32, R_ZT + i * 4 * KB) for i in range(2)] + [P.sb("tb2", [128, D], F32, R_Q + 54 * KB)]
    assert R_Q + 58 * KB <= R_W
    xs2 = [P.sb("xs2_%d" % i, [128, D], BF16, R_YP + i * 2 * KB) for i in range(2)]
    junks2 = [P.sb("junk2_%d" % i, [128, D], BF16, R_YP + 4 * KB + i * 2 * KB) for i in range(2)]
    junk_st2 = {"i": 0, "ev": [None, None]}
    d_xi = [P.dsem("d_xi%d" % i) for i in range(4)]
    d_x1o = [P.dsem("d_x1o%d" % i) for i in range(3)]
    xin_free = [None, None, None, None]
    x1b_free = [None, None, None]
    tb_free = [None, None, None]
    xs2_free = [None, None]
    pair_ring = Ring([(2, 3), (4, 5), (6, 7)])
    tr_ring2 = Ring([0, 1])

    def pair_ap(pr_):
        return ps[:, pr_[0]:pr_[0] + 2, :].rearrange("p a b -> p (a b)")

    x1_store_ev = {}

    def wout_stage1(i):
        b = i % 4
        ev_x = sp.dma(xin[b][:, :], xo[i * 128:(i + 1) * 128, :], d_xi[b], waits=[xin_free[b]])
        kp, pr_, frp = pair_ring.get()
        pe.wait(frp, bank_free[pr_[0]], bank_free[pr_[1]], ev_wo)
        ev = None
        for hf in range(2):
            for k in range(8):
                ev = pe.op(lambda e, k=k, hf=hf, pr_=pr_: e.matmul(ps[:, pr_[0] + hf, :], lhsT=mT[:, k, i * 128:(i + 1) * 128], rhs=wout[:, k, hf * 512:(hf + 1) * 512],
                                                                  start=(k == 0), stop=(k == 7)), signal=(k == 7 and hf == 1))
        return (i, b, kp, pr_, ev, ev_x)

    def norm_part(pr_, ev_mm, col, gvec, ev_g, b):
        yap = pair_ap(pr_)
        ev_sq = sq_accum(yap, col, [ev_mm], junks2, junk_st2)
        ev_rs = rstd_chain(ev_sq, col)
        ev_t = dve.op(lambda e: e.scalar_tensor_tensor(out=tb[b][:, :], in0=yap, scalar=rsA[:, col:col + 1], in1=gvec[:, :], op0=ALU.mult, op1=ALU.mult),
                      waits=[ev_rs, tb_free[b], ev_g])
        return ev_t

    def add_part(ev_t, xres, ev_xres, outbuf, out_free, b):
        ev_o = pool.op(lambda e: e.tensor_tensor(out=outbuf[:, :], in0=tb[b][:, :], in1=xres[:, :], op=ALU.add), waits=[ev_t, ev_xres, out_free])
        tb_free[b] = ev_o
        return ev_o

    def norm_res(pr_, ev_mm, col, gvec, ev_g, xres, ev_xres, outbuf, out_free, b):
        ev_t = norm_part(pr_, ev_mm, col, gvec, ev_g, b)
        ev_o = add_part(ev_t, xres, ev_xres, outbuf, out_free, b)
        return ev_t, ev_o

    def wout_stage2a(st):
        (i, b, kp, pr_, ev_mm, ev_x) = st
        col = ss_col[0] % 48
        ss_col[0] += 1
        ev_t = norm_part(pr_, ev_mm, col, gpost, ev_gp, i % 3)
        pair_ring.release(kp, ev_t)
        bank_free[pr_[0]] = ev_t
        bank_free[pr_[1]] = ev_t
        return (i, b, ev_t, ev_x)

    def wout_stage2add(st):
        (i, b, ev_t, ev_x) = st
        xb = i % 3
        ev_o = add_part(ev_t, xin[b], ev_x, x1b[xb], x1b_free[xb], i % 3)
        xin_free[b] = ev_o
        ev_st = sp.dma(x1s_d[i * 128:(i + 1) * 128, :], x1b[xb][:, :], d_x1o[xb], waits=[ev_o])
        x1_store_ev[i] = ev_st
        return (i, xb, ev_o, ev_st)

    def wout_stage2b1(st):
        (i, b, ev_o, ev_st) = st
        col2 = ss_col[0] % 48
        ss_col[0] += 1
        ev_sq = sq_accum(x1b[b][:, :], col2, [ev_o], junks2, junk_st2)
        ev_rs = rstd_chain(ev_sq, col2)
        return (i, b, ev_st, col2, ev_rs)

    def wout_stage2b2(st):
        (i, b, ev_st, col2, ev_rs) = st
        b2 = i % 2
        ev_xs = act.op(lambda e: e.activation(out=xs2[b2][:, :], in_=x1b[b][:, :], func=AF.Copy, scale=rsA[:, col2:col2 + 1]), waits=[ev_rs, xs2_free[b2]])
        x1b_free[b] = [ev_xs, ev_st]
        k_, bank, fr = tr_ring2.get()
        psb = ps[:, bank, :].bitcast(BF16).rearrange("p (k t) -> p k t", k=8)
        pe.wait(ev_xs, fr, bank_free[bank])
        ev_tr = None
        for k in range(8):
            ev_tr = pe.op(lambda e, k=k: e.transpose(out=psb[:, k, :], in_=xs2[b2][:, k * 128:(k + 1) * 128], identity=ident[:, :]), signal=(k == 7))
        xs2_free[b2] = ev_tr
        ev_ev = dve.op(lambda e: e.tensor_tensor(out=hT[:, :, i * 128:(i + 1) * 128], in0=psb, in1=gffn[:, :].unsqueeze(2).to_broadcast([128, 8, 128]), op=ALU.mult),
                       waits=[ev_tr])
        tr_ring2.release(k_, ev_ev)
        bank_free[bank] = ev_ev

    wgu_boot = [P.sb("wgub%d" % i, [128, 2, 8, 128], BF16, R_Q + i * 4 * KB) for i in range(2)]
    d_gub = [P.dsem("d_gub%d" % i) for i in range(2)]
    boot_ev = [pool.dma(wgu_boot[i][:, :, :, :], wgu_d[i], d_gub[i]) for i in range(2)]
    d_wd = P.dsem("d_wd")
    ev_gf = sp.dma(gfin[:, :], gfin_d[:, :], P.dsem("d_gfin"))
    st1, st2, st3, st4 = {}, {}, {}, {}
    for it in range(NT + 4):
        if it < NT:
            st1[it] = wout_stage1(it)
        if 0 <= it - 3 < NT:
            st3[it - 3] = wout_stage2add(st2[it - 3])
        if 0 <= it - 4 < NT:
            st4[it - 4] = wout_stage2b1(st3[it - 4])
        if 0 <= it - 1 < NT:
            st2[it - 1] = wout_stage2a(st1[it - 1])
        if 0 <= it - 4 < NT:
            wout_stage2b2(st4[it - 4])
    P.barrier()

    tap("h2T", hT[:, :, :])
    if stop_after == "wout":
        return finish()
    ffT = P.sb("ffT", [128, NJ, 1024], BF16, R_K)
    wgu = [P.sb("wgu%d" % i, [128, 2, 8, 128], BF16, R_W + i * 4 * KB) for i in range(3)]
    sa = [P.sb("sa%d" % i, [128, 512], F32, R_W + 12 * KB + i * 2 * KB) for i in range(2)]
    d_gu = [P.dsem("d_gu%d" % i) for i in range(3)]
    gu_free = [None, None, None]
    sa_free = [None, None]
    d_x1i = [P.dsem("d_x1i%d" % i) for i in range(2)]
    d_out = [P.dsem("d_out%d" % i) for i in range(2)]
    obuf = x1b
    obuf_free = [x1b_free[0], x1b_free[1]]
    tb_free = [pool.last(), pool.last(), pool.last()]
    ab_ring = Ring([(0, 1), (2, 3)])
    pair_ring2 = Ring([(4, 5), (6, 7)])
    out_evs = []
    fu = 0
    gl = 0
    loads = [(hf, j) for hf in range(2) for j in range(NJ)]
    ld_ev = {}

    def issue_gu(idx):
        hf, j = loads[idx]
        b = idx % 3
        ld_ev[idx] = pool.dma(wgu[b][:, :, :, :], wgu_d[j], d_gu[b], waits=[gu_free[b]])

    ld_ev[0], ld_ev[1] = boot_ev

    def wgu_buf(idx):
        return wgu_boot[idx] if idx < 2 else wgu[idx % 3]
    for hf in range(2):
        for j in range(NJ):
            idx = hf * NJ + j
            if idx + 2 < len(loads):
                issue_gu(idx + 2)
            if hf == 0:
                ev_wd = pool.dma(wdn[:, j, :], wdn_d[:, j, :], d_wd)
            wb = idx % 3
            wbuf = wgu_buf(idx)
            for b in range(2):
                ub = fu % 2
                fu += 1
                tok = slice(hf * 1024 + b * 512, hf * 1024 + (b + 1) * 512)
                ka, (ba, bb), fra = ab_ring.get()
                pe.wait(fra, bank_free[ba], bank_free[bb], ld_ev[idx])
                for k in range(8):
                    e1 = pe.op(lambda e, k=k, wbuf=wbuf, ba=ba, tok=tok: e.matmul(ps[:, ba, :], lhsT=wbuf[:, 0, k, :], rhs=hT[:, k, tok], start=(k == 0), stop=(k == 7)),
                               signal=(k == 7))
                for k in range(8):
                    e2 = pe.op(lambda e, k=k, wbuf=wbuf, bb=bb, tok=tok: e.matmul(ps[:, bb, :], lhsT=wbuf[:, 1, k, :], rhs=hT[:, k, tok], start=(k == 0), stop=(k == 7)),
                               signal=(k == 7))
                if b == 1:
                    gu_free[wb] = e2
                es = act.op(lambda e, ub=ub, ba=ba: e.activation(out=sa[ub][:, :], in_=ps[:, ba, :], func=AF.Silu), waits=[e1, sa_free[ub]])
                ef = dve.op(lambda e, ub=ub, bb=bb, j=j, b=b: e.tensor_tensor(out=ffT[:, j, b * 512:(b + 1) * 512], in0=ps[:, bb, :], in1=sa[ub][:, :], op=ALU.mult),
                            waits=[es, e2])
                sa_free[ub] = ef
                ab_ring.release(ka, ef)
                bank_free[ba] = ef
                bank_free[bb] = ef
        ev_ff = dve.last()
        for il in range(8):
            i = hf * 8 + il
            b = i % 2
            ev_x1 = sp.dma(xin[b][:, :], x1s_d[i * 128:(i + 1) * 128, :], d_x1i[b], waits=[xin_free[b], x1_store_ev[i]])
            kp, pr_, frp = pair_ring2.get()
            pe.wait(frp, bank_free[pr_[0]], bank_free[pr_[1]], ev_wd, ev_ff)
            ev = None
            for h2 in range(2):
                for j in range(NJ):
                    ev = pe.op(lambda e, j=j, h2=h2, pr_=pr_, il=il: e.matmul(ps[:, pr_[0] + h2, :], lhsT=ffT[:, j, il * 128:(il + 1) * 128],
                                                                               rhs=wdn[:, j, h2 * 512:(h2 + 1) * 512], start=(j == 0), stop=(j == NJ - 1)),
                               signal=(j == NJ - 1 and h2 == 1))
            col = ss_col[0] % 48
            ss_col[0] += 1
            ev_t, ev_o = norm_res(pr_, ev, col, gfin, ev_gf, xin[b], ev_x1, obuf[b], obuf_free[b], b)
            pair_ring2.release(kp, ev_t)
            bank_free[pr_[0]] = ev_t
            bank_free[pr_[1]] = ev_t
            xin_free[b] = ev_o
            ev_out = sp.dma(out_d[i * 128:(i + 1) * 128, :], obuf[b][:, :], d_out[b], waits=[ev_o])
            obuf_free[b] = ev_out
            out_evs.append(ev_out)
        dve.wait(pe.last())
    sp.wait(out_evs[-1], out_evs[-2])
    for e_ in P.cengs:
        e_.wait(out_evs[-1], out_evs[-2])
    P.run()
    pscm.__exit__(None, None, None)
    return nc


def _rope_tables():
    half = 32
    inv_freq = 10000.0 ** (-(np.arange(half, dtype=np.float64) * 2.0 / 64.0))
    pos = np.arange(-HT, S, dtype=np.float64)
    ang = pos[:, None] * inv_freq[None, :]
    cos = np.cos(ang).astype(np.float32)
    sin = np.sin(ang).astype(np.float32)
    d = np.arange(128) % 64
    i = d % 32
    sgn = np.where(d < 32, -1.0, 1.0).astype(np.float32)
    C = np.ascontiguousarray(cos[:, i].T)
    Sn = np.ascontiguousarray((sin[:, i] * sgn[None, :]).T)
    return C, Sn


_PROG_CACHE = {}


def _host_inputs(x, g_pre_mix, w_in, w_pool_mix, pool_scale, w_proj_attn, w_proj_pool, w_out,
                 g_post_mix, g_pre_ffn, w_gate_up, w_down, g_post_ffn):
    f = np.float32
    x = np.asarray(x, f).reshape(S, D)
    w_in = np.asarray(w_in, f)[0]
    def chunks(wcols, n):
        nc_ = wcols.shape[1] // n
        return np.ascontiguousarray(wcols.reshape(8, 128, nc_, n).transpose(2, 1, 0, 3))
    wq = chunks(w_in[:, 0:768], 128)
    wk = chunks(w_in[:, 768:1536], 128)
    wv = chunks(w_in[:, 1536:2304], 256)
    wu = chunks(w_in[:, 2304:2560], 128)
    wga = chunks(w_in[:, 2560:3584], 128)
    wgp = chunks(w_in[:, 3584:4608], 128)
    wg = np.ascontiguousarray(np.stack([wga, wgp], axis=2))
    wmix = np.ascontiguousarray(np.asarray(w_pool_mix, f)[0])
    wpa = np.ascontiguousarray(np.asarray(w_proj_attn, f)[0].reshape(2, 128, D).transpose(1, 0, 2))
    sel = np.zeros((64, 2, 128), f)
    sel[np.arange(64), 0, np.arange(64)] = 1.0
    sel[np.arange(64), 1, np.arange(64) + 64] = 1.0
    wpp = np.ascontiguousarray(np.asarray(w_proj_pool, f)[0].reshape(2, 128, D).transpose(1, 0, 2))
    wout = np.ascontiguousarray(np.asarray(w_out, f)[0].reshape(8, 128, D).transpose(1, 0, 2))
    wgu_full = np.asarray(w_gate_up, f)[0]
    wa = chunks(wgu_full[:, 0:FF], 128)
    wb = chunks(wgu_full[:, FF:2 * FF], 128)
    wgu = np.ascontiguousarray(np.stack([wa, wb], axis=2))
    wdn = np.ascontiguousarray(np.asarray(w_down, f)[0].reshape(NJ, 128, D).transpose(1, 0, 2))
    def fm(g):
        return np.ascontiguousarray(np.asarray(g, f).reshape(8, 128).T)
    gpre = fm(g_pre_mix[0])
    gffn = fm(g_pre_ffn[0])
    gpost = np.ascontiguousarray(np.broadcast_to(np.asarray(g_post_mix, f)[0][None, :], (128, D)))
    gfin = np.ascontiguousarray(np.broadcast_to(np.asarray(g_post_ffn, f)[0][None, :], (128, D)))
    pscale = np.ascontiguousarray(np.asarray(pool_scale, f)[0].reshape(2, 128).T)
    wins = np.array([2, 4, 8, 16], dtype=f)
    wpart = wins[(np.arange(256) // 64)]
    invw = np.ascontiguousarray((1.0 / wpart).astype(f).reshape(2, 128).T)
    C, Sn = _rope_tables()
    ident = np.eye(128, dtype=f)
    rotm = np.zeros((128, 128), f)
    for dd in range(128):
        base = (dd // 64) * 64
        rotm[base + ((dd % 64) + 32) % 64, dd] = 1.0
    kk = np.arange(128)[:, None]
    qq = np.arange(128)[None, :]
    mC = (kk <= qq).astype(f)
    mP = (kk >= qq).astype(f)
    common = dict(wq=wq, wk=wk, wv=wv, wu=wu, wg=wg, wmix=wmix, wpa=wpa, sel=sel, wpp=wpp, wout=wout, wgu=wgu, wdn=wdn,
                  gpre=gpre, gffn=gffn, gpost=gpost, gfin=gfin, pscale=pscale, invw=invw, ident=ident, rotm=rotm)
    in_maps = []
    for c in range(NCORES):
        mH = mP if c > 0 else np.zeros_like(mP)
        masks = np.zeros((128, 4, 512), f)
        masks[:, 0] = np.tile(mC, (1, 4))
        masks[:, 1] = np.tile(mP, (1, 4))
        masks[:, 2] = np.tile(mH, (1, 4))
        masks[:, 3] = np.concatenate([mH, mP, mP, mP], axis=1)
        xo = x[c * T:(c + 1) * T]
        xh_ = x[c * T - HT:c * T] if c > 0 else np.zeros((HT, D), f)
        tpos = np.arange(16, dtype=f) + c * T
        cnt = np.minimum(tpos[None, :] + 1.0, wpart[:, None])
        invc = np.ascontiguousarray((1.0 / cnt).astype(f).reshape(2, 128, 16).transpose(1, 0, 2))
        m = dict(common)
        m.update(xh=np.ascontiguousarray(xh_), xo=np.ascontiguousarray(xo), masks=masks, invc=invc,
                 cost=np.ascontiguousarray(C[:, c * T:c * T + HT + T]), sint=np.ascontiguousarray(Sn[:, c * T:c * T + HT + T]))
        in_maps.append(m)
    return in_maps


def kernel(x, g_pre_mix, w_in, w_pool_mix, pool_scale, w_proj_attn, w_proj_pool, w_out,
           g_post_mix, g_pre_ffn, w_gate_up, w_down, g_post_ffn):
    in_maps = _host_inputs(x, g_pre_mix, w_in, w_pool_mix, pool_scale, w_proj_attn, w_proj_pool, w_out,
                           g_post_mix, g_pre_ffn, w_gate_up, w_down, g_post_ffn)
    nc = build_program()
    res = run_bass_kernel_spmd(nc, in_maps, core_ids=list(range(NCORES)))
    out = np.concatenate([np.asarray(r["out"], np.float32) for r in res.results], axis=0)
    return out.reshape(1, S, D)
```
